# Optimizing a Trainium2 kernel written in Bass

```python
import math
import jax, jax.numpy as jnp
from jax import lax
import numpy as np

D_MODEL = 1024
BATCH = 16
SEQ = 256
DEPTH = 2
DEC_BATCH = 4
DEC_SEQ = 4096
PAST_LEN = 256

GRID_W = 64
N_MIXERS = 2
N_CONV_LAYERS = (DEPTH + 1) // 2
N_SSD_LAYERS = DEPTH // 2
CONV_W = 3
SSD_EXPAND = 2
D_INNER = SSD_EXPAND * D_MODEL
SSD_HEAD_DIM = 64
SSD_HEADS = D_INNER // SSD_HEAD_DIM
SSD_STATE = 128
SSD_GROUPS = 8
SSD_CHUNK = 128
SSD_CONV_DIM = D_INNER + 2 * SSD_GROUPS * SSD_STATE
SSD_IN_DIM = D_INNER + SSD_CONV_DIM + 2 * SSD_HEADS
N_KEYS = 128
N_EXPERTS = N_KEYS * N_KEYS
PEER_HEADS = 8
PEER_TOPK = 16
PEER_KEY_DIM = 256
PEER_GROUP = 128
EPS = 1e-6

kernel_name = "hybrid_conv_ssd_peer_diffusion_step"


def rmsnorm(x, g):
    xf = x.astype(jnp.float32)
    y = xf * lax.rsqrt(jnp.mean(xf * xf, axis=-1, keepdims=True) + EPS)
    return (y * g.astype(jnp.float32)).astype(x.dtype)


def dwconv3(x, w, grid):
    b, L, C = x.shape
    rows, cols = grid
    xp = jnp.pad(x.reshape(b, rows, cols, C), ((0, 0), (0, 0), (1, 1), (0, 0)))
    y = xp[:, :, :-2] * w[0] + xp[:, :, 1:-1] * w[1] + xp[:, :, 2:] * w[2]
    return y.reshape(b, L, C)


def short_conv_mixer(h, w_in, conv_w, w_out, grid):
    bg, cg, xv = jnp.split(h @ w_in, 3, axis=-1)
    return (bg * dwconv3(cg * xv, conv_w, grid)) @ w_out


def ssd_scan(x, dt, a, bm, cm, h0):
    b, L, H, P = x.shape
    G, N = bm.shape[-2:]
    R = H // G
    Q = SSD_CHUNK
    nc = L // Q
    f32 = jnp.float32
    x = x.astype(f32).reshape(b, nc, Q, G, R, P)
    dt = dt.astype(f32).reshape(b, nc, Q, G, R)
    bm = bm.astype(f32).reshape(b, nc, Q, G, N)
    cm = cm.astype(f32).reshape(b, nc, Q, G, N)
    cum = jnp.cumsum(dt * a.astype(f32).reshape(G, R), axis=2)
    xdt = x * dt[..., None]
    mask = jnp.tril(jnp.ones((Q, Q), dtype=bool))[None, None, :, :, None, None]
    seg = cum[:, :, :, None] - cum[:, :, None, :]
    decay = jnp.exp(jnp.where(mask, seg, -jnp.inf))
    cb = jnp.einsum('bcign,bcjgn->bcijg', cm, bm)
    y_diag = jnp.einsum('bcijgr,bcjgrp->bcigrp', decay * cb[..., None], xdt)
    end_decay = jnp.exp(cum[:, :, -1:] - cum)
    states = jnp.einsum('bcjgn,bcjgrp->bcgrpn', bm, xdt * end_decay[..., None])
    chunk_decay = jnp.exp(cum[:, :, -1])

    def step(h, inp):
        s, d = inp
        return d[..., None, None] * h + s, h

    h_last, h_prev = lax.scan(step, h0.astype(f32).reshape(b, G, R, P, N),
                              (jnp.moveaxis(states, 1, 0), jnp.moveaxis(chunk_decay, 1, 0)))
    h_prev = jnp.moveaxis(h_prev, 0, 1)
    y_off = jnp.einsum('bcign,bcgrpn->bcigrp', cm, h_prev) * jnp.exp(cum)[..., None]
    return (y_diag + y_off).reshape(b, L, H, P), h_last.reshape(b, H, P, N)


def ssd_mixer(h, w_in, conv_w, conv_b, dt_bias, a_log, d_skip, norm_g, w_out, grid, h0):
    b, L, _ = h.shape
    proj = h @ w_in
    z = proj[..., :D_INNER]
    xbc = proj[..., D_INNER:D_INNER + SSD_CONV_DIM]
    dt_raw = proj[..., D_INNER + SSD_CONV_DIM:]
    xbc = jax.nn.silu(dwconv3(xbc, conv_w, grid) + conv_b)
    gn = SSD_GROUPS * SSD_STATE
    xs = xbc[..., :D_INNER].reshape(b, L, SSD_HEADS, SSD_HEAD_DIM)
    bm = xbc[..., D_INNER:D_INNER + gn].reshape(b, L, SSD_GROUPS, SSD_STATE)
    cm = xbc[..., D_INNER + gn:].reshape(b, L, SSD_GROUPS, SSD_STATE)
    dt = jax.nn.softplus(dt_raw.astype(jnp.float32).reshape(b, L, 2, SSD_HEADS)
                         + dt_bias.astype(jnp.float32))
    a = -jnp.exp(a_log.astype(jnp.float32))
    y_f, h_f = ssd_scan(xs, dt[:, :, 0], a[0], bm, cm, h0[:, 0])
    flip = lambda t: jnp.flip(t, axis=1)
    y_b, h_b = ssd_scan(flip(xs), flip(dt[:, :, 1]), a[1], flip(bm), flip(cm), h0[:, 1])
    d_tot = (d_skip[0] + d_skip[1]).astype(jnp.float32)[:, None]
    y = y_f + flip(y_b) + d_tot * xs.astype(jnp.float32)
    y = y.reshape(b, L, D_INNER).astype(h.dtype)
    y = rmsnorm(y * jax.nn.silu(z), norm_g)
    return y @ w_out, jnp.stack([h_f, h_b], axis=1).astype(h.dtype)


def peer(h, wq, keys, u, v):
    b, L, D = h.shape
    T = b * L
    t = h.reshape(T, D)
    q = (t @ wq).reshape(T, PEER_HEADS, 2, PEER_KEY_DIM // 2)
    s = jnp.einsum('thpe,hpne->thpn', q, keys).astype(jnp.float32)
    s1, i1 = lax.top_k(s[:, :, 0], PEER_TOPK)
    s2, i2 = lax.top_k(s[:, :, 1], PEER_TOPK)
    cand = (s1[..., :, None] + s2[..., None, :]).reshape(T, PEER_HEADS, PEER_TOPK * PEER_TOPK)
    cidx = (i1[..., :, None] * N_KEYS + i2[..., None, :]).reshape(T, PEER_HEADS, PEER_TOPK * PEER_TOPK)
    sc, pos = lax.top_k(cand, PEER_TOPK)
    idx = jnp.take_along_axis(cidx, pos, axis=-1)
    g = jax.nn.softmax(sc, axis=-1).astype(h.dtype)
    ng = T // PEER_GROUP

    def group(args):
        tb, ib, gb = args
        act = jnp.einsum('td,thkd->thk', tb, u[ib])
        return jnp.einsum('thk,thkd->td', jax.nn.gelu(act, approximate=False) * gb, v[ib])

    out = lax.map(group, (t.reshape(ng, PEER_GROUP, D),
                          idx.reshape(ng, PEER_GROUP, PEER_HEADS, PEER_TOPK),
                          g.reshape(ng, PEER_GROUP, PEER_HEADS, PEER_TOPK)))
    return out.reshape(b, L, D)


def run_trunk(x, cond, grid, ssm_init, norm_mix_g, norm_ffn_g, norm_f_g, ada_w, ada_b,
              sc_w_in, sc_conv_w, sc_w_out, ssd_w_in, ssd_conv_w, ssd_conv_b, ssd_dt_bias,
              ssd_a_log, ssd_d, ssd_norm_g, ssd_w_out, peer_wq, peer_keys, peer_u, peer_v):
    b = x.shape[0]
    states = []
    for i in range(DEPTH):
        mod = (jax.nn.silu(cond) @ ada_w[i] + ada_b[i])[:, None, :]
        sh_a, scl_a, gt_a, sh_f, scl_f, gt_f = jnp.split(mod, 6, axis=-1)
        hn = rmsnorm(x, norm_mix_g[i]) * (1 + scl_a) + sh_a
        j = i // N_MIXERS
        if i % N_MIXERS == 0:
            mix = short_conv_mixer(hn, sc_w_in[j], sc_conv_w[j], sc_w_out[j], grid)
        else:
            if ssm_init is None:
                h0 = jnp.zeros((b, 2, SSD_HEADS, SSD_HEAD_DIM, SSD_STATE), x.dtype)
            else:
                h0 = ssm_init[:, j]
            mix, st = ssd_mixer(hn, ssd_w_in[j], ssd_conv_w[j], ssd_conv_b[j], ssd_dt_bias[j],
                                ssd_a_log[j], ssd_d[j], ssd_norm_g[j], ssd_w_out[j], grid, h0)
            states.append(st)
        x = x + gt_a * mix
        hn = rmsnorm(x, norm_ffn_g[i]) * (1 + scl_f) + sh_f
        x = x + gt_f * peer(hn, peer_wq[i], peer_keys[i], peer_u[i], peer_v[i])
    return rmsnorm(x, norm_f_g), jnp.stack(states, axis=1)


def setup_inputs(seed: int = 0) -> dict:
    key = jax.random.key(seed)
    ks = iter(jax.random.split(key, 32))
    nrm = lambda shape, scale: jax.random.normal(next(ks), shape, jnp.float32) * scale
    D = D_MODEL
    dt0 = jnp.exp(jax.random.uniform(next(ks), (N_SSD_LAYERS, 2, SSD_HEADS))
                  * (math.log(0.1) - math.log(0.001)) + math.log(0.001))
    return {
        "x_prompt": nrm((BATCH, SEQ, D), 1.0),
        "x_sample": nrm((DEC_BATCH, DEC_SEQ, D), 1.0),
        "state_ssm": nrm((DEC_BATCH, N_SSD_LAYERS, 2, SSD_HEADS, SSD_HEAD_DIM, SSD_STATE), 0.5),
        "c": nrm((DEC_BATCH, D), 1.0),
        "c_ctx": nrm((D,), 1.0),
        "norm_mix_g": 1.0 + nrm((DEPTH, D), 0.02),
        "norm_ffn_g": 1.0 + nrm((DEPTH, D), 0.02),
        "norm_f_g": 1.0 + nrm((D,), 0.02),
        "ada_w": nrm((DEPTH, D, 6 * D), 0.5 * D ** -0.5),
        "ada_b": nrm((DEPTH, 6 * D), 0.02),
        "sc_w_in": nrm((N_CONV_LAYERS, D, 3 * D), D ** -0.5),
        "sc_conv_w": nrm((N_CONV_LAYERS, CONV_W, D), CONV_W ** -0.5),
        "sc_w_out": nrm((N_CONV_LAYERS, D, D), D ** -0.5),
        "ssd_w_in": nrm((N_SSD_LAYERS, D, SSD_IN_DIM), D ** -0.5),
        "ssd_conv_w": nrm((N_SSD_LAYERS, CONV_W, SSD_CONV_DIM), CONV_W ** -0.5),
        "ssd_conv_b": nrm((N_SSD_LAYERS, SSD_CONV_DIM), 0.02),
        "ssd_dt_bias": dt0 + jnp.log(-jnp.expm1(-dt0)),
        "ssd_a_log": jnp.log(jax.random.uniform(next(ks), (N_SSD_LAYERS, 2, SSD_HEADS), jnp.float32, 1.0, 16.0)),
        "ssd_d": 0.5 + nrm((N_SSD_LAYERS, 2, SSD_HEADS), 0.05),
        "ssd_norm_g": 1.0 + nrm((N_SSD_LAYERS, D_INNER), 0.02),
        "ssd_w_out": nrm((N_SSD_LAYERS, D_INNER, D), D_INNER ** -0.5),
        "peer_wq": nrm((DEPTH, D, PEER_HEADS * PEER_KEY_DIM), D ** -0.5),
        "peer_keys": nrm((DEPTH, PEER_HEADS, 2, N_KEYS, PEER_KEY_DIM // 2), (PEER_KEY_DIM // 2) ** -0.5),
        "peer_u": nrm((DEPTH, N_EXPERTS, D), D ** -0.5),
        "peer_v": nrm((DEPTH, N_EXPERTS, D), PEER_HEADS ** -0.5),
    }


def reference(x_prompt, x_sample, state_ssm, c, c_ctx, norm_mix_g, norm_ffn_g, norm_f_g,
              ada_w, ada_b, sc_w_in, sc_conv_w, sc_w_out, ssd_w_in, ssd_conv_w, ssd_conv_b,
              ssd_dt_bias, ssd_a_log, ssd_d, ssd_norm_g, ssd_w_out, peer_wq, peer_keys,
              peer_u, peer_v):
    weights = dict(norm_mix_g=norm_mix_g, norm_ffn_g=norm_ffn_g, norm_f_g=norm_f_g,
                   ada_w=ada_w, ada_b=ada_b, sc_w_in=sc_w_in, sc_conv_w=sc_conv_w,
                   sc_w_out=sc_w_out, ssd_w_in=ssd_w_in, ssd_conv_w=ssd_conv_w,
                   ssd_conv_b=ssd_conv_b, ssd_dt_bias=ssd_dt_bias, ssd_a_log=ssd_a_log,
                   ssd_d=ssd_d, ssd_norm_g=ssd_norm_g, ssd_w_out=ssd_w_out, peer_wq=peer_wq,
                   peer_keys=peer_keys, peer_u=peer_u, peer_v=peer_v)
    ctx_len = x_prompt.shape[1]
    y_prompt, state_ssm_new = run_trunk(x_prompt, c_ctx[None, :], (1, ctx_len), None, **weights)
    rows = x_sample.shape[1] // GRID_W
    y_sample, _ = run_trunk(x_sample, c, (rows, GRID_W), state_ssm, **weights)
    return (y_prompt, y_sample, state_ssm_new)
```

```python
import contextlib
import os
import types

import numpy as np
import concourse.bass as bass
import concourse.mybir as mybir
from concourse.bass_utils import run_bass_kernel_spmd

F32 = mybir.dt.float32
BF16 = mybir.dt.bfloat16
I32 = mybir.dt.int32
U32 = mybir.dt.uint32
ALU = mybir.AluOpType
AF = mybir.ActivationFunctionType

PE, ACT, DVE, POOL, SP = "pe", "act", "dve", "pool", "sp"
EPOCH = 1000000

D = 1024
KC = 8
T = 2560
NG = 5
GT = 512
EPS = 1e-6
NEG = -1.0e30


def _freeze(fn):
    if fn is None or fn.__closure__ is None:
        return fn
    cells = []
    for c in fn.__closure__:
        try:
            cells.append(types.CellType(c.cell_contents))
        except ValueError:
            cells.append(c)
    return types.FunctionType(fn.__code__, fn.__globals__, fn.__name__, fn.__defaults__, tuple(cells))


class Res:
    __slots__ = ("name", "semname", "w", "w_eng", "rd", "wsem", "wcnt", "rsem", "rcnt")

    def __init__(self, name, semname=None):
        self.name = name
        self.semname = semname or name
        self.w = None
        self.w_eng = None
        self.rd = {}
        self.wsem = None
        self.wcnt = 0
        self.rsem = None
        self.rcnt = 0


class Prog:
    def __init__(self, nc, stack):
        self.nc = nc
        self.stack = stack
        self.streams = {e: [] for e in (PE, ACT, DVE, POOL, SP)}
        self.cnt = {e: 0 for e in self.streams}
        self.pending = {e: [] for e in self.streams}
        self.known = {e: {} for e in self.streams}
        self.sems = {}
        self.nsem = 0
        self.final_tokens = {}
        self.all_dma_tokens = {}
        self.dma_cnt = {}

    def _sem(self, key):
        if key not in self.sems:
            self.sems[key] = self.stack.enter_context(self.nc.semaphore("s%d" % self.nsem))
            self.nsem += 1
        return self.sems[key]

    def _eng_token(self, eng, count):
        ep = (count - 1) // EPOCH
        return (("E", eng, ep), count - ep * EPOCH)

    def _deps(self, eng, reads, writes, dma_wsem_key=None):
        need = {}

        def add(tok):
            if tok is None:
                return
            k, v = tok
            if need.get(k, 0) < v:
                need[k] = v

        for r in reads:
            if r.w is not None:
                add(r.w)
        for w in writes:
            if w.w is not None:
                if dma_wsem_key is not None and w.w[0] == dma_wsem_key:
                    pass
                elif w.w_eng == PE and eng == PE:
                    pass
                else:
                    add(w.w)
            for k, (v, e) in w.rd.items():
                if e == eng and e in (PE, ACT, DVE):
                    continue
                add((k, v))
        waits = []
        kn = self.known[eng]
        for k, v in need.items():
            if kn.get(k, 0) < v:
                kn[k] = v
                waits.append((k, v))
        return waits

    def op(self, eng, fn, reads=(), writes=(), inc=True):
        waits = self._deps(eng, reads, writes)
        item = {"waits": waits, "fn": _freeze(fn), "inc": None}
        self.streams[eng].append(item)
        if inc:
            self.cnt[eng] += 1
            tok = self._eng_token(eng, self.cnt[eng])
            item["inc"] = (tok[0], 1)
            self._sem(tok[0])
            for (rs, ws) in self.pending[eng] + [(reads, writes)]:
                for r in rs:
                    r.rd[tok[0]] = (tok[1], eng)
                for w in ws:
                    w.w = tok
                    w.w_eng = eng
                    w.rd = {}
            self.pending[eng] = []
        else:
            self.pending[eng].append((reads, writes))
        return item

    def dma(self, eng, fn, reads=(), writes=(), sbuf=None, store=False):
        assert not self.pending[eng]
        key = ("R" if store else "W", sbuf.semname)
        if store:
            sbuf.rsem = key
        else:
            sbuf.wsem = key
        self.dma_cnt[key] = self.dma_cnt.get(key, 0) + 1
        val = 16 * self.dma_cnt[key]
        waits = self._deps(eng, reads, writes, dma_wsem_key=None if store else key)
        self._sem(key)
        item = {"waits": waits, "fn": _freeze(fn), "inc": (key, 16)}
        self.streams[eng].append(item)
        tok = (key, val)
        for r in reads:
            r.rd[key] = (val, "dma")
        for w in writes:
            w.w = tok
            w.w_eng = "dma"
            w.rd = {}
        self.all_dma_tokens[key] = val
        if store:
            self.final_tokens[key] = val
        return item

    def barrier(self):
        toks = dict(self.all_dma_tokens)
        for e in self.streams:
            assert not self.pending[e]
            if self.cnt[e] > 0:
                k, v = self._eng_token(e, self.cnt[e])
                toks[k] = v
        for e in self.streams:
            waits = []
            kn = self.known[e]
            for k, v in toks.items():
                if k == ("E", e, (self.cnt[e] - 1) // EPOCH) and e != POOL:
                    pass
                if kn.get(k, 0) < v:
                    kn[k] = v
                    waits.append((k, v))
            self.streams[e].append({"waits": waits, "fn": None, "inc": None})

    def emit(self):
        nc = self.nc
        for e in self.streams:
            assert not self.pending[e], e
        fin = [(k, v) for k, v in self.final_tokens.items()]
        self.streams[SP].append({"waits": fin, "fn": None, "inc": None})
        sems = self.sems
        streams = self.streams

        def run(engobj, items):
            for it in items:
                for (k, v) in it["waits"]:
                    engobj.wait_ge(sems[k], v)
                if it["fn"] is None:
                    continue
                try:
                    ins = it["fn"](engobj)
                except Exception:
                    print("EMIT FAIL at item", items.index(it), "of", len(items), it["waits"], it["inc"])
                    raise
                if it["inc"] is not None:
                    ins.then_inc(sems[it["inc"][0]], it["inc"][1])

        with nc.Block() as block:
            @block.tensor
            def _(e):
                run(e, streams[PE])

            @block.scalar
            def _(e):
                run(e, streams[ACT])

            @block.vector
            def _(e):
                run(e, streams[DVE])

            @block.gpsimd
            def _(e):
                run(e, streams[POOL])

            @block.sync
            def _(e):
                run(e, streams[SP])


class Buf:
    def __init__(self, t, name, base=None):
        self.t = t
        self.r = Res(name, base)

    def __getitem__(self, k):
        return self.t[k]


SP_NMG = 0
SP_NFG = 16
SP_NFF = 32
SP_ADAB = 40
SP_SCW = 136
SP_SSW = 160
SP_SSB = 256
SP_SNG = 288
NSP = 304
RP_DTB = 0
RP_ALOG = 64
RP_D = 128
ROWP_N = 192


class Builder:
    def __init__(self, do_ssd=True, do_peer=True, n_slots=128):
        self.do_ssd = do_ssd
        self.do_peer = do_peer
        self.n_slots = n_slots
        self.nc = bass.Bass("TRN2", target_bir_lowering=False)
        self.uid = 0

    def dram_in(self, name, shape, dt=F32):
        return self.nc.dram_tensor(name, list(shape), dt, kind="ExternalInput").ap()

    def dram_out(self, name, shape, dt=F32):
        return self.nc.dram_tensor(name, list(shape), dt, kind="ExternalOutput").ap()

    def sb(self, stack, name, shape, dt=F32):
        self.uid += 1
        nm = "%s_%d" % (name, self.uid)
        return Buf(stack.enter_context(self.nc.sbuf_tensor(nm, list(shape), dt)), nm, name)

    def ps(self):
        b = self.psb[self.psi % len(self.psb)]
        self.psi += 1
        return b

    def build(self):
        nc = self.nc
        di = self.dram_in
        self.xT_d = di("xT", [D, T])
        self.condT_d = di("condT", [128, KC, 2])
        self.spk_d = di("spk", [128, NSP])
        self.ident_d = di("ident", [128, 128])
        self.ones_d = di("ones", [128, 128])
        self.iota_d = di("iota", [128, 256])
        self.ada_w = di("ada_w", [2, D, 6 * D])
        self.sc_w_in = di("sc_w_in", [1, D, 3 * D])
        self.sc_w_out = di("sc_w_out", [1, D, D])
        self.peer_wq = di("peer_wq", [2, D, 2048])
        self.peer_keys = di("peer_keys", [2, 8, 2, 128, 128])
        self.peer_uv = di("peer_uv", [2, 16384, 2 * D])
        self.ssd_w_in = di("ssd_w_in", [1, D, 6208])
        self.ssd_w_out = di("ssd_w_out", [1, 2048, D])
        self.rowp_d = di("rowp", [128, ROWP_N])
        self.tri_d = [di("trif", [128, 128]), di("trib", [128, 128])]
        self.mneg_d = [di("mnegf", [128, 128]), di("mnegb", [128, 128])]
        self.flags_d = di("flags", [128, 4])
        self.stinit_d = di("stinit", [2, 128, 2048])
        self.st_out = self.dram_out("st_out", [2, 2, 128, 2048])
        self.ypart_d = nc.dram_tensor("ypart", [20, 128, 2048], F32).ap()
        self.ypart_r = [Res("ypart%d" % i) for i in range(20)]
        self.w_in_bf = nc.dram_tensor("w_in_bf", [D, 6208], BF16).ap()
        self.w_out_bf = nc.dram_tensor("w_out_bf", [2048, D], BF16).ap()
        fe_shapes = ((4096, BF16), (2048, BF16), (2048, BF16), (2048, BF16), (128, F32), (128, F32))
        self.fe_d = [nc.dram_tensor("fe%d" % k_, [10, 128, n_], dt_).ap() for k_, (n_, dt_) in enumerate(fe_shapes)]
        self.fe_r = [[Res("fe%d_%d" % (k_, p_)) for p_ in range(10)] for k_ in range(6)]
        self.cc_in = nc.dram_tensor("cc_in", [128, 4096], F32).ap()
        self.cc_out = nc.dram_tensor("cc_out", [128, 4096], F32).ap()
        self.cc_in_r = Res("cc_in")
        self.init_true = [nc.dram_tensor("init_true%d" % d, [128, 2048], F32).ap() for d in range(2)]
        self.init_true_r = [Res("init_true%d" % d) for d in range(2)]
        self.peer_uv_flat = self.peer_uv.rearrange("l n d -> (l n) d")
        self.uv_bf = nc.dram_tensor("uv_bf", [2 * 16384, 2 * D], BF16).ap()
        self.yT_d = self.dram_out("yT", [D, T])

        with contextlib.ExitStack() as st:
            self.st = st
            P = self.P = Prog(nc, st)
            self.psb = [Buf(st.enter_context(nc.psum_tensor("ps%d" % i, [128, 512], F32)), "ps%d" % i)
                        for i in range(8)]
            self.psi = 0
            sb = self.sb
            self.xT = sb(st, "xT", [128, KC, T])
            self.xr = [[Res("x_%d_%d" % (c, g), "xload") for g in range(NG)] for c in range(KC)]
            self.ident = sb(st, "ident", [128, 128])
            self.ones = sb(st, "ones", [128, 128])
            self.iota = sb(st, "iota", [128, 256])
            self.spk = sb(st, "spk", [128, NSP])
            self.condT = sb(st, "condT", [128, KC, 2])
            self.modT = [sb(st, "modT%d" % i, [128, 48, 2]) for i in range(2)]
            self.gs = sb(st, "gs", [128, 2, 2, KC, 2])
            self.epsb = sb(st, "epsb", [128, 1])

            xv = self.xT_d.rearrange("(c p) t -> p c t", p=128)
            for c in range(KC):
                for g in range(NG):
                    P.dma(SP, lambda e, c=c, g=g: e.dma_start(out=self.xT[:, c, g * GT:(g + 1) * GT],
                                                               in_=xv[:, c, g * GT:(g + 1) * GT]),
                          writes=[self.xr[c][g]], sbuf=self.xr[c][g])
            for c in range(KC):
                for g in range(NG):
                    self.xr[c][g].w = (("W", "xload"), 16 * KC * NG)
            for (buf, src) in ((self.ident, self.ident_d), (self.ones, self.ones_d), (self.iota, self.iota_d),
                               (self.spk, self.spk_d), (self.condT, self.condT_d)):
                P.dma(SP, lambda e, buf=buf, src=src: e.dma_start(out=buf[:], in_=src),
                      writes=[buf.r], sbuf=buf.r)
            P.op(DVE, lambda e: e.memset(self.epsb[:], EPS), writes=[self.epsb.r])

            self.phase_tabcast()
            self.phase_ada()
            self.phase_conv_mixer()
            if self.do_peer:
                self.phase_peer(0)
            if self.do_ssd:
                self.phase_ssd()
            if self.do_peer:
                self.phase_peer(1)
            self.phase_final()
            P.emit()
        return nc

    def phase_tabcast(self):
        nc, P = self.nc, self.P
        R = 4
        NBUF = 2
        src = self.peer_uv_flat[0:16384].rearrange("(p a r) d -> a p (r d)", p=128, r=R)
        dst = self.uv_bf[0:16384].rearrange("(p a r) d -> a p (r d)", p=128, r=R)
        nslab = 16384 // (128 * R)
        with contextlib.ExitStack() as ph:
            stg = [self.sb(ph, "tstg%d" % i, [128, R * 2 * D]) for i in range(NBUF)]
            obf = [self.sb(ph, "tobf%d" % i, [128, R * 2 * D], BF16) for i in range(NBUF)]
            def ld(a):
                sg = stg[a % NBUF]
                P.dma(SP, lambda e, sg=sg, a=a: e.dma_start(out=sg[:], in_=src[a]), writes=[sg.r], sbuf=sg.r)

            for a in range(min(NBUF, nslab)):
                ld(a)
            for a in range(nslab):
                sg, ob = stg[a % NBUF], obf[a % NBUF]
                P.op(DVE, lambda e, sg=sg, ob=ob: e.tensor_copy(out=ob[:], in_=sg[:]), reads=[sg.r], writes=[ob.r])
                P.dma(POOL, lambda e, ob=ob, a=a: e.dma_start(out=dst[a], in_=ob[:]), reads=[ob.r], sbuf=ob.r, store=True)
                if a + NBUF < nslab:
                    ld(a + NBUF)
            P.barrier()

    def phase_ada(self):
        nc, P = self.nc, self.P
        with contextlib.ExitStack() as ph:
            sct = self.sb(ph, "sct", [128, KC, 2])
            wst = [self.sb(ph, "adaw%d" % i, [128, KC, 512]) for i in range(2)]
            P.op(ACT, lambda e: e.activation(out=sct[:], in_=self.condT[:], func=AF.Silu),
                 reads=[self.condT.r], writes=[sct.r])
            n = 0
            for i in range(2):
                wv = self.ada_w[i].rearrange("(k p) c -> p k c", p=128)
                pst = self.ps()
                for s in range(12):
                    w = wst[n % 2]
                    n += 1
                    P.dma(SP, lambda e, w=w, s=s, wv=wv: e.dma_start(out=w[:], in_=wv[:, :, s * 512:(s + 1) * 512]),
                          writes=[w.r], sbuf=w.r)
                    for cb in range(4):
                        dc = s * 4 + cb
                        for k in range(KC):
                            P.op(PE, lambda e, w=w, cb=cb, k=k, dc=dc, pst=pst: e.matmul(
                                pst[:, dc * 2:dc * 2 + 2], lhsT=w[:, k, cb * 128:(cb + 1) * 128],
                                rhs=sct[:, k, :], start=(k == 0), stop=(k == KC - 1)),
                                reads=[w.r, sct.r], writes=[pst.r], inc=(k == KC - 1))
                mod = self.modT[i]
                P.op(DVE, lambda e, mod=mod, pst=pst, i=i: e.tensor_tensor(
                    out=mod[:], in0=pst[:, 0:96].rearrange("p (c j) -> p c j", j=2),
                    in1=self.spk[:, SP_ADAB + 48 * i:SP_ADAB + 48 * (i + 1)].unsqueeze(2).to_broadcast([128, 48, 2]),
                    op=ALU.add), reads=[pst.r, self.spk.r], writes=[mod.r])
                for wh, (goff, sclo) in enumerate(((SP_NMG, 8), (SP_NFG, 32))):
                    P.op(DVE, lambda e, mod=mod, i=i, wh=wh, goff=goff, sclo=sclo: e.scalar_tensor_tensor(
                        out=self.gs[:, i, wh], in0=mod[:, sclo:sclo + 8, :], scalar=1.0,
                        in1=self.spk[:, goff + 8 * i:goff + 8 * (i + 1)].unsqueeze(2).to_broadcast([128, 8, 2]),
                        op0=ALU.add, op1=ALU.mult), reads=[mod.r, self.spk.r], writes=[self.gs.r])
            P.barrier()

    def m_sh(self, i, wh, c, ci):
        return self.modT[i][:, (0 if wh == 0 else 24) + c, ci:ci + 1]

    def m_gs(self, i, wh, c, ci):
        return self.gs[:, i, wh, c, ci:ci + 1]

    def m_gt(self, i, wh, c, ci):
        return self.modT[i][:, (16 if wh == 0 else 40) + c, ci:ci + 1]

    def norm_group(self, g, gs_fn, sh_fn, outs, rstd, sq, tmp):
        P = self.P
        ci = 0 if g < 4 else 1
        pss = self.ps()
        sl = slice(g * GT, (g + 1) * GT)
        for c in range(KC):
            P.op(ACT, lambda e, c=c: e.activation(out=sq[:, c % 2, :], in_=self.xT[:, c, sl], func=AF.Square),
                 reads=[self.xr[c][g]], writes=[sq.r])
            P.op(PE, lambda e, c=c: e.matmul(pss[:], lhsT=self.ones[:], rhs=sq[:, c % 2, :],
                                             start=(c == 0), stop=(c == KC - 1)),
                 reads=[self.ones.r, sq.r], writes=[pss.r], inc=True)
        P.op(ACT, lambda e: e.activation(out=rstd[:], in_=pss[:], func=AF.Sqrt, bias=self.epsb[:], scale=1.0 / D),
             reads=[pss.r, self.epsb.r], writes=[rstd.r])
        P.op(DVE, lambda e: e.reciprocal(out=rstd[:], in_=rstd[:]), reads=[rstd.r], writes=[rstd.r])
        for c in range(KC):
            P.op(DVE, lambda e, c=c: e.tensor_tensor(out=tmp[:, c % 2, :], in0=self.xT[:, c, sl], in1=rstd[:],
                                                    op=ALU.mult),
                 reads=[self.xr[c][g], rstd.r], writes=[tmp.r])
            for ob in outs:
                if sh_fn is None:
                    P.op(ACT, lambda e, c=c, ob=ob: e.activation(out=ob[:, c, :], in_=tmp[:, c % 2, :],
                                                                 func=AF.Identity, scale=gs_fn(c, ci)),
                         reads=[tmp.r, self.gs.r, self.spk.r], writes=[ob.r])
                else:
                    P.op(ACT, lambda e, c=c, ob=ob: e.activation(out=ob[:, c, :], in_=tmp[:, c % 2, :],
                                                                 func=AF.Identity, scale=gs_fn(c, ci),
                                                                 bias=sh_fn(c, ci)),
                         reads=[tmp.r, self.gs.r, self.modT[0].r, self.modT[1].r], writes=[ob.r])

    def phase_conv_mixer(self):
        nc, P = self.nc, self.P
        with contextlib.ExitStack() as ph:
            sb = self.sb
            hnb = sb(ph, "hnb", [128, KC, GT], BF16)
            rstd = sb(ph, "rstd", [128, GT])
            sq = sb(ph, "sq", [128, 2, GT])
            tmp = sb(ph, "tmp", [128, 2, GT])
            wst = [sb(ph, "wst%d" % i, [128, KC, 512]) for i in range(2)]
            wbf = [sb(ph, "wbf%d" % i, [128, KC, 512], BF16) for i in range(2)]
            bg = sb(ph, "bg", [128, KC, GT])
            cg = [sb(ph, "cg%d" % k, [128, GT]) for k in range(KC)]
            yv = [sb(ph, "yv%d" % k, [128, GT]) for k in range(2)]
            mm = sb(ph, "mm", [128, KC, GT], BF16)
            bgr = [Res("bg%d" % k) for k in range(KC)]
            mmr = [Res("mm%d" % k) for k in range(KC)]
            w_in = self.sc_w_in[0].rearrange("(k p) c -> p k c", p=128)
            w_out = self.sc_w_out[0].rearrange("(k p) c -> p k c", p=128)
            nload = 0

            def load_w(view, c0):
                nonlocal nload
                ws, wb = wst[nload % 2], wbf[nload % 2]
                ce = (ACT, DVE)[nload % 2]
                nload += 1
                P.dma(SP, lambda e: e.dma_start(out=ws[:], in_=view[:, :, c0:c0 + 512]), writes=[ws.r], sbuf=ws.r)
                if ce == ACT:
                    P.op(ACT, lambda e: e.copy(out=wb[:], in_=ws[:]), reads=[ws.r], writes=[wb.r])
                else:
                    P.op(DVE, lambda e: e.tensor_copy(out=wb[:], in_=ws[:]), reads=[ws.r], writes=[wb.r])
                return wb

            for g in range(NG):
                ci = 0 if g < 4 else 1
                sl = slice(g * GT, (g + 1) * GT)
                rows, cols = (8, 64) if g < 4 else (2, 256)
                self.norm_group(g, lambda c, ci: self.m_gs(0, 0, c, ci), lambda c, ci: self.m_sh(0, 0, c, ci),
                                [hnb], rstd, sq, tmp)
                for s in range(6):
                    wb = load_w(w_in, s * 512)
                    for cb in range(4):
                        blk = s * 4 + cb
                        pt = self.ps()
                        for k in range(KC):
                            P.op(PE, lambda e, wb=wb, cb=cb, k=k, pt=pt: e.matmul(
                                pt[:], lhsT=wb[:, k, cb * 128:(cb + 1) * 128], rhs=hnb[:, k, :],
                                start=(k == 0), stop=(k == KC - 1)),
                                reads=[wb.r, hnb.r], writes=[pt.r], inc=(k == KC - 1))
                        if blk < 8:
                            P.op(ACT, lambda e, blk=blk, pt=pt: e.copy(out=bg[:, blk, :], in_=pt[:]),
                                 reads=[pt.r], writes=[bgr[blk]])
                        elif blk < 16:
                            cc = cg[blk - 8]
                            P.op(ACT, lambda e, cc=cc, pt=pt: e.copy(out=cc[:], in_=pt[:]),
                                 reads=[pt.r], writes=[cc.r])
                        else:
                            k8 = blk - 16
                            cc = cg[k8]
                            y = yv[k8 % 2]
                            P.op(DVE, lambda e, cc=cc, pt=pt: e.tensor_tensor(out=cc[:], in0=cc[:], in1=pt[:],
                                                                              op=ALU.mult),
                                 reads=[cc.r, pt.r], writes=[cc.r])
                            wcol = lambda tap, k8=k8: self.spk[:, SP_SCW + tap * 8 + k8:SP_SCW + tap * 8 + k8 + 1]
                            u3 = cc[:].rearrange("p (r w) -> p r w", w=cols)
                            y3 = y[:].rearrange("p (r w) -> p r w", w=cols)
                            P.op(DVE, lambda e, cc=cc, y=y, wcol=wcol: e.tensor_scalar(
                                out=y[:], in0=cc[:], scalar1=wcol(1), scalar2=None, op0=ALU.mult),
                                reads=[cc.r, self.spk.r], writes=[y.r])
                            P.op(DVE, lambda e, u3=u3, y3=y3, wcol=wcol: e.scalar_tensor_tensor(
                                out=y3[:, :, 1:], in0=u3[:, :, :cols - 1], scalar=wcol(0), in1=y3[:, :, 1:],
                                op0=ALU.mult, op1=ALU.add), reads=[cc.r, y.r, self.spk.r], writes=[y.r])
                            P.op(DVE, lambda e, u3=u3, y3=y3, wcol=wcol: e.scalar_tensor_tensor(
                                out=y3[:, :, :cols - 1], in0=u3[:, :, 1:], scalar=wcol(2), in1=y3[:, :, :cols - 1],
                                op0=ALU.mult, op1=ALU.add), reads=[cc.r, y.r, self.spk.r], writes=[y.r])
                            P.op(DVE, lambda e, k8=k8, y=y: e.tensor_tensor(out=mm[:, k8, :], in0=bg[:, k8, :],
                                                                          in1=y[:], op=ALU.mult),
                                 reads=[bgr[k8], y.r], writes=[mmr[k8]])
                for s in range(2):
                    wb = load_w(w_out, s * 512)
                    for cb in range(4):
                        dc = s * 4 + cb
                        pt = self.ps()
                        for k in range(KC):
                            P.op(PE, lambda e, wb=wb, cb=cb, k=k, pt=pt: e.matmul(
                                pt[:], lhsT=wb[:, k, cb * 128:(cb + 1) * 128], rhs=mm[:, k, :],
                                start=(k == 0), stop=(k == KC - 1)),
                                reads=[wb.r] + mmr, writes=[pt.r], inc=(k == KC - 1))
                        P.op(DVE, lambda e, dc=dc, pt=pt, ci=ci: e.scalar_tensor_tensor(
                            out=self.xT[:, dc, sl], in0=pt[:], scalar=self.m_gt(0, 0, dc, ci), in1=self.xT[:, dc, sl],
                            op0=ALU.mult, op1=ALU.add),
                            reads=[pt.r, self.modT[0].r, self.xr[dc][g]], writes=[self.xr[dc][g]])
            P.barrier()

    def phase_peer(self, li):
        nc, P = self.nc, self.P
        if li == 1 and getattr(self, "tab1_tok", None) is not None:
            P.streams[POOL].append({"waits": [self.tab1_tok], "fn": None, "inc": None})
        NS = self.n_slots
        with contextlib.ExitStack() as ph:
            sb = self.sb
            NB = 10
            gb = [sb(ph, "gb%d" % i, [128, 2 * D], BF16) for i in range(NB)]
            hnf = sb(ph, "hnf", [128, KC, GT])
            rstd = sb(ph, "rstd", [128, GT])
            sq = sb(ph, "sq", [128, 2, GT])
            tmp = sb(ph, "tmp", [128, 2, GT])
            keyT = sb(ph, "keyT", [128, 16, 128])
            wqb = [sb(ph, "wqb%d" % i, [128, KC, 128]) for i in range(2)]
            qT = [sb(ph, "qT%d" % i, [128, GT]) for i in range(2)]
            scs = [sb(ph, "scs%d" % i, [128, 4, 128]) for i in range(2)]
            mrtL = [sb(ph, "mrt%d" % i, [128, 128]) for i in range(4)]
            vv_r = [Res("vv%d" % i) for i in range(4)]
            iu_r = [Res("iu%d" % i) for i in range(4)]
            vv = sb(ph, "vv", [128, 4, 16, 16])
            iu = sb(ph, "iu", [128, 4, 16, 16], U32)
            i_f = sb(ph, "i_f", [128, 16, 16])
            i1s = sb(ph, "i1s", [128, 8, 16])
            pos = sb(ph, "pos", [128, 8, 16], U32)
            posf = sb(ph, "posf", [128, 8, 16])
            pa_ = sb(ph, "pa_", [128, 8, 16])
            pb_ = sb(ph, "pb_", [128, 8, 16])
            idx1 = sb(ph, "idx1", [128, 8, 16])
            idx2 = sb(ph, "idx2", [128, 8, 16])
            thr16 = sb(ph, "thr16", [128, 16])
            P.op(DVE, lambda e: e.tensor_scalar(out=thr16[:], in0=self.iota[:, 0:16], scalar1=16.0, scalar2=16.0,
                                               op0=ALU.mult, op1=ALU.add), reads=[self.iota.r], writes=[thr16.r])
            P.op(DVE, lambda e: e.memset(thr16[:, 15:16], 1.0e9), writes=[thr16.r])
            cand = sb(ph, "cand", [128, 8, 256])
            cidx = cand
            cwork = sb(ph, "cwork", [128, 256])
            sc16 = sb(ph, "sc16", [128, 8, 16])
            idxf = sb(ph, "idxf", [128, 128])
            idxiL = [sb(ph, "idxi%d" % i, [128, 128], I32) for i in range(2)]
            ee = sb(ph, "ee", [128, 8, 16])
            ggL = [sb(ph, "gg%d" % i, [128, 128]) for i in range(2)]
            act_r = [Res("act%d" % j) for j in range(128)]
            coef_r = [Res("coef%d" % j) for j in range(128)]
            negm = sb(ph, "negm", [128, 8])
            esum = sb(ph, "esum", [128, 8])
            hn2 = sb(ph, "hn2", [128, D])
            act = sb(ph, "act", [128, 128])
            coef = sb(ph, "coef", [128, 128])
            junk2v = cand[:, 0:4, :].rearrange("p a b -> p (a b)")
            dg = [sb(ph, "dg%d" % i, [128, 128], BF16) for i in range(3)]
            zer = sb(ph, "zer", [128, 128])
            P.op(DVE, lambda e: e.memset(zer[:], 0.0), writes=[zer.r])
            self.psb_all = self.psb
            pacc = self.psb_all[6:8]
            self.psb = self.psb_all[0:6]
            nq = 0
            ngb = 0

            keyn_b = hnf
            keyn = hnf[:, 0:4, :].rearrange("p k (a e) -> p (k a) e", e=128)
            kv = self.peer_keys[li].rearrange("h p n e -> n (h p) e")
            P.dma(SP, lambda e: e.dma_start(out=keyn, in_=kv), writes=[keyn_b.r], sbuf=keyn_b.r)
            for q4 in range(4):
                pt = self.ps()
                for j in range(4):
                    hp = q4 * 4 + j
                    P.op(PE, lambda e, hp=hp, j=j, pt=pt: e.transpose(out=pt[:, j * 128:(j + 1) * 128],
                                                                    in_=keyn[:, hp, :], identity=self.ident[:]),
                         reads=[keyn_b.r, self.ident.r], writes=[pt.r], inc=(j == 3))
                P.op(ACT, lambda e, q4=q4, pt=pt: e.copy(out=keyT[:, q4 * 4:(q4 + 1) * 4, :],
                                                        in_=pt[:].rearrange("p (j n) -> p j n", n=128)),
                     reads=[pt.r], writes=[keyT.r])

            wqv = self.peer_wq[li].rearrange("(k p) c -> p k c", p=128)
            for g in range(NG):
                ci = 0 if g < 4 else 1
                self.norm_group(g, lambda c, ci: self.m_gs(li, 1, c, ci), lambda c, ci: self.m_sh(li, 1, c, ci),
                                [hnf], rstd, sq, tmp)
                for hp in range(16):
                    wq = wqb[nq % 2]
                    qt = qT[nq % 2]
                    ss = scs[nq % 2]
                    nq += 1
                    P.dma(SP, lambda e, wq=wq, hp=hp: e.dma_start(out=wq[:], in_=wqv[:, :, hp * 128:(hp + 1) * 128]),
                          writes=[wq.r], sbuf=wq.r)
                    pq = self.ps()
                    for k in range(KC):
                        P.op(PE, lambda e, wq=wq, k=k, pq=pq: e.matmul(pq[:], lhsT=wq[:, k, :], rhs=hnf[:, k, :],
                                                                       start=(k == 0), stop=(k == KC - 1)),
                             reads=[wq.r, hnf.r], writes=[pq.r], inc=(k == KC - 1))
                    P.op(ACT, lambda e, qt=qt, pq=pq: e.copy(out=qt[:], in_=pq[:]), reads=[pq.r], writes=[qt.r])
                    psc = self.ps()
                    for tt in range(4):
                        P.op(PE, lambda e, qt=qt, tt=tt, psc=psc, hp=hp: e.matmul(
                            psc[:, tt * 128:(tt + 1) * 128], lhsT=qt[:, tt * 128:(tt + 1) * 128],
                            rhs=keyT[:, hp, :], start=True, stop=True),
                            reads=[qt.r, keyT.r], writes=[psc.r], inc=(tt == 3))
                    P.op(ACT, lambda e, ss=ss, psc=psc: e.copy(out=ss[:], in_=psc[:].rearrange("p (t n) -> p t n", n=128)),
                         reads=[psc.r], writes=[ss.r])
                    for tt in range(4):
                        P.op(DVE, lambda e, ss=ss, tt=tt, hp=hp: e.max(out=vv[:, tt, hp, 0:8], in_=ss[:, tt, :]),
                             reads=[ss.r], writes=[vv_r[tt]])
                    for tt in range(4):
                        P.op(DVE, lambda e, ss=ss, tt=tt, hp=hp: e.match_replace(
                            out=mrtL[tt][:], in_to_replace=vv[:, tt, hp, 0:8], in_values=ss[:, tt, :], imm_value=NEG),
                            reads=[ss.r, vv_r[tt]], writes=[mrtL[tt].r])
                    for tt in range(4):
                        P.op(DVE, lambda e, tt=tt, hp=hp: e.max(out=vv[:, tt, hp, 8:16], in_=mrtL[tt][:]),
                             reads=[mrtL[tt].r], writes=[vv_r[tt]])
                    for tt in range(4):
                        P.op(DVE, lambda e, ss=ss, tt=tt, hp=hp: e.max_index(
                            out=iu[:, tt, hp, 0:8], in_max=vv[:, tt, hp, 0:8], in_values=ss[:, tt, :]),
                            reads=[ss.r, vv_r[tt]], writes=[iu_r[tt]])
                    for tt in range(4):
                        P.op(DVE, lambda e, ss=ss, tt=tt, hp=hp: e.max_index(
                            out=iu[:, tt, hp, 8:16], in_max=vv[:, tt, hp, 8:16], in_values=ss[:, tt, :]),
                            reads=[ss.r, vv_r[tt]], writes=[iu_r[tt]])
                def prep(tt):
                    idxi, gg = idxiL[tt % 2], ggL[tt % 2]
                    P.op(DVE, lambda e, tt=tt: e.tensor_copy(out=i_f[:], in_=iu[:, tt]), reads=[iu_r[tt]], writes=[i_f.r])
                    v1 = vv[:, tt, 0:16:2, :]
                    v2 = vv[:, tt, 1:16:2, :]
                    c4 = cand[:].rearrange("p h (a b) -> p h a b", b=16)
                    x4 = cidx[:].rearrange("p h (a b) -> p h a b", b=16)
                    P.op(DVE, lambda e, v1=v1, v2=v2, c4=c4: e.tensor_tensor(
                        out=c4, in0=v1.unsqueeze(3).to_broadcast([128, 8, 16, 16]),
                        in1=v2.unsqueeze(2).to_broadcast([128, 8, 16, 16]), op=ALU.add),
                        reads=[vv_r[tt]], writes=[cand.r])
                    P.op(DVE, lambda e: e.tensor_scalar(out=i1s[:], in0=i_f[:, 0:16:2, :], scalar1=128.0,
                                                       scalar2=float(li * 16384), op0=ALU.mult, op1=ALU.add),
                         reads=[i_f.r], writes=[i1s.r])
                    for h in range(8):
                        P.op(DVE, lambda e, h=h: e.max(out=sc16[:, h, 0:8], in_=cand[:, h, :]),
                             reads=[cand.r], writes=[sc16.r])
                        P.op(DVE, lambda e, h=h: e.match_replace(out=cwork[:], in_to_replace=sc16[:, h, 0:8],
                                                                in_values=cand[:, h, :], imm_value=NEG),
                             reads=[cand.r, sc16.r], writes=[cwork.r])
                        P.op(DVE, lambda e, h=h: e.max(out=sc16[:, h, 8:16], in_=cwork[:]),
                             reads=[cwork.r], writes=[sc16.r])
                    for h in range(8):
                        for k8 in range(2):
                            P.op(DVE, lambda e, h=h, k8=k8: e.max_index(
                                out=pos[:, h, k8 * 8:(k8 + 1) * 8], in_max=sc16[:, h, k8 * 8:(k8 + 1) * 8],
                                in_values=cand[:, h, :]), reads=[cand.r, sc16.r], writes=[pos.r])
                    P.op(DVE, lambda e: e.tensor_copy(out=posf[:], in_=pos[:]), reads=[pos.r], writes=[posf.r])
                    P.op(DVE, lambda e, x4=x4: e.tensor_tensor(
                        out=x4, in0=posf[:].unsqueeze(3).to_broadcast([128, 8, 16, 16]),
                        in1=thr16[:].unsqueeze(1).unsqueeze(1).to_broadcast([128, 8, 16, 16]), op=ALU.is_ge),
                        reads=[posf.r, thr16.r], writes=[cand.r])
                    P.op(DVE, lambda e, x4=x4: e.reduce_sum(out=pa_[:], in_=x4, axis=mybir.AxisListType.X),
                         reads=[cand.r], writes=[pa_.r])
                    P.op(DVE, lambda e: e.scalar_tensor_tensor(out=pb_[:], in0=pa_[:], scalar=-16.0, in1=posf[:],
                                                               op0=ALU.mult, op1=ALU.add), reads=[pa_.r, posf.r], writes=[pb_.r])
                    io16 = self.iota[:, 0:16].unsqueeze(1).unsqueeze(1).to_broadcast([128, 8, 16, 16])
                    for (sel, tab, dsti) in ((pa_, i1s[:], idx1), (pb_, i_f[:, 1:16:2, :], idx2)):
                        P.op(DVE, lambda e, sel=sel, x4=x4: e.tensor_tensor(
                            out=x4, in0=io16, in1=sel[:].unsqueeze(3).to_broadcast([128, 8, 16, 16]), op=ALU.is_equal),
                            reads=[self.iota.r, sel.r], writes=[cand.r])
                        P.op(DVE, lambda e, tab=tab, x4=x4: e.tensor_tensor(
                            out=x4, in0=x4, in1=tab.unsqueeze(2).to_broadcast([128, 8, 16, 16]), op=ALU.mult),
                            reads=[cand.r, i1s.r, i_f.r], writes=[cand.r])
                        P.op(DVE, lambda e, dsti=dsti, x4=x4: e.reduce_sum(out=dsti[:], in_=x4, axis=mybir.AxisListType.X),
                             reads=[cand.r], writes=[dsti.r])
                    P.op(DVE, lambda e: e.tensor_tensor(out=idxf[:].rearrange("p (h k) -> p h k", k=16), in0=idx1[:], in1=idx2[:],
                                                        op=ALU.add), reads=[idx1.r, idx2.r], writes=[idxf.r])
                    P.op(DVE, lambda e: e.tensor_copy(out=idxi[:], in_=idxf[:]), reads=[idxf.r], writes=[idxi.r])
                    P.op(DVE, lambda e: e.tensor_scalar(out=negm[:], in0=sc16[:, :, 0], scalar1=-1.0, scalar2=None,
                                                       op0=ALU.mult), reads=[sc16.r], writes=[negm.r])
                    for h in range(8):
                        P.op(ACT, lambda e, h=h: e.activation(out=ee[:, h, :], in_=sc16[:, h, :], func=AF.Exp,
                                                             bias=negm[:, h:h + 1], scale=1.0),
                             reads=[sc16.r, negm.r], writes=[ee.r])
                    P.op(DVE, lambda e: e.reduce_sum(out=esum[:], in_=ee[:], axis=mybir.AxisListType.X),
                         reads=[ee.r], writes=[esum.r])
                    P.op(DVE, lambda e: e.reciprocal(out=esum[:], in_=esum[:]), reads=[esum.r], writes=[esum.r])
                    P.op(DVE, lambda e: e.tensor_tensor(
                        out=gg[:].rearrange("p (h k) -> p h k", k=16), in0=ee[:],
                        in1=esum[:].unsqueeze(2).to_broadcast([128, 8, 16]), op=ALU.mult),
                        reads=[ee.r, esum.r], writes=[gg.r])

                def slots(tt):
                    nonlocal ngb
                    idxi, gg = idxiL[tt % 2], ggL[tt % 2]
                    tsl = slice(g * GT + tt * 128, g * GT + (tt + 1) * 128)
                    for half in range(2):
                        pt = self.ps()
                        for j in range(4):
                            c = half * 4 + j
                            P.op(PE, lambda e, c=c, j=j, pt=pt, tt=tt: e.transpose(
                                out=pt[:, j * 128:(j + 1) * 128], in_=hnf[:, c, tt * 128:(tt + 1) * 128],
                                identity=self.ident[:]), reads=[hnf.r, self.ident.r], writes=[pt.r], inc=(j == 3))
                        P.op(ACT, lambda e, half=half, pt=pt: e.copy(out=hn2[:, half * 512:(half + 1) * 512], in_=pt[:]),
                             reads=[pt.r], writes=[hn2.r])
                    for pa in pacc:
                        P.op(PE, lambda e, pa=pa: e.matmul(pa[:], lhsT=zer[:], rhs=hnf[:, 0, :], start=True, stop=False),
                             reads=[zer.r, hnf.r], writes=[pa.r])
                    for j in range(NS):
                        b = gb[ngb % NB]
                        dgb = dg[ngb % 3]
                        ngb += 1
                        P.dma(POOL, lambda e, b=b, j=j: e.indirect_dma_start(
                            out=b[:], out_offset=None, in_=self.uv_bf,
                            in_offset=bass.IndirectOffsetOnAxis(ap=idxi[:, j:j + 1], axis=0)),
                            reads=[idxi.r], writes=[b.r], sbuf=b.r)
                        P.op(DVE, lambda e, b=b, j=j: e.scalar_tensor_tensor(
                            out=junk2v, in0=hn2[:], scalar=1.0, in1=b[:, 0:D], op0=ALU.mult, op1=ALU.mult,
                            accum_out=act[:, j:j + 1]), reads=[hn2.r, b.r], writes=[act_r[j]])
                        P.op(ACT, lambda e, j=j: e.activation(out=coef[:, j:j + 1], in_=act[:, j:j + 1], func=AF.Gelu),
                             reads=[act_r[j]], writes=[coef_r[j]])
                        P.op(ACT, lambda e, j=j: e.activation(out=coef[:, j:j + 1], in_=coef[:, j:j + 1], func=AF.Copy,
                                                             scale=gg[:, j:j + 1]), reads=[coef_r[j], gg.r], writes=[coef_r[j]])
                        P.op(ACT, lambda e, dgb=dgb, j=j: e.activation(out=dgb[:], in_=self.ident[:], func=AF.Copy,
                                                                      scale=coef[:, j:j + 1]),
                             reads=[self.ident.r, coef_r[j]], writes=[dgb.r])
                        for c in range(KC):
                            pa = pacc[c // 4]
                            P.op(PE, lambda e, b=b, dgb=dgb, c=c, pa=pa, j=j: e.matmul(
                                pa[:, (c % 4) * 128:(c % 4 + 1) * 128], lhsT=b[:, D + c * 128:D + (c + 1) * 128], rhs=dgb[:],
                                start=False, stop=(j == NS - 1)),
                                reads=[b.r, dgb.r], writes=[pa.r], inc=(c == KC - 1))
                    for c in range(KC):
                        pa = pacc[c // 4]
                        P.op(DVE, lambda e, c=c, pa=pa, ci=ci, tsl=tsl: e.scalar_tensor_tensor(
                            out=self.xT[:, c, tsl], in0=pa[:, (c % 4) * 128:(c % 4 + 1) * 128],
                            scalar=self.m_gt(li, 1, c, ci), in1=self.xT[:, c, tsl], op0=ALU.mult, op1=ALU.add),
                            reads=[pa.r, self.modT[li].r, self.xr[c][g]], writes=[self.xr[c][g]])

                prep(0)
                for tt in range(4):
                    if tt < 3:
                        prep(tt + 1)
                    slots(tt)
            self.psb = self.psb_all
            P.barrier()

    def norm_slice(self, tok0, W, g, ci, gs_fn, sh_fn, ob, rstd, sq, tmp):
        P = self.P
        pss = self.ps()
        sl = slice(tok0, tok0 + W)
        for c in range(KC):
            P.op(ACT, lambda e, c=c: e.activation(out=sq[:, c % 2, :], in_=self.xT[:, c, sl], func=AF.Square),
                 reads=[self.xr[c][g]], writes=[sq.r])
            P.op(PE, lambda e, c=c: e.matmul(pss[:, 0:W], lhsT=self.ones[:], rhs=sq[:, c % 2, :],
                                             start=(c == 0), stop=(c == KC - 1)),
                 reads=[self.ones.r, sq.r], writes=[pss.r], inc=True)
        P.op(ACT, lambda e: e.activation(out=rstd[:], in_=pss[:, 0:W], func=AF.Sqrt, bias=self.epsb[:], scale=1.0 / D),
             reads=[pss.r, self.epsb.r], writes=[rstd.r])
        P.op(DVE, lambda e: e.reciprocal(out=rstd[:], in_=rstd[:]), reads=[rstd.r], writes=[rstd.r])
        for c in range(KC):
            P.op(DVE, lambda e, c=c: e.tensor_tensor(out=tmp[:, c % 2, :], in0=self.xT[:, c, sl], in1=rstd[:], op=ALU.mult),
                 reads=[self.xr[c][g], rstd.r], writes=[tmp.r])
            P.op(ACT, lambda e, c=c: e.activation(out=ob[:, c, :], in_=tmp[:, c % 2, :], func=AF.Identity,
                                                  scale=gs_fn(c, ci), bias=sh_fn(c, ci)),
                 reads=[tmp.r, self.gs.r, self.modT[0].r, self.modT[1].r], writes=[ob.r])

    def phase_ssd(self):
        nc, P = self.nc, self.P
        key = ("X", "tabl1")
        P._sem(key)
        NCH = 16
        for k_ in range(NCH):
            r0 = 16384 + k_ * (16384 // NCH)
            r1 = r0 + 16384 // NCH
            P.streams[POOL].append({"waits": [], "inc": (key, 16), "fn": _freeze(
                lambda e, r0=r0, r1=r1: e.dma_start(out=self.uv_bf[r0:r1, :], in_=self.peer_uv_flat[r0:r1, :]))})
        self.tab1_tok = (key, 16 * NCH)
        X = mybir.AxisListType.X
        W = 256
        with contextlib.ExitStack() as ph:
            sb = self.sb
            rowp = sb(ph, "rowp", [128, ROWP_N])
            par = [0]
            two = lambda name, shape, dt=F32: [sb(ph, name + str(i), shape, dt) for i in range(2)]
            tri = [sb(ph, "tri%d" % d, [128, 128]) for d in range(2)]
            mneg = [sb(ph, "mneg%d" % d, [128, 128]) for d in range(2)]
            flags = sb(ph, "flags", [128, 4])
            onec = sb(ph, "onec", [128, 1])
            hnb = sb(ph, "hnb", [128, KC, W], BF16)
            rstd = sb(ph, "rstd", [128, W])
            sq = sb(ph, "sq", [128, 2, W])
            tmp = sb(ph, "tmp", [128, 2, W])
            wbf = [sb(ph, "wbf%d" % i, [128, KC * 256], BF16) for i in range(2)]
            wdtb = sb(ph, "wdtb", [128, KC, 64], BF16)
            yb = [sb(ph, "yb%d" % i, [128, W]) for i in range(3)]
            silb = [sb(ph, "silb%d" % i, [128, W]) for i in range(3)]
            BT = sb(ph, "BT", [128, 8, W], BF16)
            CT = sb(ph, "CT", [128, 8, W], BF16)
            xs_tok = sb(ph, "xs_tok", [128, 2, 2048], BF16)
            B_tok = sb(ph, "B_tok", [128, 2, 1024], BF16)
            zs = sb(ph, "zs", [128, 2048], BF16)
            xx = sb(ph, "xx", [128, 64])
            ax = sb(ph, "ax", [128, 64])
            dt = sb(ph, "dt", [128, 2, 64])
            dA = sb(ph, "dA", [128, 2, 64])
            arow = sb(ph, "arow", [128, 64])
            drow = sb(ph, "drow", [128, 32])
            cumcL = two("cumc", [128, 32])
            totL = two("tot", [128, 32])
            wv = sb(ph, "wv", [128, 32])
            dec = sb(ph, "dec", [128, 32])
            Pb = sb(ph, "Pb", [128, 32])
            xdt = [sb(ph, "xdt%d" % d, [128, 2048], BF16) for d in range(2)]
            xw = sb(ph, "xw", [128, 2048], BF16)
            RgL = two("Rg", [128, 512])
            segL = two("seg", [128, 512])
            GmL = [two("Gm%d_" % d, [128, 512], BF16) for d in range(2)]
            EeL = two("Ee", [128, 512])
            CsL = two("Cs", [128, 512], BF16)
            CBtL = two("CBt", [128, 128])
            hT = [sb(ph, "hT%d" % d, [128, 2048]) for d in range(2)]
            hTb = [sb(ph, "hTb%d" % d, [128, 2048], BF16) for d in range(2)]
            ysb = sb(ph, "ysb", [128, 2048])

            t1 = sb(ph, "t1", [128, 256])
            ssq = sb(ph, "ssq", [128, 1])
            ynT = sb(ph, "ynT", [128, 16, 128], BF16)
            class _AB:
                def __init__(self, ap, name):
                    self.ap = ap
                    self.r = Res(name)

                def __getitem__(self, k):
                    return self.ap[k]

            _h0b = hT[0][:].bitcast(BF16)
            zsL = [zs, _AB(_h0b[:, 0:2048], "zs_alt")]
            ynTL = [ynT, _AB(_h0b[:, 2048:4096].rearrange("p (k n) -> p k n", n=128), "ynT_alt")]
            w_in = self.ssd_w_in[0].rearrange("(k p) c -> p k c", p=128)
            w_out = self.ssd_w_out[0].rearrange("(k p) c -> p k c", p=128)
            w_in_bf = self.w_in_bf.rearrange("(k p) c -> p k c", p=128)
            w_out_bf = self.w_out_bf.rearrange("(k p) c -> p k c", p=128)
            nld = [0]

            def load_w(view, c0, wide=True):
                wb = wbf[nld[0] % 2]
                nld[0] += 1
                ncol = 256 if wide else 128
                dst = wb[:].rearrange("p (k c) -> p k c", c=ncol)
                P.dma(SP, lambda e: e.dma_start(out=dst, in_=view[:, :, c0:c0 + ncol]), writes=[wb.r], sbuf=wb.r)
                return wb, dst

            for (buf, src) in ((rowp, self.rowp_d), (tri[0], self.tri_d[0]), (tri[1], self.tri_d[1]),
                               (mneg[0], self.mneg_d[0]), (mneg[1], self.mneg_d[1]), (flags, self.flags_d)):
                P.dma(SP, lambda e, buf=buf, src=src: e.dma_start(out=buf[:], in_=src), writes=[buf.r], sbuf=buf.r)
            P.op(DVE, lambda e: e.memset(onec[:], 1.0), writes=[onec.r])
            P.op(ACT, lambda e: e.activation(out=arow[:], in_=rowp[:, RP_ALOG:RP_ALOG + 64], func=AF.Exp),
                 reads=[rowp.r], writes=[arow.r])
            P.op(DVE, lambda e: e.tensor_scalar(out=arow[:], in0=arow[:], scalar1=-1.0, scalar2=None, op0=ALU.mult),
                 reads=[arow.r], writes=[arow.r])
            P.op(DVE, lambda e: e.tensor_tensor(out=drow[:], in0=rowp[:, RP_D:RP_D + 32], in1=rowp[:, RP_D + 32:RP_D + 64],
                                                op=ALU.add), reads=[rowp.r], writes=[drow.r])
            stgs = [ysb, hT[0], hT[1]]
            cast_eng = (DVE, ACT, DVE)
            nst = 0
            for (src, dstv, K_, ncol, total) in ((w_in, w_in_bf, KC, 256, 6208), (w_out, w_out_bf, 16, 128, D)):
                for c0 in range(0, total, ncol):
                    n_ = min(ncol, total - c0)
                    stg = stgs[nst % 3]
                    wb = wbf[nst % 2]
                    ce = cast_eng[nst % 3]
                    nst += 1
                    sv = stg[:].rearrange("p (k c) -> p k c", c=ncol)[:, :, 0:n_]
                    wv_ = wb[:].rearrange("p (k c) -> p k c", c=ncol)[:, :, 0:n_]
                    P.dma(SP, lambda e, sv=sv, src=src, c0=c0, n_=n_: e.dma_start(out=sv, in_=src[:, :, c0:c0 + n_]),
                          writes=[stg.r], sbuf=stg.r)
                    if ce == ACT:
                        P.op(ACT, lambda e, sv=sv, wv_=wv_: e.copy(out=wv_, in_=sv), reads=[stg.r], writes=[wb.r])
                    elif ce == DVE:
                        P.op(DVE, lambda e, sv=sv, wv_=wv_: e.tensor_copy(out=wv_, in_=sv), reads=[stg.r], writes=[wb.r])
                    else:
                        P.op(POOL, lambda e, sv=sv, wv_=wv_: e.tensor_copy(out=wv_, in_=sv), reads=[stg.r], writes=[wb.r])
                    P.dma(SP, lambda e, wv_=wv_, dstv=dstv, c0=c0, n_=n_: e.dma_start(out=dstv[:, :, c0:c0 + n_], in_=wv_),
                          reads=[wb.r], sbuf=wb.r, store=True)
            P.barrier()
            P.dma(SP, lambda e: e.dma_start(out=wdtb[:], in_=w_in_bf[:, :, 6144:6208]), writes=[wdtb.r], sbuf=wdtb.r)

            def fe_io(pi, store):
                items = ((xs_tok, "p a b -> p (a b)"), (B_tok, "p a b -> p (a b)"), (BT, "p a b -> p (a b)"),
                         (CT, "p a b -> p (a b)"), (dt, "p a b -> p (a b)"), (dA, "p a b -> p (a b)"))
                for k_, (buf, pat) in enumerate(items):
                    view = buf[:].rearrange(pat)
                    dram = self.fe_d[k_][pi]
                    rr = self.fe_r[k_][pi]
                    if store:
                        P.dma(SP, lambda e, view=view, dram=dram: e.dma_start(out=dram, in_=view), reads=[buf.r], writes=[rr],
                              sbuf=buf.r, store=True)
                    else:
                        P.dma(SP, lambda e, view=view, dram=dram: e.dma_start(out=view, in_=dram), reads=[rr], writes=[buf.r],
                              sbuf=buf.r)

            def frontend(tok0, cols, ci, g, only_norm=False):
                self.norm_slice(tok0, W, g, ci, lambda c, ci: self.m_gs(1, 0, c, ci), lambda c, ci: self.m_sh(1, 0, c, ci),
                                hnb, rstd, sq, tmp)
                if only_norm:
                    return
                for ch in range(2):
                    pd = self.ps()
                    for k in range(KC):
                        P.op(PE, lambda e, k=k, ch=ch, pd=pd: e.matmul(pd[:, 0:64], lhsT=hnb[:, k, ch * 128:(ch + 1) * 128],
                                                                       rhs=wdtb[:, k, :], start=(k == 0), stop=(k == KC - 1)),
                             reads=[hnb.r, wdtb.r], writes=[pd.r], inc=(k == KC - 1))
                    P.op(DVE, lambda e, pd=pd: e.tensor_tensor(out=xx[:], in0=pd[:, 0:64], in1=rowp[:, RP_DTB:RP_DTB + 64],
                                                               op=ALU.add), reads=[pd.r, rowp.r], writes=[xx.r])
                    P.op(ACT, lambda e: e.activation(out=ax[:], in_=xx[:], func=AF.Abs), reads=[xx.r], writes=[ax.r])
                    P.op(ACT, lambda e: e.activation(out=ax[:], in_=ax[:], func=AF.Exp, scale=-1.0), reads=[ax.r], writes=[ax.r])
                    P.op(ACT, lambda e: e.activation(out=ax[:], in_=ax[:], func=AF.Ln, bias=onec[:], scale=1.0),
                         reads=[ax.r, onec.r], writes=[ax.r])
                    P.op(DVE, lambda e, ch=ch: e.scalar_tensor_tensor(out=dt[:, ch, :], in0=xx[:], scalar=0.0, in1=ax[:],
                                                                      op0=ALU.max, op1=ALU.add),
                         reads=[xx.r, ax.r], writes=[dt.r])
                    P.op(DVE, lambda e, ch=ch: e.tensor_tensor(out=dA[:, ch, :], in0=dt[:, ch, :], in1=arow[:], op=ALU.mult),
                         reads=[dt.r, arow.r], writes=[dA.r])
                nblk = 0
                for s in range(16):
                    wbr, wb = load_w(w_in_bf, 2048 + s * 256)
                    for cb in range(2):
                        blk = s * 2 + cb
                        y, sl_ = yb[nblk % 3], silb[nblk % 3]
                        nblk += 1
                        pt = self.ps()
                        for k in range(KC):
                            P.op(PE, lambda e, wb=wb, cb=cb, k=k, pt=pt: e.matmul(
                                pt[:, 0:W], lhsT=wb[:, k, cb * 128:(cb + 1) * 128], rhs=hnb[:, k, :],
                                start=(k == 0), stop=(k == KC - 1)), reads=[wbr.r, hnb.r], writes=[pt.r], inc=(k == KC - 1))
                        u = pt
                        wcol = lambda tap, blk=blk: self.spk[:, SP_SSW + tap * 32 + blk:SP_SSW + tap * 32 + blk + 1]
                        bcol = self.spk[:, SP_SSB + blk:SP_SSB + blk + 1]
                        u3 = pt[:, 0:W].rearrange("p (r w) -> p r w", w=cols)
                        y3 = y[:].rearrange("p (r w) -> p r w", w=cols)
                        P.op(DVE, lambda e, u=u, y=y, wcol=wcol: e.tensor_scalar(out=y[:], in0=u[:, 0:W], scalar1=wcol(1), scalar2=None,
                                                                                op0=ALU.mult), reads=[u.r, self.spk.r], writes=[y.r])
                        P.op(DVE, lambda e, u3=u3, y3=y3, wcol=wcol: e.scalar_tensor_tensor(
                            out=y3[:, :, 1:], in0=u3[:, :, :cols - 1], scalar=wcol(0), in1=y3[:, :, 1:], op0=ALU.mult, op1=ALU.add),
                            reads=[u.r, y.r, self.spk.r], writes=[y.r])
                        P.op(DVE, lambda e, u3=u3, y3=y3, wcol=wcol: e.scalar_tensor_tensor(
                            out=y3[:, :, :cols - 1], in0=u3[:, :, 1:], scalar=wcol(2), in1=y3[:, :, :cols - 1], op0=ALU.mult, op1=ALU.add),
                            reads=[u.r, y.r, self.spk.r], writes=[y.r])
                        if blk >= 24:
                            P.op(ACT, lambda e, y=y, blk=blk, bcol=bcol: e.activation(out=CT[:, blk - 24, :], in_=y[:], func=AF.Silu,
                                                                                     bias=bcol, scale=1.0),
                                 reads=[y.r, self.spk.r], writes=[CT.r])
                            continue
                        P.op(ACT, lambda e, y=y, sl_=sl_, bcol=bcol: e.activation(out=sl_[:], in_=y[:], func=AF.Silu, bias=bcol, scale=1.0),
                             reads=[y.r, self.spk.r], writes=[sl_.r])
                        p2 = self.ps()
                        for ch in range(2):
                            P.op(PE, lambda e, sl_=sl_, ch=ch, p2=p2: e.transpose(out=p2[:, ch * 128:(ch + 1) * 128],
                                                                                 in_=sl_[:, ch * 128:(ch + 1) * 128], identity=self.ident[:]),
                                 reads=[sl_.r, self.ident.r], writes=[p2.r], inc=(ch == 1))
                        p23 = p2[:, 0:256].rearrange("p (c n) -> p c n", n=128)
                        if blk < 16:
                            P.op(ACT, lambda e, blk=blk, p23=p23: e.copy(out=xs_tok[:, :, blk * 128:(blk + 1) * 128], in_=p23),
                                 reads=[p2.r], writes=[xs_tok.r])
                        else:
                            gb_ = blk - 16
                            P.op(ACT, lambda e, gb_=gb_, p23=p23: e.copy(out=B_tok[:, :, gb_ * 128:(gb_ + 1) * 128], in_=p23),
                                 reads=[p2.r], writes=[B_tok.r])
                            P.op(DVE, lambda e, gb_=gb_, sl_=sl_: e.tensor_copy(out=BT[:, gb_, :], in_=sl_[:]),
                                 reads=[sl_.r], writes=[BT.r])

            def zproj_pair():
                for s in range(8):
                    wbr, wb = load_w(w_in_bf, s * 256)
                    for ch in range(2):
                        pz = self.ps()
                        zz = zsL[ch]
                        for k in range(KC):
                            P.op(PE, lambda e, wb=wb, k=k, pz=pz, ch=ch: e.matmul(pz[:, 0:256], lhsT=hnb[:, k, ch * 128:(ch + 1) * 128],
                                                                                rhs=wb[:, k, :], start=(k == 0), stop=(k == KC - 1)),
                                 reads=[wbr.r, hnb.r], writes=[pz.r], inc=(k == KC - 1))
                        P.op(ACT, lambda e, s=s, pz=pz, zz=zz: e.activation(out=zz[:, s * 256:(s + 1) * 256], in_=pz[:, 0:256], func=AF.Silu),
                             reads=[pz.r], writes=[zz.r])

            def dir_common(ch, d):
                pc = self.ps()
                P.op(PE, lambda e, pc=pc: e.matmul(pc[:, 0:32], lhsT=tri[d][:], rhs=dA[:, ch, d * 32:(d + 1) * 32], start=True, stop=True),
                     reads=[tri[d].r, dA.r], writes=[pc.r], inc=False)
                P.op(PE, lambda e, pc=pc: e.matmul(pc[:, 32:64], lhsT=self.ones[:], rhs=dA[:, ch, d * 32:(d + 1) * 32], start=True, stop=True),
                     reads=[self.ones.r, dA.r], writes=[pc.r], inc=True)
                P.op(ACT, lambda e, pc=pc: e.copy(out=cumcL[d][:], in_=pc[:, 0:32]), reads=[pc.r], writes=[cumcL[d].r])
                P.op(ACT, lambda e, pc=pc: e.copy(out=totL[d][:], in_=pc[:, 32:64]), reads=[pc.r], writes=[totL[d].r])
                P.op(DVE, lambda e: e.tensor_tensor(
                    out=xdt[d][:].rearrange("p (h q) -> p h q", q=64), in0=xs_tok[:, ch, :].rearrange("p (h q) -> p h q", q=64),
                    in1=dt[:, ch, d * 32:(d + 1) * 32].unsqueeze(2).to_broadcast([128, 32, 64]), op=ALU.mult),
                    reads=[xs_tok.r, dt.r], writes=[xdt[d].r])

            def cumbc_group(ch, d, g):
                Rg = RgL[par[0] % 2]
                P.op(POOL, lambda e: e.tensor_tensor(
                    out=Rg[:].rearrange("p (h i) -> p h i", i=128), in0=tri[d][:].unsqueeze(1).to_broadcast([128, 4, 128]),
                    in1=dA[:, ch, d * 32 + g * 4:d * 32 + g * 4 + 4].unsqueeze(2).to_broadcast([128, 4, 128]), op=ALU.mult),
                    reads=[tri[d].r, dA.r], writes=[Rg.r])
                pb = self.ps()
                P.op(PE, lambda e, pb=pb: e.matmul(pb[:], lhsT=self.ones[:], rhs=Rg[:], start=True, stop=True),
                     reads=[self.ones.r, Rg.r], writes=[pb.r])
                return pb

            def diag_group(ch, d, g, pb):
                sg = segL[(par[0] + d) % 2]
                CBt = CBtL[par[0] % 2]
                Gm = [GmL[0][par[0] % 2], GmL[1][par[0] % 2]]
                P.op(DVE, lambda e, pb=pb: e.tensor_tensor(
                    out=sg[:].rearrange("p (h i) -> p h i", i=128), in0=pb[:].rearrange("p (h i) -> p h i", i=128),
                    in1=cumcL[d][:, g * 4:g * 4 + 4].unsqueeze(2).to_broadcast([128, 4, 128]), op=ALU.subtract),
                    reads=[pb.r, cumcL[d].r], writes=[sg.r])
                P.op(DVE, lambda e: e.tensor_tensor(
                    out=sg[:].rearrange("p (h i) -> p h i", i=128), in0=sg[:].rearrange("p (h i) -> p h i", i=128),
                    in1=mneg[d][:].unsqueeze(1).to_broadcast([128, 4, 128]), op=ALU.add),
                    reads=[sg.r, mneg[d].r], writes=[sg.r])
                P.op(ACT, lambda e: e.activation(out=sg[:], in_=sg[:], func=AF.Exp), reads=[sg.r], writes=[sg.r])
                P.op(DVE, lambda e: e.tensor_tensor(
                    out=Gm[d][:].rearrange("p (h i) -> p h i", i=128), in0=sg[:].rearrange("p (h i) -> p h i", i=128),
                    in1=CBt[:].unsqueeze(1).to_broadcast([128, 4, 128]), op=ALU.mult),
                    reads=[sg.r, CBt.r], writes=[Gm[d].r])

            def off_group(ch, d, g, pb):
                Ee = EeL[par[0] % 2]
                Cs = CsL[par[0] % 2]
                P.op(ACT, lambda e, pb=pb: e.activation(out=Ee[:], in_=pb[:], func=AF.Exp), reads=[pb.r], writes=[Ee.r])
                P.op(DVE, lambda e: e.tensor_tensor(
                    out=Cs[:].rearrange("p (h i) -> p h i", i=128), in0=Ee[:].rearrange("p (h i) -> p h i", i=128),
                    in1=CT[:, g, ch * 128:(ch + 1) * 128].unsqueeze(1).to_broadcast([128, 4, 128]), op=ALU.mult),
                    reads=[Ee.r, CT.r], writes=[Cs.r])

            def state_update(ch, d, accumulate_prefix=False):
                P.op(DVE, lambda e: e.tensor_tensor(out=wv[:], in0=totL[d][:], in1=cumcL[d][:], op=ALU.subtract),
                     reads=[totL[d].r, cumcL[d].r], writes=[wv.r])
                P.op(ACT, lambda e: e.activation(out=wv[:], in_=wv[:], func=AF.Exp), reads=[wv.r], writes=[wv.r])
                P.op(ACT, lambda e: e.activation(out=dec[:], in_=totL[d][:], func=AF.Exp), reads=[totL[d].r], writes=[dec.r])
                P.op(DVE, lambda e: e.tensor_tensor(
                    out=xw[:].rearrange("p (h q) -> p h q", q=64), in0=xdt[d][:].rearrange("p (h q) -> p h q", q=64),
                    in1=wv[:].unsqueeze(2).to_broadcast([128, 32, 64]), op=ALU.mult), reads=[xdt[d].r, wv.r], writes=[xw.r])
                for g in range(8):
                    pS = self.ps()
                    P.op(PE, lambda e, g=g, pS=pS: e.matmul(pS[:, 0:256], lhsT=B_tok[:, ch, g * 128:(g + 1) * 128],
                                                           rhs=xw[:, g * 256:(g + 1) * 256], start=True, stop=True),
                         reads=[B_tok.r, xw.r], writes=[pS.r])
                    hv = hT[d][:, g * 256:(g + 1) * 256].rearrange("p (h q) -> p h q", q=64)
                    if accumulate_prefix:
                        P.op(DVE, lambda e, g=g, pS=pS: e.tensor_tensor(
                            out=t1[:].rearrange("p (h q) -> p h q", q=64), in0=pS[:, 0:256].rearrange("p (h q) -> p h q", q=64),
                            in1=Pb[:, g * 4:g * 4 + 4].unsqueeze(2).to_broadcast([128, 4, 64]), op=ALU.mult),
                            reads=[pS.r, Pb.r], writes=[t1.r])
                        P.op(DVE, lambda e, g=g: e.tensor_tensor(out=hT[d][:, g * 256:(g + 1) * 256], in0=hT[d][:, g * 256:(g + 1) * 256],
                                                                 in1=t1[:], op=ALU.add), reads=[t1.r, hT[d].r], writes=[hT[d].r])
                    else:
                        P.op(DVE, lambda e, hv=hv, g=g: e.tensor_tensor(
                            out=hv, in0=hv, in1=dec[:, g * 4:g * 4 + 4].unsqueeze(2).to_broadcast([128, 4, 64]), op=ALU.mult),
                            reads=[hT[d].r, dec.r], writes=[hT[d].r])
                        P.op(DVE, lambda e, g=g, pS=pS: e.tensor_tensor(out=hT[d][:, g * 256:(g + 1) * 256],
                                                                        in0=hT[d][:, g * 256:(g + 1) * 256], in1=pS[:, 0:256], op=ALU.add),
                             reads=[pS.r, hT[d].r], writes=[hT[d].r])
                if accumulate_prefix:
                    P.op(DVE, lambda e: e.tensor_tensor(out=Pb[:], in0=Pb[:], in1=dec[:], op=ALU.mult), reads=[Pb.r, dec.r], writes=[Pb.r])
                else:
                    P.op(ACT, lambda e: e.copy(out=hTb[d][:], in_=hT[d][:]), reads=[hT[d].r], writes=[hTb[d].r])

            def scan_chunk(ch, mode, ychunk, tok0, ci):
                if mode == "S":
                    for d in range(2):
                        dir_common(ch, d)
                        state_update(ch, d, accumulate_prefix=(d == 1))
                    return
                dirs = (0, 1) if mode == "F" else (1,)
                od = 0 if mode == "F" else 1
                for d in dirs:
                    dir_common(ch, d)
                    if d == od:
                        pass
                if mode == "B":
                    P.dma(SP, lambda e: e.dma_start(out=ysb[:], in_=self.ypart_d[ychunk]), reads=[self.ypart_r[ychunk]],
                          writes=[ysb.r], sbuf=ysb.r)
                for g in range(8):
                    par[0] += 1
                    CBt = CBtL[par[0] % 2]
                    Gm = [GmL[0][par[0] % 2], GmL[1][par[0] % 2]]
                    Cs = CsL[par[0] % 2]
                    py = self.ps()
                    seq = []
                    if mode == "F":
                        pcb = self.ps()
                        P.op(PE, lambda e, g=g, pcb=pcb: e.matmul(pcb[:, 0:128], lhsT=BT[:, g, ch * 128:(ch + 1) * 128],
                                                                  rhs=CT[:, g, ch * 128:(ch + 1) * 128], start=True, stop=True),
                             reads=[BT.r, CT.r], writes=[pcb.r])
                        P.op(ACT, lambda e, pcb=pcb: e.copy(out=CBt[:], in_=pcb[:, 0:128]), reads=[pcb.r], writes=[CBt.r])
                    for d in dirs:
                        pb = cumbc_group(ch, d, g)
                        if mode == "F":
                            diag_group(ch, d, g, pb)
                            seq.append((Gm[d], xdt[d]))
                        if d == od:
                            off_group(ch, d, g, pb)
                            seq.append((Cs, None))
                    n = len(seq)
                    for hh in range(4):
                        h = g * 4 + hh
                        for qi, (lt, rx) in enumerate(seq):
                            if rx is None:
                                P.op(PE, lambda e, lt=lt, hh=hh, h=h, py=py, qi=qi: e.matmul(
                                    py[:, hh * 64:(hh + 1) * 64], lhsT=lt[:, hh * 128:(hh + 1) * 128],
                                    rhs=hTb[od][:, h * 64:(h + 1) * 64], start=(qi == 0), stop=(qi == n - 1)),
                                    reads=[lt.r, hTb[od].r], writes=[py.r], inc=(qi == n - 1 and hh == 3))
                            else:
                                P.op(PE, lambda e, lt=lt, rx=rx, hh=hh, h=h, py=py, qi=qi: e.matmul(
                                    py[:, hh * 64:(hh + 1) * 64], lhsT=lt[:, hh * 128:(hh + 1) * 128],
                                    rhs=rx[:, h * 64:(h + 1) * 64], start=(qi == 0), stop=(qi == n - 1)),
                                    reads=[lt.r, rx.r], writes=[py.r], inc=(qi == n - 1 and hh == 3))
                    if mode == "F":
                        P.op(DVE, lambda e, g=g: e.tensor_tensor(
                            out=t1[:].rearrange("p (h q) -> p h q", q=64),
                            in0=xs_tok[:, ch, g * 256:(g + 1) * 256].rearrange("p (h q) -> p h q", q=64),
                            in1=drow[:, g * 4:g * 4 + 4].unsqueeze(2).to_broadcast([128, 4, 64]), op=ALU.mult),
                            reads=[xs_tok.r, drow.r], writes=[t1.r])
                        P.op(DVE, lambda e, g=g, py=py: e.tensor_tensor(out=ysb[:, g * 256:(g + 1) * 256], in0=t1[:], in1=py[:, 0:256],
                                                                        op=ALU.add), reads=[t1.r, py.r], writes=[ysb.r])
                    else:
                        P.op(DVE, lambda e, g=g, py=py: e.tensor_tensor(out=ysb[:, g * 256:(g + 1) * 256], in0=ysb[:, g * 256:(g + 1) * 256],
                                                                        in1=py[:, 0:256], op=ALU.add), reads=[ysb.r, py.r], writes=[ysb.r])
                state_update(ch, od)
                if mode == "F":
                    P.dma(SP, lambda e: e.dma_start(out=self.ypart_d[ychunk], in_=ysb[:]), reads=[ysb.r],
                          writes=[self.ypart_r[ychunk]], sbuf=ysb.r, store=True)
                else:
                    zz = zsL[ch]
                    yT_ = ynTL[ch]
                    P.op(DVE, lambda e: e.tensor_tensor(out=ysb[:], in0=ysb[:], in1=zz[:, 0:2048], op=ALU.mult), reads=[ysb.r, zz.r], writes=[ysb.r])
                    P.op(DVE, lambda e: e.scalar_tensor_tensor(out=xdt[0][:], in0=ysb[:], scalar=1.0, in1=ysb[:], op0=ALU.mult, op1=ALU.mult,
                                                               accum_out=ssq[:]), reads=[ysb.r], writes=[xdt[0].r, ssq.r])
                    P.op(ACT, lambda e: e.activation(out=ssq[:], in_=ssq[:], func=AF.Sqrt, bias=self.epsb[:], scale=1.0 / 2048.0),
                         reads=[ssq.r, self.epsb.r], writes=[ssq.r])
                    P.op(DVE, lambda e: e.reciprocal(out=ssq[:], in_=ssq[:]), reads=[ssq.r], writes=[ssq.r])
                    P.op(DVE, lambda e: e.tensor_scalar(out=ysb[:], in0=ysb[:], scalar1=ssq[:], scalar2=None, op0=ALU.mult),
                         reads=[ysb.r, ssq.r], writes=[ysb.r])
                    for q4 in range(4):
                        pt = self.ps()
                        for j in range(4):
                            k = q4 * 4 + j
                            P.op(PE, lambda e, k=k, j=j, pt=pt: e.transpose(out=pt[:, j * 128:(j + 1) * 128], in_=ysb[:, k * 128:(k + 1) * 128],
                                                                            identity=self.ident[:]),
                                 reads=[ysb.r, self.ident.r], writes=[pt.r], inc=(j == 3))
                        for j in range(4):
                            k = q4 * 4 + j
                            P.op(ACT, lambda e, k=k, j=j, pt=pt: e.activation(
                                out=yT_[:, k, :], in_=pt[:, j * 128:(j + 1) * 128], func=AF.Copy,
                                scale=self.spk[:, SP_SNG + k:SP_SNG + k + 1]), reads=[pt.r, self.spk.r], writes=[yT_.r])

            def out_proj_pair(tok0, g, ci):
                for s in range(8):
                    wbr, wb = load_w(w_out_bf, s * 128, wide=False)
                    dc = s
                    pt = self.ps()
                    for ch in range(2):
                        yT_ = ynTL[ch]
                        for k in range(16):
                            P.op(PE, lambda e, wb=wb, k=k, pt=pt, ch=ch, yT_=yT_: e.matmul(
                                pt[:, ch * 128:(ch + 1) * 128], lhsT=wb[:, k, :], rhs=yT_[:, k, :], start=(k == 0), stop=(k == 15)),
                                reads=[wbr.r, yT_.r], writes=[pt.r], inc=(k == 15))
                    P.op(DVE, lambda e, dc=dc, pt=pt: e.scalar_tensor_tensor(
                        out=self.xT[:, dc, tok0:tok0 + W], in0=pt[:, 0:W], scalar=self.m_gt(1, 0, dc, ci),
                        in1=self.xT[:, dc, tok0:tok0 + W], op0=ALU.mult, op1=ALU.add),
                        reads=[pt.r, self.modT[1].r, self.xr[dc][g]], writes=[self.xr[dc][g]])

            def run_pair(pi, mode):
                tok0 = pi * W
                g = tok0 // GT
                sample = pi < 8
                ci = 0 if sample else 1
                cols = 64 if sample else 256
                if mode == "S" or (mode == "F" and not sample):
                    frontend(tok0, cols, ci, g)
                    fe_io(pi, True)
                else:
                    if mode == "B":
                        frontend(tok0, cols, ci, g, only_norm=True)
                    fe_io(pi, False)
                if mode == "B":
                    zproj_pair()
                order = (0, 1) if mode != "B" else (1, 0)
                for ch in order:
                    scan_chunk(ch, mode, pi * 2 + ch, tok0, ci)
                if mode == "B":
                    out_proj_pair(tok0, g, ci)

            def set_state(d, src_ap=None):
                if src_ap is None:
                    P.op(DVE, lambda e: e.memset(hT[d][:], 0.0), writes=[hT[d].r])
                else:
                    P.dma(SP, lambda e: e.dma_start(out=hT[d][:], in_=src_ap), writes=[hT[d].r], sbuf=hT[d].r)
                P.op(ACT, lambda e: e.copy(out=hTb[d][:], in_=hT[d][:]), reads=[hT[d].r], writes=[hTb[d].r])

            set_state(0, self.stinit_d[0])
            set_state(1, None)
            P.op(DVE, lambda e: e.memset(Pb[:], 1.0), writes=[Pb.r])
            for pi in range(8):
                run_pair(pi, "S")
            P.dma(SP, lambda e: e.dma_start(out=ysb[:], in_=self.stinit_d[1]), writes=[ysb.r], sbuf=ysb.r)
            P.op(DVE, lambda e: e.tensor_tensor(out=ysb[:].rearrange("p (h q) -> p h q", q=64), in0=ysb[:].rearrange("p (h q) -> p h q", q=64),
                                                in1=Pb[:].unsqueeze(2).to_broadcast([128, 32, 64]), op=ALU.mult),
                 reads=[ysb.r, Pb.r], writes=[ysb.r])
            P.op(DVE, lambda e: e.tensor_tensor(out=hT[1][:], in0=hT[1][:], in1=ysb[:], op=ALU.add), reads=[hT[1].r, ysb.r], writes=[hT[1].r])
            for d in range(2):
                P.op(DVE, lambda e, d=d: e.tensor_scalar(out=hT[d][:], in0=hT[d][:], scalar1=flags[:, d:d + 1], scalar2=None, op0=ALU.mult),
                     reads=[hT[d].r, flags.r], writes=[hT[d].r])
                P.dma(SP, lambda e, d=d: e.dma_start(out=self.cc_in[:, d * 2048:(d + 1) * 2048], in_=hT[d][:]),
                      reads=[hT[d].r], writes=[self.cc_in_r], sbuf=hT[d].r, store=True)
            ccsem = self.st.enter_context(nc.semaphore("ccsem"))
            P.barrier()
            self.P.streams[POOL].append({"waits": [], "fn": lambda e: e.collective_compute(
                "AllReduce", ALU.add, replica_groups=[[0, 1], [2, 3], [4, 5], [6, 7]],
                ins=[self.cc_in], outs=[self.cc_out]).then_inc(ccsem, 1), "inc": None})
            for eng in (POOL, SP, DVE, ACT, PE):
                self.P.streams[eng].append({"waits": [], "fn": lambda e: e.wait_ge(ccsem, 1), "inc": None})
            inits = []
            for d in range(2):
                P.dma(SP, lambda e, d=d: e.dma_start(out=ysb[:], in_=self.cc_out[:, d * 2048:(d + 1) * 2048]), writes=[ysb.r], sbuf=ysb.r)
                P.dma(SP, lambda e, d=d: e.dma_start(out=hT[d][:], in_=self.stinit_d[d]), writes=[hT[d].r], sbuf=hT[d].r)
                P.op(DVE, lambda e, d=d: e.scalar_tensor_tensor(out=hT[d][:], in0=ysb[:], scalar=flags[:, 2 + d:3 + d], in1=hT[d][:],
                                                                 op0=ALU.mult, op1=ALU.add), reads=[ysb.r, flags.r, hT[d].r], writes=[hT[d].r])
                P.dma(SP, lambda e, d=d: e.dma_start(out=self.init_true[d], in_=hT[d][:]), reads=[hT[d].r], writes=[self.init_true_r[d]],
                      sbuf=hT[d].r, store=True)
            P.op(ACT, lambda e: e.copy(out=hTb[0][:], in_=hT[0][:]), reads=[hT[0].r], writes=[hTb[0].r])
            for pi in range(8):
                run_pair(pi, "F")
            for q in range(2):
                set_state(0, None)
                run_pair(8 + q, "F")
                P.dma(SP, lambda e, q=q: e.dma_start(out=self.st_out[q, 0], in_=hT[0][:]), reads=[hT[0].r], sbuf=hT[0].r, store=True)
            P.barrier()
            for q in (1, 0):
                set_state(1, None)
                run_pair(8 + q, "B")
                P.dma(SP, lambda e, q=q: e.dma_start(out=self.st_out[q, 1], in_=hT[1][:]), reads=[hT[1].r], sbuf=hT[1].r, store=True)
            P.dma(SP, lambda e: e.dma_start(out=hT[1][:], in_=self.init_true[1]), reads=[self.init_true_r[1]], writes=[hT[1].r], sbuf=hT[1].r)
            P.op(ACT, lambda e: e.copy(out=hTb[1][:], in_=hT[1][:]), reads=[hT[1].r], writes=[hTb[1].r])
            for pi in range(7, -1, -1):
                run_pair(pi, "B")
            P.barrier()

    def _cum_for(self, ch, d, tri, dA, cumc, tot):
        P = self.P
        pc = self.ps()
        P.op(PE, lambda e, pc=pc: e.matmul(pc[:, 0:32], lhsT=tri[d][:], rhs=dA[:, ch, d * 32:(d + 1) * 32], start=True, stop=True),
             reads=[tri[d].r, dA.r], writes=[pc.r], inc=False)
        P.op(PE, lambda e, pc=pc: e.matmul(pc[:, 32:64], lhsT=self.ones[:], rhs=dA[:, ch, d * 32:(d + 1) * 32], start=True, stop=True),
             reads=[self.ones.r, dA.r], writes=[pc.r], inc=True)
        P.op(ACT, lambda e, pc=pc: e.copy(out=cumc[:], in_=pc[:, 0:32]), reads=[pc.r], writes=[cumc.r])
        P.op(ACT, lambda e, pc=pc: e.copy(out=tot[:], in_=pc[:, 32:64]), reads=[pc.r], writes=[tot.r])

    def phase_final(self):
        nc, P = self.nc, self.P
        with contextlib.ExitStack() as ph:
            sb = self.sb
            yo = [sb(ph, "yo%d" % i, [128, KC, GT]) for i in range(2)]
            rstd = sb(ph, "rstd", [128, GT])
            sq = sb(ph, "sq", [128, 2, GT])
            tmp = sb(ph, "tmp", [128, 2, GT])
            yv = self.yT_d.rearrange("(c p) t -> p c t", p=128)
            for g in range(NG):
                y = yo[g % 2]
                self.norm_group(g, lambda c, ci: self.spk[:, SP_NFF + c:SP_NFF + c + 1], None, [y], rstd, sq, tmp)
                P.dma(SP, lambda e, y=y, g=g: e.dma_start(out=yv[:, :, g * GT:(g + 1) * GT], in_=y[:]),
                      reads=[y.r], sbuf=y.r, store=True)


def _pc(v):
    v = np.asarray(v, np.float32)
    return np.ascontiguousarray(v.reshape(-1, 128).T)


def make_in_maps(inp):
    f = lambda a: np.ascontiguousarray(np.asarray(a, np.float32))
    spk = np.zeros((128, NSP), np.float32)
    for i in range(2):
        spk[:, SP_NMG + 8 * i:SP_NMG + 8 * (i + 1)] = _pc(inp["norm_mix_g"][i])
        spk[:, SP_NFG + 8 * i:SP_NFG + 8 * (i + 1)] = _pc(inp["norm_ffn_g"][i])
        spk[:, SP_ADAB + 48 * i:SP_ADAB + 48 * (i + 1)] = _pc(inp["ada_b"][i])
    spk[:, SP_NFF:SP_NFF + 8] = _pc(inp["norm_f_g"])
    for tap in range(3):
        spk[:, SP_SCW + 8 * tap:SP_SCW + 8 * (tap + 1)] = _pc(inp["sc_conv_w"][0][tap])
        spk[:, SP_SSW + 32 * tap:SP_SSW + 32 * (tap + 1)] = _pc(inp["ssd_conv_w"][0][tap])
    spk[:, SP_SSB:SP_SSB + 32] = _pc(inp["ssd_conv_b"][0])
    spk[:, SP_SNG:SP_SNG + 16] = _pc(inp["ssd_norm_g"][0])
    rowp = np.concatenate([np.asarray(inp["ssd_dt_bias"][0], np.float32).reshape(-1),
                           np.asarray(inp["ssd_a_log"][0], np.float32).reshape(-1),
                           np.asarray(inp["ssd_d"][0], np.float32).reshape(-1)])
    ii = np.arange(128)
    trif = (ii[:, None] <= ii[None, :]).astype(np.float32)
    trib = (ii[:, None] >= ii[None, :]).astype(np.float32)
    shared = {
        "rowp": np.ascontiguousarray(np.broadcast_to(rowp, (128, ROWP_N))),
        "trif": trif, "trib": trib,
        "mnegf": (trif - 1.0) * 30000.0, "mnegb": (trib - 1.0) * 30000.0,
        "ssd_w_in": f(inp["ssd_w_in"]), "ssd_w_out": f(inp["ssd_w_out"]),
        "spk": spk,
        "ident": np.eye(128, dtype=np.float32),
        "ones": np.ones((128, 128), np.float32),
        "iota": np.ascontiguousarray(np.broadcast_to(np.arange(256, dtype=np.float32), (128, 256))),
        "ada_w": f(inp["ada_w"]),
        "sc_w_in": f(inp["sc_w_in"]),
        "sc_w_out": f(inp["sc_w_out"]),
        "peer_wq": f(inp["peer_wq"]),
        "peer_keys": f(inp["peer_keys"]),
        "peer_uv": np.ascontiguousarray(np.concatenate(
            [np.asarray(inp["peer_u"], np.float32), np.asarray(inp["peer_v"], np.float32)], axis=-1)),
    }
    xs = np.asarray(inp["x_sample"], np.float32)
    xp = np.asarray(inp["x_prompt"], np.float32)
    maps = []
    for r in range(8):
        b, hf = r // 2, r % 2
        tok = np.concatenate([xs[b, hf * 2048:(hf + 1) * 2048], xp[2 * r], xp[2 * r + 1]], axis=0)
        condT = np.stack([_pc(inp["c"][b]), _pc(inp["c_ctx"])], axis=2)
        m = dict(shared)
        m["xT"] = np.ascontiguousarray(tok.T)
        m["condT"] = np.ascontiguousarray(condT)
        st = np.asarray(inp["state_ssm"], np.float32)[b, 0]
        stinit = np.zeros((2, 128, 2048), np.float32)
        stinit[hf] = st[hf].reshape(2048, 128).T
        m["stinit"] = stinit
        fl = np.array([1, 0, 0, 1] if hf == 0 else [0, 1, 1, 0], np.float32)
        m["flags"] = np.ascontiguousarray(np.broadcast_to(fl, (128, 4)))
        maps.append(m)
    return maps


_NC_CACHE = {}


def kernel(**inputs):
    key = "full"
    if key not in _NC_CACHE:
        _NC_CACHE[key] = Builder().build()
    nc = _NC_CACHE[key]
    maps = make_in_maps(inputs)
    res = run_bass_kernel_spmd(nc, maps, core_ids=list(range(8)))
    y_prompt = np.zeros((16, 256, D), np.float32)
    y_sample = np.zeros((4, 4096, D), np.float32)
    states = np.zeros((16, 1, 2, 32, 64, 128), np.float32)
    for r in range(8):
        yT = np.asarray(res.results[r]["yT"])
        y = yT.T
        b, hf = r // 2, r % 2
        y_sample[b, hf * 2048:(hf + 1) * 2048] = y[0:2048]
        y_prompt[2 * r] = y[2048:2304]
        y_prompt[2 * r + 1] = y[2304:2560]
        if "st_out" in res.results[r]:
            so = np.asarray(res.results[r]["st_out"])
            for q in range(2):
                for d in range(2):
                    states[2 * r + q, 0, d] = so[q, d].T.reshape(32, 64, 128)
    return (y_prompt, y_sample, states)
```

```python
import contextlib
import os
import types

import numpy as np
import concourse.bass as bass
import concourse.mybir as mybir
from concourse.bass_utils import run_bass_kernel_spmd

F32 = mybir.dt.float32
BF16 = mybir.dt.bfloat16
I32 = mybir.dt.int32
U32 = mybir.dt.uint32
ALU = mybir.AluOpType
AF = mybir.ActivationFunctionType

PE, ACT, DVE, POOL, SP = "pe", "act", "dve", "pool", "sp"
EPOCH = 1000000

D = 1024
KC = 8
T = 2560
NG = 5
GT = 512
EPS = 1e-6
NEG = -1.0e30


def _freeze(fn):
    if fn is None or fn.__closure__ is None:
        return fn
    cells = []
    for c in fn.__closure__:
        try:
            cells.append(types.CellType(c.cell_contents))
        except ValueError:
            cells.append(c)
    return types.FunctionType(fn.__code__, fn.__globals__, fn.__name__, fn.__defaults__, tuple(cells))


class Res:
    __slots__ = ("name", "semname", "w", "w_eng", "rd", "wsem", "wcnt", "rsem", "rcnt")

    def __init__(self, name, semname=None):
        self.name = name
        self.semname = semname or name
        self.w = None
        self.w_eng = None
        self.rd = {}
        self.wsem = None
        self.wcnt = 0
        self.rsem = None
        self.rcnt = 0


class Prog:
    def __init__(self, nc, stack):
        self.nc = nc
        self.stack = stack
        self.streams = {e: [] for e in (PE, ACT, DVE, POOL, SP)}
        self.cnt = {e: 0 for e in self.streams}
        self.pending = {e: [] for e in self.streams}
        self.known = {e: {} for e in self.streams}
        self.sems = {}
        self.nsem = 0
        self.final_tokens = {}
        self.all_dma_tokens = {}
        self.dma_cnt = {}

    def _sem(self, key):
        if key not in self.sems:
            self.sems[key] = self.stack.enter_context(self.nc.semaphore("s%d" % self.nsem))
            self.nsem += 1
        return self.sems[key]

    def _eng_token(self, eng, count):
        ep = (count - 1) // EPOCH
        return (("E", eng, ep), count - ep * EPOCH)

    def _deps(self, eng, reads, writes, dma_wsem_key=None):
        need = {}

        def add(tok):
            if tok is None:
                return
            k, v = tok
            if need.get(k, 0) < v:
                need[k] = v

        for r in reads:
            if r.w is not None:
                add(r.w)
        for w in writes:
            if w.w is not None:
                if dma_wsem_key is not None and w.w[0] == dma_wsem_key:
                    pass
                elif w.w_eng == PE and eng == PE:
                    pass
                else:
                    add(w.w)
            for k, (v, e) in w.rd.items():
                if e == eng and e in (PE, ACT, DVE):
                    continue
                add((k, v))
        waits = []
        kn = self.known[eng]
        for k, v in need.items():
            if kn.get(k, 0) < v:
                kn[k] = v
                waits.append((k, v))
        return waits

    def op(self, eng, fn, reads=(), writes=(), inc=True):
        waits = self._deps(eng, reads, writes)
        item = {"waits": waits, "fn": _freeze(fn), "inc": None}
        self.streams[eng].append(item)
        if inc:
            self.cnt[eng] += 1
            tok = self._eng_token(eng, self.cnt[eng])
            item["inc"] = (tok[0], 1)
            self._sem(tok[0])
            for (rs, ws) in self.pending[eng] + [(reads, writes)]:
                for r in rs:
                    r.rd[tok[0]] = (tok[1], eng)
                for w in ws:
                    w.w = tok
                    w.w_eng = eng
                    w.rd = {}
            self.pending[eng] = []
        else:
            self.pending[eng].append((reads, writes))
        return item

    def dma(self, eng, fn, reads=(), writes=(), sbuf=None, store=False):
        assert not self.pending[eng]
        key = ("R" if store else "W", sbuf.semname)
        if store:
            sbuf.rsem = key
        else:
            sbuf.wsem = key
        self.dma_cnt[key] = self.dma_cnt.get(key, 0) + 1
        val = 16 * self.dma_cnt[key]
        waits = self._deps(eng, reads, writes, dma_wsem_key=None if store else key)
        self._sem(key)
        item = {"waits": waits, "fn": _freeze(fn), "inc": (key, 16)}
        self.streams[eng].append(item)
        tok = (key, val)
        for r in reads:
            r.rd[key] = (val, "dma")
        for w in writes:
            w.w = tok
            w.w_eng = "dma"
            w.rd = {}
        self.all_dma_tokens[key] = val
        if store:
            self.final_tokens[key] = val
        return item

    def barrier(self):
        toks = dict(self.all_dma_tokens)
        for e in self.streams:
            assert not self.pending[e]
            if self.cnt[e] > 0:
                k, v = self._eng_token(e, self.cnt[e])
                toks[k] = v
        for e in self.streams:
            waits = []
            kn = self.known[e]
            for k, v in toks.items():
                if k == ("E", e, (self.cnt[e] - 1) // EPOCH) and e != POOL:
                    pass
                if kn.get(k, 0) < v:
                    kn[k] = v
                    waits.append((k, v))
            self.streams[e].append({"waits": waits, "fn": None, "inc": None})

    def emit(self):
        nc = self.nc
        for e in self.streams:
            assert not self.pending[e], e
        fin = [(k, v) for k, v in self.final_tokens.items()]
        self.streams[SP].append({"waits": fin, "fn": None, "inc": None})
        sems = self.sems
        streams = self.streams

        def run(engobj, items):
            for it in items:
                for (k, v) in it["waits"]:
                    engobj.wait_ge(sems[k], v)
                if it["fn"] is None:
                    continue
                try:
                    ins = it["fn"](engobj)
                except Exception:
                    print("EMIT FAIL at item", items.index(it), "of", len(items), it["waits"], it["inc"])
                    raise
                if it["inc"] is not None:
                    ins.then_inc(sems[it["inc"][0]], it["inc"][1])

        with nc.Block() as block:
            @block.tensor
            def _(e):
                run(e, streams[PE])

            @block.scalar
            def _(e):
                run(e, streams[ACT])

            @block.vector
            def _(e):
                run(e, streams[DVE])

            @block.gpsimd
            def _(e):
                run(e, streams[POOL])

            @block.sync
            def _(e):
                run(e, streams[SP])


class Buf:
    def __init__(self, t, name, base=None):
        self.t = t
        self.r = Res(name, base)

    def __getitem__(self, k):
        return self.t[k]


SP_NMG = 0
SP_NFG = 16
SP_NFF = 32
SP_ADAB = 40
SP_SCW = 136
SP_SSW = 160
SP_SSB = 256
SP_SNG = 288
NSP = 304
RP_DTB = 0
RP_ALOG = 64
RP_D = 128
ROWP_N = 192


class Builder:
    def __init__(self, do_ssd=True, do_peer=True, n_slots=128):
        self.do_ssd = do_ssd
        self.do_peer = do_peer
        self.n_slots = n_slots
        self.nc = bass.Bass("TRN2", target_bir_lowering=False)
        self.uid = 0

    def dram_in(self, name, shape, dt=F32):
        return self.nc.dram_tensor(name, list(shape), dt, kind="ExternalInput").ap()

    def dram_out(self, name, shape, dt=F32):
        return self.nc.dram_tensor(name, list(shape), dt, kind="ExternalOutput").ap()

    def sb(self, stack, name, shape, dt=F32):
        self.uid += 1
        nm = "%s_%d" % (name, self.uid)
        return Buf(stack.enter_context(self.nc.sbuf_tensor(nm, list(shape), dt)), nm, name)

    def ps(self):
        b = self.psb[self.psi % len(self.psb)]
        self.psi += 1
        return b

    def build(self):
        nc = self.nc
        di = self.dram_in
        self.xT_d = di("xT", [D, T])
        self.condT_d = di("condT", [128, KC, 2])
        self.spk_d = di("spk", [128, NSP])
        self.ident_d = di("ident", [128, 128])
        self.ones_d = di("ones", [128, 128])
        self.iota_d = di("iota", [128, 256])
        self.ada_w = di("ada_w", [2, D, 6 * D])
        self.sc_w_in = di("sc_w_in", [1, D, 3 * D])
        self.sc_w_out = di("sc_w_out", [1, D, D])
        self.peer_wq = di("peer_wq", [2, D, 2048])
        self.peer_keys = di("peer_keys", [2, 8, 2, 128, 128])
        self.peer_uv = di("peer_uv", [2, 16384, 2 * D])
        self.ssd_w_in = di("ssd_w_in", [1, D, 6208])
        self.ssd_w_out = di("ssd_w_out", [1, 2048, D])
        self.rowp_d = di("rowp", [128, ROWP_N])
        self.tri_d = [di("trif", [128, 128]), di("trib", [128, 128])]
        self.mneg_d = [di("mnegf", [128, 128]), di("mnegb", [128, 128])]
        self.flags_d = di("flags", [128, 4])
        self.stinit_d = di("stinit", [2, 128, 2048])
        self.st_out = self.dram_out("st_out", [2, 2, 128, 2048])
        self.ypart_d = nc.dram_tensor("ypart", [20, 128, 2048], F32).ap()
        self.ypart_r = [Res("ypart%d" % i) for i in range(20)]
        self.w_in_bf = nc.dram_tensor("w_in_bf", [D, 6208], BF16).ap()
        self.w_out_bf = nc.dram_tensor("w_out_bf", [2048, D], BF16).ap()
        fe_shapes = ((4096, BF16), (2048, BF16), (2048, BF16), (2048, BF16), (128, F32), (128, F32))
        self.fe_d = [nc.dram_tensor("fe%d" % k_, [10, 128, n_], dt_).ap() for k_, (n_, dt_) in enumerate(fe_shapes)]
        self.fe_r = [[Res("fe%d_%d" % (k_, p_)) for p_ in range(10)] for k_ in range(6)]
        self.cc_in = nc.dram_tensor("cc_in", [128, 4096], F32).ap()
        self.cc_out = nc.dram_tensor("cc_out", [128, 4096], F32).ap()
        self.cc_in_r = Res("cc_in")
        self.init_true = [nc.dram_tensor("init_true%d" % d, [128, 2048], F32).ap() for d in range(2)]
        self.init_true_r = [Res("init_true%d" % d) for d in range(2)]
        self.peer_uv_flat = self.peer_uv.rearrange("l n d -> (l n) d")
        self.uv_bf = nc.dram_tensor("uv_bf", [2 * 16384, 2 * D], BF16).ap()
        self.yT_d = self.dram_out("yT", [D, T])

        with contextlib.ExitStack() as st:
            self.st = st
            P = self.P = Prog(nc, st)
            self.psb = [Buf(st.enter_context(nc.psum_tensor("ps%d" % i, [128, 512], F32)), "ps%d" % i)
                        for i in range(8)]
            self.psi = 0
            sb = self.sb
            self.xT = sb(st, "xT", [128, KC, T])
            self.xr = [[Res("x_%d_%d" % (c, g), "xload") for g in range(NG)] for c in range(KC)]
            self.ident = sb(st, "ident", [128, 128])
            self.ones = sb(st, "ones", [128, 128])
            self.iota = sb(st, "iota", [128, 256])
            self.spk = sb(st, "spk", [128, NSP])
            self.condT = sb(st, "condT", [128, KC, 2])
            self.modT = [sb(st, "modT%d" % i, [128, 48, 2]) for i in range(2)]
            self.gs = sb(st, "gs", [128, 2, 2, KC, 2])
            self.epsb = sb(st, "epsb", [128, 1])

            xv = self.xT_d.rearrange("(c p) t -> p c t", p=128)
            for c in range(KC):
                for g in range(NG):
                    P.dma(SP, lambda e, c=c, g=g: e.dma_start(out=self.xT[:, c, g * GT:(g + 1) * GT],
                                                               in_=xv[:, c, g * GT:(g + 1) * GT]),
                          writes=[self.xr[c][g]], sbuf=self.xr[c][g])
            for c in range(KC):
                for g in range(NG):
                    self.xr[c][g].w = (("W", "xload"), 16 * KC * NG)
            for (buf, src) in ((self.ident, self.ident_d), (self.ones, self.ones_d), (self.iota, self.iota_d),
                               (self.spk, self.spk_d), (self.condT, self.condT_d)):
                P.dma(SP, lambda e, buf=buf, src=src: e.dma_start(out=buf[:], in_=src),
                      writes=[buf.r], sbuf=buf.r)
            P.op(DVE, lambda e: e.memset(self.epsb[:], EPS), writes=[self.epsb.r])

            self.phase_tabcast()
            self.phase_ada()
            self.phase_conv_mixer()
            if self.do_peer:
                self.phase_peer(0)
            if self.do_ssd:
                self.phase_ssd()
            if self.do_peer:
                self.phase_peer(1)
            self.phase_final()
            P.emit()
        return nc

    def phase_tabcast(self):
        nc, P = self.nc, self.P
        R = 4
        NBUF = 2
        src = self.peer_uv_flat[0:16384].rearrange("(p a r) d -> a p (r d)", p=128, r=R)
        dst = self.uv_bf[0:16384].rearrange("(p a r) d -> a p (r d)", p=128, r=R)
        nslab = 16384 // (128 * R)
        with contextlib.ExitStack() as ph:
            stg = [self.sb(ph, "tstg%d" % i, [128, R * 2 * D]) for i in range(NBUF)]
            obf = [self.sb(ph, "tobf%d" % i, [128, R * 2 * D], BF16) for i in range(NBUF)]
            def ld(a):
                sg = stg[a % NBUF]
                P.dma(SP, lambda e, sg=sg, a=a: e.dma_start(out=sg[:], in_=src[a]), writes=[sg.r], sbuf=sg.r)

            for a in range(min(NBUF, nslab)):
                ld(a)
            for a in range(nslab):
                sg, ob = stg[a % NBUF], obf[a % NBUF]
                P.op(DVE, lambda e, sg=sg, ob=ob: e.tensor_copy(out=ob[:], in_=sg[:]), reads=[sg.r], writes=[ob.r])
                P.dma(POOL, lambda e, ob=ob, a=a: e.dma_start(out=dst[a], in_=ob[:]), reads=[ob.r], sbuf=ob.r, store=True)
                if a + NBUF < nslab:
                    ld(a + NBUF)
            P.barrier()

    def phase_ada(self):
        nc, P = self.nc, self.P
        with contextlib.ExitStack() as ph:
            sct = self.sb(ph, "sct", [128, KC, 2])
            wst = [self.sb(ph, "adaw%d" % i, [128, KC, 512]) for i in range(2)]
            P.op(ACT, lambda e: e.activation(out=sct[:], in_=self.condT[:], func=AF.Silu),
                 reads=[self.condT.r], writes=[sct.r])
            n = 0
            for i in range(2):
                wv = self.ada_w[i].rearrange("(k p) c -> p k c", p=128)
                pst = self.ps()
                for s in range(12):
                    w = wst[n % 2]
                    n += 1
                    P.dma(SP, lambda e, w=w, s=s, wv=wv: e.dma_start(out=w[:], in_=wv[:, :, s * 512:(s + 1) * 512]),
                          writes=[w.r], sbuf=w.r)
                    for cb in range(4):
                        dc = s * 4 + cb
                        for k in range(KC):
                            P.op(PE, lambda e, w=w, cb=cb, k=k, dc=dc, pst=pst: e.matmul(
                                pst[:, dc * 2:dc * 2 + 2], lhsT=w[:, k, cb * 128:(cb + 1) * 128],
                                rhs=sct[:, k, :], start=(k == 0), stop=(k == KC - 1)),
                                reads=[w.r, sct.r], writes=[pst.r], inc=(k == KC - 1))
                mod = self.modT[i]
                P.op(DVE, lambda e, mod=mod, pst=pst, i=i: e.tensor_tensor(
                    out=mod[:], in0=pst[:, 0:96].rearrange("p (c j) -> p c j", j=2),
                    in1=self.spk[:, SP_ADAB + 48 * i:SP_ADAB + 48 * (i + 1)].unsqueeze(2).to_broadcast([128, 48, 2]),
                    op=ALU.add), reads=[pst.r, self.spk.r], writes=[mod.r])
                for wh, (goff, sclo) in enumerate(((SP_NMG, 8), (SP_NFG, 32))):
                    P.op(DVE, lambda e, mod=mod, i=i, wh=wh, goff=goff, sclo=sclo: e.scalar_tensor_tensor(
                        out=self.gs[:, i, wh], in0=mod[:, sclo:sclo + 8, :], scalar=1.0,
                        in1=self.spk[:, goff + 8 * i:goff + 8 * (i + 1)].unsqueeze(2).to_broadcast([128, 8, 2]),
                        op0=ALU.add, op1=ALU.mult), reads=[mod.r, self.spk.r], writes=[self.gs.r])
            P.barrier()

    def m_sh(self, i, wh, c, ci):
        return self.modT[i][:, (0 if wh == 0 else 24) + c, ci:ci + 1]

    def m_gs(self, i, wh, c, ci):
        return self.gs[:, i, wh, c, ci:ci + 1]

    def m_gt(self, i, wh, c, ci):
        return self.modT[i][:, (16 if wh == 0 else 40) + c, ci:ci + 1]

    def norm_group(self, g, gs_fn, sh_fn, outs, rstd, sq, tmp):
        P = self.P
        ci = 0 if g < 4 else 1
        pss = self.ps()
        sl = slice(g * GT, (g + 1) * GT)
        for c in range(KC):
            P.op(ACT, lambda e, c=c: e.activation(out=sq[:, c % 2, :], in_=self.xT[:, c, sl], func=AF.Square),
                 reads=[self.xr[c][g]], writes=[sq.r])
            P.op(PE, lambda e, c=c: e.matmul(pss[:], lhsT=self.ones[:], rhs=sq[:, c % 2, :],
                                             start=(c == 0), stop=(c == KC - 1)),
                 reads=[self.ones.r, sq.r], writes=[pss.r], inc=True)
        P.op(ACT, lambda e: e.activation(out=rstd[:], in_=pss[:], func=AF.Sqrt, bias=self.epsb[:], scale=1.0 / D),
             reads=[pss.r, self.epsb.r], writes=[rstd.r])
        P.op(DVE, lambda e: e.reciprocal(out=rstd[:], in_=rstd[:]), reads=[rstd.r], writes=[rstd.r])
        for c in range(KC):
            P.op(DVE, lambda e, c=c: e.tensor_tensor(out=tmp[:, c % 2, :], in0=self.xT[:, c, sl], in1=rstd[:],
                                                    op=ALU.mult),
                 reads=[self.xr[c][g], rstd.r], writes=[tmp.r])
            for ob in outs:
                if sh_fn is None:
                    P.op(ACT, lambda e, c=c, ob=ob: e.activation(out=ob[:, c, :], in_=tmp[:, c % 2, :],
                                                                 func=AF.Identity, scale=gs_fn(c, ci)),
                         reads=[tmp.r, self.gs.r, self.spk.r], writes=[ob.r])
                else:
                    P.op(ACT, lambda e, c=c, ob=ob: e.activation(out=ob[:, c, :], in_=tmp[:, c % 2, :],
                                                                 func=AF.Identity, scale=gs_fn(c, ci),
                                                                 bias=sh_fn(c, ci)),
                         reads=[tmp.r, self.gs.r, self.modT[0].r, self.modT[1].r], writes=[ob.r])

    def phase_conv_mixer(self):
        nc, P = self.nc, self.P
        with contextlib.ExitStack() as ph:
            sb = self.sb
            hnb = sb(ph, "hnb", [128, KC, GT], BF16)
            rstd = sb(ph, "rstd", [128, GT])
            sq = sb(ph, "sq", [128, 2, GT])
            tmp = sb(ph, "tmp", [128, 2, GT])
            wst = [sb(ph, "wst%d" % i, [128, KC, 512]) for i in range(2)]
            wbf = [sb(ph, "wbf%d" % i, [128, KC, 512], BF16) for i in range(2)]
            bg = sb(ph, "bg", [128, KC, GT])
            cg = [sb(ph, "cg%d" % k, [128, GT]) for k in range(KC)]
            yv = [sb(ph, "yv%d" % k, [128, GT]) for k in range(2)]
            mm = sb(ph, "mm", [128, KC, GT], BF16)
            bgr = [Res("bg%d" % k) for k in range(KC)]
            mmr = [Res("mm%d" % k) for k in range(KC)]
            w_in = self.sc_w_in[0].rearrange("(k p) c -> p k c", p=128)
            w_out = self.sc_w_out[0].rearrange("(k p) c -> p k c", p=128)
            nload = 0

            def load_w(view, c0):
                nonlocal nload
                ws, wb = wst[nload % 2], wbf[nload % 2]
                ce = (ACT, DVE)[nload % 2]
                nload += 1
                P.dma(SP, lambda e: e.dma_start(out=ws[:], in_=view[:, :, c0:c0 + 512]), writes=[ws.r], sbuf=ws.r)
                if ce == ACT:
                    P.op(ACT, lambda e: e.copy(out=wb[:], in_=ws[:]), reads=[ws.r], writes=[wb.r])
                else:
                    P.op(DVE, lambda e: e.tensor_copy(out=wb[:], in_=ws[:]), reads=[ws.r], writes=[wb.r])
                return wb

            for g in range(NG):
                ci = 0 if g < 4 else 1
                sl = slice(g * GT, (g + 1) * GT)
                rows, cols = (8, 64) if g < 4 else (2, 256)
                self.norm_group(g, lambda c, ci: self.m_gs(0, 0, c, ci), lambda c, ci: self.m_sh(0, 0, c, ci),
                                [hnb], rstd, sq, tmp)
                for s in range(6):
                    wb = load_w(w_in, s * 512)
                    for cb in range(4):
                        blk = s * 4 + cb
                        pt = self.ps()
                        for k in range(KC):
                            P.op(PE, lambda e, wb=wb, cb=cb, k=k, pt=pt: e.matmul(
                                pt[:], lhsT=wb[:, k, cb * 128:(cb + 1) * 128], rhs=hnb[:, k, :],
                                start=(k == 0), stop=(k == KC - 1)),
                                reads=[wb.r, hnb.r], writes=[pt.r], inc=(k == KC - 1))
                        if blk < 8:
                            P.op(ACT, lambda e, blk=blk, pt=pt: e.copy(out=bg[:, blk, :], in_=pt[:]),
                                 reads=[pt.r], writes=[bgr[blk]])
                        elif blk < 16:
                            cc = cg[blk - 8]
                            P.op(ACT, lambda e, cc=cc, pt=pt: e.copy(out=cc[:], in_=pt[:]),
                                 reads=[pt.r], writes=[cc.r])
                        else:
                            k8 = blk - 16
                            cc = cg[k8]
                            y = yv[k8 % 2]
                            P.op(DVE, lambda e, cc=cc, pt=pt: e.tensor_tensor(out=cc[:], in0=cc[:], in1=pt[:],
                                                                              op=ALU.mult),
                                 reads=[cc.r, pt.r], writes=[cc.r])
                            wcol = lambda tap, k8=k8: self.spk[:, SP_SCW + tap * 8 + k8:SP_SCW + tap * 8 + k8 + 1]
                            u3 = cc[:].rearrange("p (r w) -> p r w", w=cols)
                            y3 = y[:].rearrange("p (r w) -> p r w", w=cols)
                            P.op(DVE, lambda e, cc=cc, y=y, wcol=wcol: e.tensor_scalar(
                                out=y[:], in0=cc[:], scalar1=wcol(1), scalar2=None, op0=ALU.mult),
                                reads=[cc.r, self.spk.r], writes=[y.r])
                            P.op(DVE, lambda e, u3=u3, y3=y3, wcol=wcol: e.scalar_tensor_tensor(
                                out=y3[:, :, 1:], in0=u3[:, :, :cols - 1], scalar=wcol(0), in1=y3[:, :, 1:],
                                op0=ALU.mult, op1=ALU.add), reads=[cc.r, y.r, self.spk.r], writes=[y.r])
                            P.op(DVE, lambda e, u3=u3, y3=y3, wcol=wcol: e.scalar_tensor_tensor(
                                out=y3[:, :, :cols - 1], in0=u3[:, :, 1:], scalar=wcol(2), in1=y3[:, :, :cols - 1],
                                op0=ALU.mult, op1=ALU.add), reads=[cc.r, y.r, self.spk.r], writes=[y.r])
                            P.op(DVE, lambda e, k8=k8, y=y: e.tensor_tensor(out=mm[:, k8, :], in0=bg[:, k8, :],
                                                                          in1=y[:], op=ALU.mult),
                                 reads=[bgr[k8], y.r], writes=[mmr[k8]])
                for s in range(2):
                    wb = load_w(w_out, s * 512)
                    for cb in range(4):
                        dc = s * 4 + cb
                        pt = self.ps()
                        for k in range(KC):
                            P.op(PE, lambda e, wb=wb, cb=cb, k=k, pt=pt: e.matmul(
                                pt[:], lhsT=wb[:, k, cb * 128:(cb + 1) * 128], rhs=mm[:, k, :],
                                start=(k == 0), stop=(k == KC - 1)),
                                reads=[wb.r] + mmr, writes=[pt.r], inc=(k == KC - 1))
                        P.op(DVE, lambda e, dc=dc, pt=pt, ci=ci: e.scalar_tensor_tensor(
                            out=self.xT[:, dc, sl], in0=pt[:], scalar=self.m_gt(0, 0, dc, ci), in1=self.xT[:, dc, sl],
                            op0=ALU.mult, op1=ALU.add),
                            reads=[pt.r, self.modT[0].r, self.xr[dc][g]], writes=[self.xr[dc][g]])
            P.barrier()

    def phase_peer(self, li):
        nc, P = self.nc, self.P
        if li == 1 and getattr(self, "tab1_tok", None) is not None:
            P.streams[POOL].append({"waits": [self.tab1_tok], "fn": None, "inc": None})
        NS = self.n_slots
        with contextlib.ExitStack() as ph:
            sb = self.sb
            NB = 10
            gb = [sb(ph, "gb%d" % i, [128, 2 * D], BF16) for i in range(NB)]
            hnf = sb(ph, "hnf", [128, KC, GT])
            rstd = sb(ph, "rstd", [128, GT])
            sq = sb(ph, "sq", [128, 2, GT])
            tmp = sb(ph, "tmp", [128, 2, GT])
            keyT = sb(ph, "keyT", [128, 16, 128])
            wqb = [sb(ph, "wqb%d" % i, [128, KC, 128]) for i in range(2)]
            qT = [sb(ph, "qT%d" % i, [128, GT]) for i in range(2)]
            scs = [sb(ph, "scs%d" % i, [128, 4, 128]) for i in range(2)]
            mrtL = [sb(ph, "mrt%d" % i, [128, 128]) for i in range(4)]
            vv_r = [Res("vv%d" % i) for i in range(4)]
            iu_r = [Res("iu%d" % i) for i in range(4)]
            vv = sb(ph, "vv", [128, 4, 16, 16])
            iu = sb(ph, "iu", [128, 4, 16, 16], U32)
            i_f = sb(ph, "i_f", [128, 16, 16])
            i1s = sb(ph, "i1s", [128, 8, 16])
            pos = sb(ph, "pos", [128, 8, 16], U32)
            posf = sb(ph, "posf", [128, 8, 16])
            pa_ = sb(ph, "pa_", [128, 8, 16])
            pb_ = sb(ph, "pb_", [128, 8, 16])
            idx1 = sb(ph, "idx1", [128, 8, 16])
            idx2 = sb(ph, "idx2", [128, 8, 16])
            thr16 = sb(ph, "thr16", [128, 16])
            P.op(DVE, lambda e: e.tensor_scalar(out=thr16[:], in0=self.iota[:, 0:16], scalar1=16.0, scalar2=16.0,
                                               op0=ALU.mult, op1=ALU.add), reads=[self.iota.r], writes=[thr16.r])
            P.op(DVE, lambda e: e.memset(thr16[:, 15:16], 1.0e9), writes=[thr16.r])
            cand = sb(ph, "cand", [128, 8, 256])
            cidx = cand
            cwork = sb(ph, "cwork", [128, 256])
            sc16 = sb(ph, "sc16", [128, 8, 16])
            idxf = sb(ph, "idxf", [128, 128])
            idxiL = [sb(ph, "idxi%d" % i, [128, 128], I32) for i in range(2)]
            ee = sb(ph, "ee", [128, 8, 16])
            ggL = [sb(ph, "gg%d" % i, [128, 128]) for i in range(2)]
            act_r = [Res("act%d" % j) for j in range(128)]
            coef_r = [Res("coef%d" % j) for j in range(128)]
            negm = sb(ph, "negm", [128, 8])
            esum = sb(ph, "esum", [128, 8])
            hn2 = sb(ph, "hn2", [128, D])
            act = sb(ph, "act", [128, 128])
            coef = sb(ph, "coef", [128, 128])
            junk2v = cand[:, 0:4, :].rearrange("p a b -> p (a b)")
            dg = [sb(ph, "dg%d" % i, [128, 128], BF16) for i in range(3)]
            zer = sb(ph, "zer", [128, 128])
            P.op(DVE, lambda e: e.memset(zer[:], 0.0), writes=[zer.r])
            self.psb_all = self.psb
            pacc = self.psb_all[6:8]
            self.psb = self.psb_all[0:6]
            nq = 0
            ngb = 0

            keyn_b = hnf
            keyn = hnf[:, 0:4, :].rearrange("p k (a e) -> p (k a) e", e=128)
            kv = self.peer_keys[li].rearrange("h p n e -> n (h p) e")
            P.dma(SP, lambda e: e.dma_start(out=keyn, in_=kv), writes=[keyn_b.r], sbuf=keyn_b.r)
            for q4 in range(4):
                pt = self.ps()
                for j in range(4):
                    hp = q4 * 4 + j
                    P.op(PE, lambda e, hp=hp, j=j, pt=pt: e.transpose(out=pt[:, j * 128:(j + 1) * 128],
                                                                    in_=keyn[:, hp, :], identity=self.ident[:]),
                         reads=[keyn_b.r, self.ident.r], writes=[pt.r], inc=(j == 3))
                P.op(ACT, lambda e, q4=q4, pt=pt: e.copy(out=keyT[:, q4 * 4:(q4 + 1) * 4, :],
                                                        in_=pt[:].rearrange("p (j n) -> p j n", n=128)),
                     reads=[pt.r], writes=[keyT.r])

            wqv = self.peer_wq[li].rearrange("(k p) c -> p k c", p=128)
            for g in range(NG):
                ci = 0 if g < 4 else 1
                self.norm_group(g, lambda c, ci: self.m_gs(li, 1, c, ci), lambda c, ci: self.m_sh(li, 1, c, ci),
                                [hnf], rstd, sq, tmp)
                for hp in range(16):
                    wq = wqb[nq % 2]
                    qt = qT[nq % 2]
                    ss = scs[nq % 2]
                    nq += 1
                    P.dma(SP, lambda e, wq=wq, hp=hp: e.dma_start(out=wq[:], in_=wqv[:, :, hp * 128:(hp + 1) * 128]),
                          writes=[wq.r], sbuf=wq.r)
                    pq = self.ps()
                    for k in range(KC):
                        P.op(PE, lambda e, wq=wq, k=k, pq=pq: e.matmul(pq[:], lhsT=wq[:, k, :], rhs=hnf[:, k, :],
                                                                       start=(k == 0), stop=(k == KC - 1)),
                             reads=[wq.r, hnf.r], writes=[pq.r], inc=(k == KC - 1))
                    P.op(ACT, lambda e, qt=qt, pq=pq: e.copy(out=qt[:], in_=pq[:]), reads=[pq.r], writes=[qt.r])
                    psc = self.ps()
                    for tt in range(4):
                        P.op(PE, lambda e, qt=qt, tt=tt, psc=psc, hp=hp: e.matmul(
                            psc[:, tt * 128:(tt + 1) * 128], lhsT=qt[:, tt * 128:(tt + 1) * 128],
                            rhs=keyT[:, hp, :], start=True, stop=True),
                            reads=[qt.r, keyT.r], writes=[psc.r], inc=(tt == 3))
                    P.op(ACT, lambda e, ss=ss, psc=psc: e.copy(out=ss[:], in_=psc[:].rearrange("p (t n) -> p t n", n=128)),
                         reads=[psc.r], writes=[ss.r])
                    for tt in range(4):
                        P.op(DVE, lambda e, ss=ss, tt=tt, hp=hp: e.max(out=vv[:, tt, hp, 0:8], in_=ss[:, tt, :]),
                             reads=[ss.r], writes=[vv_r[tt]])
                    for tt in range(4):
                        P.op(DVE, lambda e, ss=ss, tt=tt, hp=hp: e.match_replace(
                            out=mrtL[tt][:], in_to_replace=vv[:, tt, hp, 0:8], in_values=ss[:, tt, :], imm_value=NEG),
                            reads=[ss.r, vv_r[tt]], writes=[mrtL[tt].r])
                    for tt in range(4):
                        P.op(DVE, lambda e, tt=tt, hp=hp: e.max(out=vv[:, tt, hp, 8:16], in_=mrtL[tt][:]),
                             reads=[mrtL[tt].r], writes=[vv_r[tt]])
                    for tt in range(4):
                        P.op(DVE, lambda e, ss=ss, tt=tt, hp=hp: e.max_index(
                            out=iu[:, tt, hp, 0:8], in_max=vv[:, tt, hp, 0:8], in_values=ss[:, tt, :]),
                            reads=[ss.r, vv_r[tt]], writes=[iu_r[tt]])
                    for tt in range(4):
                        P.op(DVE, lambda e, ss=ss, tt=tt, hp=hp: e.max_index(
                            out=iu[:, tt, hp, 8:16], in_max=vv[:, tt, hp, 8:16], in_values=ss[:, tt, :]),
                            reads=[ss.r, vv_r[tt]], writes=[iu_r[tt]])
                def prep(tt):
                    idxi, gg = idxiL[tt % 2], ggL[tt % 2]
                    P.op(DVE, lambda e, tt=tt: e.tensor_copy(out=i_f[:], in_=iu[:, tt]), reads=[iu_r[tt]], writes=[i_f.r])
                    v1 = vv[:, tt, 0:16:2, :]
                    v2 = vv[:, tt, 1:16:2, :]
                    c4 = cand[:].rearrange("p h (a b) -> p h a b", b=16)
                    x4 = cidx[:].rearrange("p h (a b) -> p h a b", b=16)
                    P.op(DVE, lambda e, v1=v1, v2=v2, c4=c4: e.tensor_tensor(
                        out=c4, in0=v1.unsqueeze(3).to_broadcast([128, 8, 16, 16]),
                        in1=v2.unsqueeze(2).to_broadcast([128, 8, 16, 16]), op=ALU.add),
                        reads=[vv_r[tt]], writes=[cand.r])
                    P.op(DVE, lambda e: e.tensor_scalar(out=i1s[:], in0=i_f[:, 0:16:2, :], scalar1=128.0,
                                                       scalar2=float(li * 16384), op0=ALU.mult, op1=ALU.add),
                         reads=[i_f.r], writes=[i1s.r])
                    for h in range(8):
                        P.op(DVE, lambda e, h=h: e.max(out=sc16[:, h, 0:8], in_=cand[:, h, :]),
                             reads=[cand.r], writes=[sc16.r])
                        P.op(DVE, lambda e, h=h: e.match_replace(out=cwork[:], in_to_replace=sc16[:, h, 0:8],
                                                                in_values=cand[:, h, :], imm_value=NEG),
                             reads=[cand.r, sc16.r], writes=[cwork.r])
                        P.op(DVE, lambda e, h=h: e.max(out=sc16[:, h, 8:16], in_=cwork[:]),
                             reads=[cwork.r], writes=[sc16.r])
                    for h in range(8):
                        for k8 in range(2):
                            P.op(DVE, lambda e, h=h, k8=k8: e.max_index(
                                out=pos[:, h, k8 * 8:(k8 + 1) * 8], in_max=sc16[:, h, k8 * 8:(k8 + 1) * 8],
                                in_values=cand[:, h, :]), reads=[cand.r, sc16.r], writes=[pos.r])
                    P.op(DVE, lambda e: e.tensor_copy(out=posf[:], in_=pos[:]), reads=[pos.r], writes=[posf.r])
                    P.op(DVE, lambda e, x4=x4: e.tensor_tensor(
                        out=x4, in0=posf[:].unsqueeze(3).to_broadcast([128, 8, 16, 16]),
                        in1=thr16[:].unsqueeze(1).unsqueeze(1).to_broadcast([128, 8, 16, 16]), op=ALU.is_ge),
                        reads=[posf.r, thr16.r], writes=[cand.r])
                    P.op(DVE, lambda e, x4=x4: e.reduce_sum(out=pa_[:], in_=x4, axis=mybir.AxisListType.X),
                         reads=[cand.r], writes=[pa_.r])
                    P.op(DVE, lambda e: e.scalar_tensor_tensor(out=pb_[:], in0=pa_[:], scalar=-16.0, in1=posf[:],
                                                               op0=ALU.mult, op1=ALU.add), reads=[pa_.r, posf.r], writes=[pb_.r])
                    io16 = self.iota[:, 0:16].unsqueeze(1).unsqueeze(1).to_broadcast([128, 8, 16, 16])
                    for (sel, tab, dsti) in ((pa_, i1s[:], idx1), (pb_, i_f[:, 1:16:2, :], idx2)):
                        P.op(DVE, lambda e, sel=sel, x4=x4: e.tensor_tensor(
                            out=x4, in0=io16, in1=sel[:].unsqueeze(3).to_broadcast([128, 8, 16, 16]), op=ALU.is_equal),
                            reads=[self.iota.r, sel.r], writes=[cand.r])
                        P.op(DVE, lambda e, tab=tab, x4=x4: e.tensor_tensor(
                            out=x4, in0=x4, in1=tab.unsqueeze(2).to_broadcast([128, 8, 16, 16]), op=ALU.mult),
                            reads=[cand.r, i1s.r, i_f.r], writes=[cand.r])
                        P.op(DVE, lambda e, dsti=dsti, x4=x4: e.reduce_sum(out=dsti[:], in_=x4, axis=mybir.AxisListType.X),
                             reads=[cand.r], writes=[dsti.r])
                    P.op(DVE, lambda e: e.tensor_tensor(out=idxf[:].rearrange("p (h k) -> p h k", k=16), in0=idx1[:], in1=idx2[:],
                                                        op=ALU.add), reads=[idx1.r, idx2.r], writes=[idxf.r])
                    P.op(DVE, lambda e: e.tensor_copy(out=idxi[:], in_=idxf[:]), reads=[idxf.r], writes=[idxi.r])
                    P.op(DVE, lambda e: e.tensor_scalar(out=negm[:], in0=sc16[:, :, 0], scalar1=-1.0, scalar2=None,
                                                       op0=ALU.mult), reads=[sc16.r], writes=[negm.r])
                    for h in range(8):
                        P.op(ACT, lambda e, h=h: e.activation(out=ee[:, h, :], in_=sc16[:, h, :], func=AF.Exp,
                                                             bias=negm[:, h:h + 1], scale=1.0),
                             reads=[sc16.r, negm.r], writes=[ee.r])
                    P.op(DVE, lambda e: e.reduce_sum(out=esum[:], in_=ee[:], axis=mybir.AxisListType.X),
                         reads=[ee.r], writes=[esum.r])
                    P.op(DVE, lambda e: e.reciprocal(out=esum[:], in_=esum[:]), reads=[esum.r], writes=[esum.r])
                    P.op(DVE, lambda e: e.tensor_tensor(
                        out=gg[:].rearrange("p (h k) -> p h k", k=16), in0=ee[:],
                        in1=esum[:].unsqueeze(2).to_broadcast([128, 8, 16]), op=ALU.mult),
                        reads=[ee.r, esum.r], writes=[gg.r])

                def slots(tt):
                    nonlocal ngb
                    idxi, gg = idxiL[tt % 2], ggL[tt % 2]
                    tsl = slice(g * GT + tt * 128, g * GT + (tt + 1) * 128)
                    for half in range(2):
                        pt = self.ps()
                        for j in range(4):
                            c = half * 4 + j
                            P.op(PE, lambda e, c=c, j=j, pt=pt, tt=tt: e.transpose(
                                out=pt[:, j * 128:(j + 1) * 128], in_=hnf[:, c, tt * 128:(tt + 1) * 128],
                                identity=self.ident[:]), reads=[hnf.r, self.ident.r], writes=[pt.r], inc=(j == 3))
                        P.op(ACT, lambda e, half=half, pt=pt: e.copy(out=hn2[:, half * 512:(half + 1) * 512], in_=pt[:]),
                             reads=[pt.r], writes=[hn2.r])
                    for pa in pacc:
                        P.op(PE, lambda e, pa=pa: e.matmul(pa[:], lhsT=zer[:], rhs=hnf[:, 0, :], start=True, stop=False),
                             reads=[zer.r, hnf.r], writes=[pa.r])
                    for j in range(NS):
                        b = gb[ngb % NB]
                        dgb = dg[ngb % 3]
                        ngb += 1
                        P.dma(POOL, lambda e, b=b, j=j: e.indirect_dma_start(
                            out=b[:], out_offset=None, in_=self.uv_bf,
                            in_offset=bass.IndirectOffsetOnAxis(ap=idxi[:, j:j + 1], axis=0)),
                            reads=[idxi.r], writes=[b.r], sbuf=b.r)
                        P.op(DVE, lambda e, b=b, j=j: e.scalar_tensor_tensor(
                            out=junk2v, in0=hn2[:], scalar=1.0, in1=b[:, 0:D], op0=ALU.mult, op1=ALU.mult,
                            accum_out=act[:, j:j + 1]), reads=[hn2.r, b.r], writes=[act_r[j]])
                        P.op(ACT, lambda e, j=j: e.activation(out=coef[:, j:j + 1], in_=act[:, j:j + 1], func=AF.Gelu),
                             reads=[act_r[j]], writes=[coef_r[j]])
                        P.op(ACT, lambda e, j=j: e.activation(out=coef[:, j:j + 1], in_=coef[:, j:j + 1], func=AF.Copy,
                                                             scale=gg[:, j:j + 1]), reads=[coef_r[j], gg.r], writes=[coef_r[j]])
                        P.op(ACT, lambda e, dgb=dgb, j=j: e.activation(out=dgb[:], in_=self.ident[:], func=AF.Copy,
                                                                      scale=coef[:, j:j + 1]),
                             reads=[self.ident.r, coef_r[j]], writes=[dgb.r])
                        for c in range(KC):
                            pa = pacc[c // 4]
                            P.op(PE, lambda e, b=b, dgb=dgb, c=c, pa=pa, j=j: e.matmul(
                                pa[:, (c % 4) * 128:(c % 4 + 1) * 128], lhsT=b[:, D + c * 128:D + (c + 1) * 128], rhs=dgb[:],
                                start=False, stop=(j == NS - 1)),
                                reads=[b.r, dgb.r], writes=[pa.r], inc=(c == KC - 1))
                    for c in range(KC):
                        pa = pacc[c // 4]
                        P.op(DVE, lambda e, c=c, pa=pa, ci=ci, tsl=tsl: e.scalar_tensor_tensor(
                            out=self.xT[:, c, tsl], in0=pa[:, (c % 4) * 128:(c % 4 + 1) * 128],
                            scalar=self.m_gt(li, 1, c, ci), in1=self.xT[:, c, tsl], op0=ALU.mult, op1=ALU.add),
                            reads=[pa.r, self.modT[li].r, self.xr[c][g]], writes=[self.xr[c][g]])

                prep(0)
                for tt in range(4):
                    if tt < 3:
                        prep(tt + 1)
                    slots(tt)
            self.psb = self.psb_all
            P.barrier()

    def norm_slice(self, tok0, W, g, ci, gs_fn, sh_fn, ob, rstd, sq, tmp):
        P = self.P
        pss = self.ps()
        sl = slice(tok0, tok0 + W)
        for c in range(KC):
            P.op(ACT, lambda e, c=c: e.activation(out=sq[:, c % 2, :], in_=self.xT[:, c, sl], func=AF.Square),
                 reads=[self.xr[c][g]], writes=[sq.r])
            P.op(PE, lambda e, c=c: e.matmul(pss[:, 0:W], lhsT=self.ones[:], rhs=sq[:, c % 2, :],
                                             start=(c == 0), stop=(c == KC - 1)),
                 reads=[self.ones.r, sq.r], writes=[pss.r], inc=True)
        P.op(ACT, lambda e: e.activation(out=rstd[:], in_=pss[:, 0:W], func=AF.Sqrt, bias=self.epsb[:], scale=1.0 / D),
             reads=[pss.r, self.epsb.r], writes=[rstd.r])
        P.op(DVE, lambda e: e.reciprocal(out=rstd[:], in_=rstd[:]), reads=[rstd.r], writes=[rstd.r])
        for c in range(KC):
            P.op(DVE, lambda e, c=c: e.tensor_tensor(out=tmp[:, c % 2, :], in0=self.xT[:, c, sl], in1=rstd[:], op=ALU.mult),
                 reads=[self.xr[c][g], rstd.r], writes=[tmp.r])
            P.op(ACT, lambda e, c=c: e.activation(out=ob[:, c, :], in_=tmp[:, c % 2, :], func=AF.Identity,
                                                  scale=gs_fn(c, ci), bias=sh_fn(c, ci)),
                 reads=[tmp.r, self.gs.r, self.modT[0].r, self.modT[1].r], writes=[ob.r])

    def phase_ssd(self):
        nc, P = self.nc, self.P
        key = ("X", "tabl1")
        P._sem(key)
        NCH = 16
        tab1_items = []
        for k_ in range(NCH):
            r0 = 16384 + k_ * (16384 // NCH)
            r1 = r0 + 16384 // NCH
            tab1_items.append({"waits": [], "inc": (key, 16), "fn": _freeze(
                lambda e, r0=r0, r1=r1: e.dma_start(out=self.uv_bf[r0:r1, :], in_=self.peer_uv_flat[r0:r1, :]))})
        self.tab1_tok = (key, 16 * NCH)
        X = mybir.AxisListType.X
        W = 256
        with contextlib.ExitStack() as ph:
            sb = self.sb
            rowp = sb(ph, "rowp", [128, ROWP_N])
            par = [0]
            two = lambda name, shape, dt=F32: [sb(ph, name + str(i), shape, dt) for i in range(2)]
            tri = [sb(ph, "tri%d" % d, [128, 128]) for d in range(2)]
            mneg = [sb(ph, "mneg%d" % d, [128, 128]) for d in range(2)]
            flags = sb(ph, "flags", [128, 4])
            onec = sb(ph, "onec", [128, 1])
            hnb = sb(ph, "hnb", [128, KC, W], BF16)
            rstd = sb(ph, "rstd", [128, W])
            sq = sb(ph, "sq", [128, 2, W])
            tmp = sb(ph, "tmp", [128, 2, W])
            wbf = [sb(ph, "wbf%d" % i, [128, KC * 256], BF16) for i in range(2)]
            wdtb = sb(ph, "wdtb", [128, KC, 64], BF16)
            yb = [sb(ph, "yb%d" % i, [128, W]) for i in range(3)]
            silb = [sb(ph, "silb%d" % i, [128, W]) for i in range(3)]
            BT = sb(ph, "BT", [128, 8, W], BF16)
            CT = sb(ph, "CT", [128, 8, W], BF16)
            xs_tok = sb(ph, "xs_tok", [128, 2, 2048], BF16)
            B_tok = sb(ph, "B_tok", [128, 2, 1024], BF16)
            zs = sb(ph, "zs", [128, 2048], BF16)
            xx = sb(ph, "xx", [128, 64])
            ax = sb(ph, "ax", [128, 64])
            dt = sb(ph, "dt", [128, 2, 64])
            dA = sb(ph, "dA", [128, 2, 64])
            arow = sb(ph, "arow", [128, 64])
            drow = sb(ph, "drow", [128, 32])
            cumcL = two("cumc", [128, 32])
            totL = two("tot", [128, 32])
            wv = sb(ph, "wv", [128, 32])
            dec = sb(ph, "dec", [128, 32])
            Pb = sb(ph, "Pb", [128, 32])
            xdt = [sb(ph, "xdt%d" % d, [128, 2048], BF16) for d in range(2)]
            xw = sb(ph, "xw", [128, 2048], BF16)
            RgL = two("Rg", [128, 512])
            segL = two("seg", [128, 512])
            GmL = [two("Gm%d_" % d, [128, 512], BF16) for d in range(2)]
            EeL = two("Ee", [128, 512])
            CsL = two("Cs", [128, 512], BF16)
            CBtL = two("CBt", [128, 128])
            hT = [sb(ph, "hT%d" % d, [128, 2048]) for d in range(2)]
            hTb = [sb(ph, "hTb%d" % d, [128, 2048], BF16) for d in range(2)]
            ysb = sb(ph, "ysb", [128, 2048])

            t1 = sb(ph, "t1", [128, 256])
            ssq = sb(ph, "ssq", [128, 1])
            ynT = sb(ph, "ynT", [128, 16, 128], BF16)
            class _AB:
                def __init__(self, ap, name):
                    self.ap = ap
                    self.r = Res(name)

                def __getitem__(self, k):
                    return self.ap[k]

            _h0b = hT[0][:].bitcast(BF16)
            zsL = [zs, _AB(_h0b[:, 0:2048], "zs_alt")]
            ynTL = [ynT, _AB(_h0b[:, 2048:4096].rearrange("p (k n) -> p k n", n=128), "ynT_alt")]
            w_in = self.ssd_w_in[0].rearrange("(k p) c -> p k c", p=128)
            w_out = self.ssd_w_out[0].rearrange("(k p) c -> p k c", p=128)
            w_in_bf = self.w_in_bf.rearrange("(k p) c -> p k c", p=128)
            w_out_bf = self.w_out_bf.rearrange("(k p) c -> p k c", p=128)
            nld = [0]

            def load_w(view, c0, wide=True):
                wb = wbf[nld[0] % 2]
                nld[0] += 1
                ncol = 256 if wide else 128
                dst = wb[:].rearrange("p (k c) -> p k c", c=ncol)
                P.dma(SP, lambda e: e.dma_start(out=dst, in_=view[:, :, c0:c0 + ncol]), writes=[wb.r], sbuf=wb.r)
                return wb, dst

            for (buf, src) in ((rowp, self.rowp_d), (tri[0], self.tri_d[0]), (tri[1], self.tri_d[1]),
                               (mneg[0], self.mneg_d[0]), (mneg[1], self.mneg_d[1]), (flags, self.flags_d)):
                P.dma(SP, lambda e, buf=buf, src=src: e.dma_start(out=buf[:], in_=src), writes=[buf.r], sbuf=buf.r)
            P.op(DVE, lambda e: e.memset(onec[:], 1.0), writes=[onec.r])
            P.op(ACT, lambda e: e.activation(out=arow[:], in_=rowp[:, RP_ALOG:RP_ALOG + 64], func=AF.Exp),
                 reads=[rowp.r], writes=[arow.r])
            P.op(DVE, lambda e: e.tensor_scalar(out=arow[:], in0=arow[:], scalar1=-1.0, scalar2=None, op0=ALU.mult),
                 reads=[arow.r], writes=[arow.r])
            P.op(DVE, lambda e: e.tensor_tensor(out=drow[:], in0=rowp[:, RP_D:RP_D + 32], in1=rowp[:, RP_D + 32:RP_D + 64],
                                                op=ALU.add), reads=[rowp.r], writes=[drow.r])
            stgs = [ysb, hT[0], hT[1]]
            cast_eng = (DVE, ACT, DVE)
            nst = 0
            for (src, dstv, K_, ncol, total) in ((w_in, w_in_bf, KC, 256, 6208), (w_out, w_out_bf, 16, 128, D)):
                for c0 in range(0, total, ncol):
                    n_ = min(ncol, total - c0)
                    stg = stgs[nst % 3]
                    wb = wbf[nst % 2]
                    ce = cast_eng[nst % 3]
                    nst += 1
                    sv = stg[:].rearrange("p (k c) -> p k c", c=ncol)[:, :, 0:n_]
                    wv_ = wb[:].rearrange("p (k c) -> p k c", c=ncol)[:, :, 0:n_]
                    P.dma(SP, lambda e, sv=sv, src=src, c0=c0, n_=n_: e.dma_start(out=sv, in_=src[:, :, c0:c0 + n_]),
                          writes=[stg.r], sbuf=stg.r)
                    if ce == ACT:
                        P.op(ACT, lambda e, sv=sv, wv_=wv_: e.copy(out=wv_, in_=sv), reads=[stg.r], writes=[wb.r])
                    elif ce == DVE:
                        P.op(DVE, lambda e, sv=sv, wv_=wv_: e.tensor_copy(out=wv_, in_=sv), reads=[stg.r], writes=[wb.r])
                    else:
                        P.op(POOL, lambda e, sv=sv, wv_=wv_: e.tensor_copy(out=wv_, in_=sv), reads=[stg.r], writes=[wb.r])
                    P.dma(SP, lambda e, wv_=wv_, dstv=dstv, c0=c0, n_=n_: e.dma_start(out=dstv[:, :, c0:c0 + n_], in_=wv_),
                          reads=[wb.r], sbuf=wb.r, store=True)
            P.barrier()
            P.dma(SP, lambda e: e.dma_start(out=wdtb[:], in_=w_in_bf[:, :, 6144:6208]), writes=[wdtb.r], sbuf=wdtb.r)

            def fe_io(pi, store):
                items = ((xs_tok, "p a b -> p (a b)"), (B_tok, "p a b -> p (a b)"), (BT, "p a b -> p (a b)"),
                         (CT, "p a b -> p (a b)"), (dt, "p a b -> p (a b)"), (dA, "p a b -> p (a b)"))
                for k_, (buf, pat) in enumerate(items):
                    view = buf[:].rearrange(pat)
                    dram = self.fe_d[k_][pi]
                    rr = self.fe_r[k_][pi]
                    if store:
                        P.dma(SP, lambda e, view=view, dram=dram: e.dma_start(out=dram, in_=view), reads=[buf.r], writes=[rr],
                              sbuf=buf.r, store=True)
                    else:
                        P.dma(SP, lambda e, view=view, dram=dram: e.dma_start(out=view, in_=dram), reads=[rr], writes=[buf.r],
                              sbuf=buf.r)

            def frontend(tok0, cols, ci, g, only_norm=False):
                self.norm_slice(tok0, W, g, ci, lambda c, ci: self.m_gs(1, 0, c, ci), lambda c, ci: self.m_sh(1, 0, c, ci),
                                hnb, rstd, sq, tmp)
                if only_norm:
                    return
                for ch in range(2):
                    pd = self.ps()
                    for k in range(KC):
                        P.op(PE, lambda e, k=k, ch=ch, pd=pd: e.matmul(pd[:, 0:64], lhsT=hnb[:, k, ch * 128:(ch + 1) * 128],
                                                                       rhs=wdtb[:, k, :], start=(k == 0), stop=(k == KC - 1)),
                             reads=[hnb.r, wdtb.r], writes=[pd.r], inc=(k == KC - 1))
                    P.op(DVE, lambda e, pd=pd: e.tensor_tensor(out=xx[:], in0=pd[:, 0:64], in1=rowp[:, RP_DTB:RP_DTB + 64],
                                                               op=ALU.add), reads=[pd.r, rowp.r], writes=[xx.r])
                    P.op(ACT, lambda e: e.activation(out=ax[:], in_=xx[:], func=AF.Abs), reads=[xx.r], writes=[ax.r])
                    P.op(ACT, lambda e: e.activation(out=ax[:], in_=ax[:], func=AF.Exp, scale=-1.0), reads=[ax.r], writes=[ax.r])
                    P.op(ACT, lambda e: e.activation(out=ax[:], in_=ax[:], func=AF.Ln, bias=onec[:], scale=1.0),
                         reads=[ax.r, onec.r], writes=[ax.r])
                    P.op(DVE, lambda e, ch=ch: e.scalar_tensor_tensor(out=dt[:, ch, :], in0=xx[:], scalar=0.0, in1=ax[:],
                                                                      op0=ALU.max, op1=ALU.add),
                         reads=[xx.r, ax.r], writes=[dt.r])
                    P.op(DVE, lambda e, ch=ch: e.tensor_tensor(out=dA[:, ch, :], in0=dt[:, ch, :], in1=arow[:], op=ALU.mult),
                         reads=[dt.r, arow.r], writes=[dA.r])
                nblk = 0
                for s in range(16):
                    wbr, wb = load_w(w_in_bf, 2048 + s * 256)
                    for cb in range(2):
                        blk = s * 2 + cb
                        y, sl_ = yb[nblk % 3], silb[nblk % 3]
                        nblk += 1
                        pt = self.ps()
                        for k in range(KC):
                            P.op(PE, lambda e, wb=wb, cb=cb, k=k, pt=pt: e.matmul(
                                pt[:, 0:W], lhsT=wb[:, k, cb * 128:(cb + 1) * 128], rhs=hnb[:, k, :],
                                start=(k == 0), stop=(k == KC - 1)), reads=[wbr.r, hnb.r], writes=[pt.r], inc=(k == KC - 1))
                        u = pt
                        wcol = lambda tap, blk=blk: self.spk[:, SP_SSW + tap * 32 + blk:SP_SSW + tap * 32 + blk + 1]
                        bcol = self.spk[:, SP_SSB + blk:SP_SSB + blk + 1]
                        u3 = pt[:, 0:W].rearrange("p (r w) -> p r w", w=cols)
                        y3 = y[:].rearrange("p (r w) -> p r w", w=cols)
                        P.op(DVE, lambda e, u=u, y=y, wcol=wcol: e.tensor_scalar(out=y[:], in0=u[:, 0:W], scalar1=wcol(1), scalar2=None,
                                                                                op0=ALU.mult), reads=[u.r, self.spk.r], writes=[y.r])
                        P.op(DVE, lambda e, u3=u3, y3=y3, wcol=wcol: e.scalar_tensor_tensor(
                            out=y3[:, :, 1:], in0=u3[:, :, :cols - 1], scalar=wcol(0), in1=y3[:, :, 1:], op0=ALU.mult, op1=ALU.add),
                            reads=[u.r, y.r, self.spk.r], writes=[y.r])
                        P.op(DVE, lambda e, u3=u3, y3=y3, wcol=wcol: e.scalar_tensor_tensor(
                            out=y3[:, :, :cols - 1], in0=u3[:, :, 1:], scalar=wcol(2), in1=y3[:, :, :cols - 1], op0=ALU.mult, op1=ALU.add),
                            reads=[u.r, y.r, self.spk.r], writes=[y.r])
                        if blk >= 24:
                            P.op(ACT, lambda e, y=y, blk=blk, bcol=bcol: e.activation(out=CT[:, blk - 24, :], in_=y[:], func=AF.Silu,
                                                                                     bias=bcol, scale=1.0),
                                 reads=[y.r, self.spk.r], writes=[CT.r])
                            continue
                        P.op(ACT, lambda e, y=y, sl_=sl_, bcol=bcol: e.activation(out=sl_[:], in_=y[:], func=AF.Silu, bias=bcol, scale=1.0),
                             reads=[y.r, self.spk.r], writes=[sl_.r])
                        p2 = self.ps()
                        for ch in range(2):
                            P.op(PE, lambda e, sl_=sl_, ch=ch, p2=p2: e.transpose(out=p2[:, ch * 128:(ch + 1) * 128],
                                                                                 in_=sl_[:, ch * 128:(ch + 1) * 128], identity=self.ident[:]),
                                 reads=[sl_.r, self.ident.r], writes=[p2.r], inc=(ch == 1))
                        p23 = p2[:, 0:256].rearrange("p (c n) -> p c n", n=128)
                        if blk < 16:
                            P.op(ACT, lambda e, blk=blk, p23=p23: e.copy(out=xs_tok[:, :, blk * 128:(blk + 1) * 128], in_=p23),
                                 reads=[p2.r], writes=[xs_tok.r])
                        else:
                            gb_ = blk - 16
                            P.op(ACT, lambda e, gb_=gb_, p23=p23: e.copy(out=B_tok[:, :, gb_ * 128:(gb_ + 1) * 128], in_=p23),
                                 reads=[p2.r], writes=[B_tok.r])
                            P.op(DVE, lambda e, gb_=gb_, sl_=sl_: e.tensor_copy(out=BT[:, gb_, :], in_=sl_[:]),
                                 reads=[sl_.r], writes=[BT.r])

            def zproj_pair():
                for s in range(8):
                    wbr, wb = load_w(w_in_bf, s * 256)
                    for ch in range(2):
                        pz = self.ps()
                        zz = zsL[ch]
                        for k in range(KC):
                            P.op(PE, lambda e, wb=wb, k=k, pz=pz, ch=ch: e.matmul(pz[:, 0:256], lhsT=hnb[:, k, ch * 128:(ch + 1) * 128],
                                                                                rhs=wb[:, k, :], start=(k == 0), stop=(k == KC - 1)),
                                 reads=[wbr.r, hnb.r], writes=[pz.r], inc=(k == KC - 1))
                        P.op(ACT, lambda e, s=s, pz=pz, zz=zz: e.activation(out=zz[:, s * 256:(s + 1) * 256], in_=pz[:, 0:256], func=AF.Silu),
                             reads=[pz.r], writes=[zz.r])

            def dir_common(ch, d):
                pc = self.ps()
                P.op(PE, lambda e, pc=pc: e.matmul(pc[:, 0:32], lhsT=tri[d][:], rhs=dA[:, ch, d * 32:(d + 1) * 32], start=True, stop=True),
                     reads=[tri[d].r, dA.r], writes=[pc.r], inc=False)
                P.op(PE, lambda e, pc=pc: e.matmul(pc[:, 32:64], lhsT=self.ones[:], rhs=dA[:, ch, d * 32:(d + 1) * 32], start=True, stop=True),
                     reads=[self.ones.r, dA.r], writes=[pc.r], inc=True)
                P.op(ACT, lambda e, pc=pc: e.copy(out=cumcL[d][:], in_=pc[:, 0:32]), reads=[pc.r], writes=[cumcL[d].r])
                P.op(ACT, lambda e, pc=pc: e.copy(out=totL[d][:], in_=pc[:, 32:64]), reads=[pc.r], writes=[totL[d].r])
                P.op(DVE, lambda e: e.tensor_tensor(
                    out=xdt[d][:].rearrange("p (h q) -> p h q", q=64), in0=xs_tok[:, ch, :].rearrange("p (h q) -> p h q", q=64),
                    in1=dt[:, ch, d * 32:(d + 1) * 32].unsqueeze(2).to_broadcast([128, 32, 64]), op=ALU.mult),
                    reads=[xs_tok.r, dt.r], writes=[xdt[d].r])

            def cumbc_group(ch, d, g):
                Rg = RgL[par[0] % 2]
                P.op(POOL, lambda e: e.tensor_tensor(
                    out=Rg[:].rearrange("p (h i) -> p h i", i=128), in0=tri[d][:].unsqueeze(1).to_broadcast([128, 4, 128]),
                    in1=dA[:, ch, d * 32 + g * 4:d * 32 + g * 4 + 4].unsqueeze(2).to_broadcast([128, 4, 128]), op=ALU.mult),
                    reads=[tri[d].r, dA.r], writes=[Rg.r])
                pb = self.ps()
                P.op(PE, lambda e, pb=pb: e.matmul(pb[:], lhsT=self.ones[:], rhs=Rg[:], start=True, stop=True),
                     reads=[self.ones.r, Rg.r], writes=[pb.r])
                return pb

            def diag_group(ch, d, g, pb):
                sg = segL[(par[0] + d) % 2]
                CBt = CBtL[par[0] % 2]
                Gm = [GmL[0][par[0] % 2], GmL[1][par[0] % 2]]
                P.op(DVE, lambda e, pb=pb: e.tensor_tensor(
                    out=sg[:].rearrange("p (h i) -> p h i", i=128), in0=pb[:].rearrange("p (h i) -> p h i", i=128),
                    in1=cumcL[d][:, g * 4:g * 4 + 4].unsqueeze(2).to_broadcast([128, 4, 128]), op=ALU.subtract),
                    reads=[pb.r, cumcL[d].r], writes=[sg.r])
                P.op(DVE, lambda e: e.tensor_tensor(
                    out=sg[:].rearrange("p (h i) -> p h i", i=128), in0=sg[:].rearrange("p (h i) -> p h i", i=128),
                    in1=mneg[d][:].unsqueeze(1).to_broadcast([128, 4, 128]), op=ALU.add),
                    reads=[sg.r, mneg[d].r], writes=[sg.r])
                P.op(ACT, lambda e: e.activation(out=sg[:], in_=sg[:], func=AF.Exp), reads=[sg.r], writes=[sg.r])
                P.op(DVE, lambda e: e.tensor_tensor(
                    out=Gm[d][:].rearrange("p (h i) -> p h i", i=128), in0=sg[:].rearrange("p (h i) -> p h i", i=128),
                    in1=CBt[:].unsqueeze(1).to_broadcast([128, 4, 128]), op=ALU.mult),
                    reads=[sg.r, CBt.r], writes=[Gm[d].r])

            def off_group(ch, d, g, pb):
                Ee = EeL[par[0] % 2]
                Cs = CsL[par[0] % 2]
                P.op(ACT, lambda e, pb=pb: e.activation(out=Ee[:], in_=pb[:], func=AF.Exp), reads=[pb.r], writes=[Ee.r])
                P.op(DVE, lambda e: e.tensor_tensor(
                    out=Cs[:].rearrange("p (h i) -> p h i", i=128), in0=Ee[:].rearrange("p (h i) -> p h i", i=128),
                    in1=CT[:, g, ch * 128:(ch + 1) * 128].unsqueeze(1).to_broadcast([128, 4, 128]), op=ALU.mult),
                    reads=[Ee.r, CT.r], writes=[Cs.r])

            def state_update(ch, d, accumulate_prefix=False):
                P.op(DVE, lambda e: e.tensor_tensor(out=wv[:], in0=totL[d][:], in1=cumcL[d][:], op=ALU.subtract),
                     reads=[totL[d].r, cumcL[d].r], writes=[wv.r])
                P.op(ACT, lambda e: e.activation(out=wv[:], in_=wv[:], func=AF.Exp), reads=[wv.r], writes=[wv.r])
                P.op(ACT, lambda e: e.activation(out=dec[:], in_=totL[d][:], func=AF.Exp), reads=[totL[d].r], writes=[dec.r])
                P.op(DVE, lambda e: e.tensor_tensor(
                    out=xw[:].rearrange("p (h q) -> p h q", q=64), in0=xdt[d][:].rearrange("p (h q) -> p h q", q=64),
                    in1=wv[:].unsqueeze(2).to_broadcast([128, 32, 64]), op=ALU.mult), reads=[xdt[d].r, wv.r], writes=[xw.r])
                for g in range(8):
                    pS = self.ps()
                    P.op(PE, lambda e, g=g, pS=pS: e.matmul(pS[:, 0:256], lhsT=B_tok[:, ch, g * 128:(g + 1) * 128],
                                                           rhs=xw[:, g * 256:(g + 1) * 256], start=True, stop=True),
                         reads=[B_tok.r, xw.r], writes=[pS.r])
                    hv = hT[d][:, g * 256:(g + 1) * 256].rearrange("p (h q) -> p h q", q=64)
                    if accumulate_prefix:
                        P.op(DVE, lambda e, g=g, pS=pS: e.tensor_tensor(
                            out=t1[:].rearrange("p (h q) -> p h q", q=64), in0=pS[:, 0:256].rearrange("p (h q) -> p h q", q=64),
                            in1=Pb[:, g * 4:g * 4 + 4].unsqueeze(2).to_broadcast([128, 4, 64]), op=ALU.mult),
                            reads=[pS.r, Pb.r], writes=[t1.r])
                        P.op(DVE, lambda e, g=g: e.tensor_tensor(out=hT[d][:, g * 256:(g + 1) * 256], in0=hT[d][:, g * 256:(g + 1) * 256],
                                                                 in1=t1[:], op=ALU.add), reads=[t1.r, hT[d].r], writes=[hT[d].r])
                    else:
                        P.op(DVE, lambda e, hv=hv, g=g: e.tensor_tensor(
                            out=hv, in0=hv, in1=dec[:, g * 4:g * 4 + 4].unsqueeze(2).to_broadcast([128, 4, 64]), op=ALU.mult),
                            reads=[hT[d].r, dec.r], writes=[hT[d].r])
                        P.op(DVE, lambda e, g=g, pS=pS: e.tensor_tensor(out=hT[d][:, g * 256:(g + 1) * 256],
                                                                        in0=hT[d][:, g * 256:(g + 1) * 256], in1=pS[:, 0:256], op=ALU.add),
                             reads=[pS.r, hT[d].r], writes=[hT[d].r])
                if accumulate_prefix:
                    P.op(DVE, lambda e: e.tensor_tensor(out=Pb[:], in0=Pb[:], in1=dec[:], op=ALU.mult), reads=[Pb.r, dec.r], writes=[Pb.r])
                else:
                    P.op(ACT, lambda e: e.copy(out=hTb[d][:], in_=hT[d][:]), reads=[hT[d].r], writes=[hTb[d].r])

            def scan_chunk(ch, mode, ychunk, tok0, ci):
                if mode == "S":
                    for d in range(2):
                        dir_common(ch, d)
                        state_update(ch, d, accumulate_prefix=(d == 1))
                    return
                dirs = (0, 1) if mode == "F" else (1,)
                od = 0 if mode == "F" else 1
                for d in dirs:
                    dir_common(ch, d)
                    if d == od:
                        pass
                if mode == "B":
                    P.dma(SP, lambda e: e.dma_start(out=ysb[:], in_=self.ypart_d[ychunk]), reads=[self.ypart_r[ychunk]],
                          writes=[ysb.r], sbuf=ysb.r)
                for g in range(8):
                    par[0] += 1
                    CBt = CBtL[par[0] % 2]
                    Gm = [GmL[0][par[0] % 2], GmL[1][par[0] % 2]]
                    Cs = CsL[par[0] % 2]
                    py = self.ps()
                    seq = []
                    if mode == "F":
                        pcb = self.ps()
                        P.op(PE, lambda e, g=g, pcb=pcb: e.matmul(pcb[:, 0:128], lhsT=BT[:, g, ch * 128:(ch + 1) * 128],
                                                                  rhs=CT[:, g, ch * 128:(ch + 1) * 128], start=True, stop=True),
                             reads=[BT.r, CT.r], writes=[pcb.r])
                        P.op(ACT, lambda e, pcb=pcb: e.copy(out=CBt[:], in_=pcb[:, 0:128]), reads=[pcb.r], writes=[CBt.r])
                    for d in dirs:
                        pb = cumbc_group(ch, d, g)
                        if mode == "F":
                            diag_group(ch, d, g, pb)
                            seq.append((Gm[d], xdt[d]))
                        if d == od:
                            off_group(ch, d, g, pb)
                            seq.append((Cs, None))
                    n = len(seq)
                    for hh in range(4):
                        h = g * 4 + hh
                        for qi, (lt, rx) in enumerate(seq):
                            if rx is None:
                                P.op(PE, lambda e, lt=lt, hh=hh, h=h, py=py, qi=qi: e.matmul(
                                    py[:, hh * 64:(hh + 1) * 64], lhsT=lt[:, hh * 128:(hh + 1) * 128],
                                    rhs=hTb[od][:, h * 64:(h + 1) * 64], start=(qi == 0), stop=(qi == n - 1)),
                                    reads=[lt.r, hTb[od].r], writes=[py.r], inc=(qi == n - 1 and hh == 3))
                            else:
                                P.op(PE, lambda e, lt=lt, rx=rx, hh=hh, h=h, py=py, qi=qi: e.matmul(
                                    py[:, hh * 64:(hh + 1) * 64], lhsT=lt[:, hh * 128:(hh + 1) * 128],
                                    rhs=rx[:, h * 64:(h + 1) * 64], start=(qi == 0), stop=(qi == n - 1)),
                                    reads=[lt.r, rx.r], writes=[py.r], inc=(qi == n - 1 and hh == 3))
                    if mode == "F":
                        P.op(DVE, lambda e, g=g: e.tensor_tensor(
                            out=t1[:].rearrange("p (h q) -> p h q", q=64),
                            in0=xs_tok[:, ch, g * 256:(g + 1) * 256].rearrange("p (h q) -> p h q", q=64),
                            in1=drow[:, g * 4:g * 4 + 4].unsqueeze(2).to_broadcast([128, 4, 64]), op=ALU.mult),
                            reads=[xs_tok.r, drow.r], writes=[t1.r])
                        P.op(DVE, lambda e, g=g, py=py: e.tensor_tensor(out=ysb[:, g * 256:(g + 1) * 256], in0=t1[:], in1=py[:, 0:256],
                                                                        op=ALU.add), reads=[t1.r, py.r], writes=[ysb.r])
                    else:
                        P.op(DVE, lambda e, g=g, py=py: e.tensor_tensor(out=ysb[:, g * 256:(g + 1) * 256], in0=ysb[:, g * 256:(g + 1) * 256],
                                                                        in1=py[:, 0:256], op=ALU.add), reads=[ysb.r, py.r], writes=[ysb.r])
                state_update(ch, od)
                if mode == "F":
                    P.dma(SP, lambda e: e.dma_start(out=self.ypart_d[ychunk], in_=ysb[:]), reads=[ysb.r],
                          writes=[self.ypart_r[ychunk]], sbuf=ysb.r, store=True)
                else:
                    zz = zsL[ch]
                    yT_ = ynTL[ch]
                    P.op(DVE, lambda e: e.tensor_tensor(out=ysb[:], in0=ysb[:], in1=zz[:, 0:2048], op=ALU.mult), reads=[ysb.r, zz.r], writes=[ysb.r])
                    P.op(DVE, lambda e: e.scalar_tensor_tensor(out=xdt[0][:], in0=ysb[:], scalar=1.0, in1=ysb[:], op0=ALU.mult, op1=ALU.mult,
                                                               accum_out=ssq[:]), reads=[ysb.r], writes=[xdt[0].r, ssq.r])
                    P.op(ACT, lambda e: e.activation(out=ssq[:], in_=ssq[:], func=AF.Sqrt, bias=self.epsb[:], scale=1.0 / 2048.0),
                         reads=[ssq.r, self.epsb.r], writes=[ssq.r])
                    P.op(DVE, lambda e: e.reciprocal(out=ssq[:], in_=ssq[:]), reads=[ssq.r], writes=[ssq.r])
                    P.op(DVE, lambda e: e.tensor_scalar(out=ysb[:], in0=ysb[:], scalar1=ssq[:], scalar2=None, op0=ALU.mult),
                         reads=[ysb.r, ssq.r], writes=[ysb.r])
                    for q4 in range(4):
                        pt = self.ps()
                        for j in range(4):
                            k = q4 * 4 + j
                            P.op(PE, lambda e, k=k, j=j, pt=pt: e.transpose(out=pt[:, j * 128:(j + 1) * 128], in_=ysb[:, k * 128:(k + 1) * 128],
                                                                            identity=self.ident[:]),
                                 reads=[ysb.r, self.ident.r], writes=[pt.r], inc=(j == 3))
                        for j in range(4):
                            k = q4 * 4 + j
                            P.op(ACT, lambda e, k=k, j=j, pt=pt: e.activation(
                                out=yT_[:, k, :], in_=pt[:, j * 128:(j + 1) * 128], func=AF.Copy,
                                scale=self.spk[:, SP_SNG + k:SP_SNG + k + 1]), reads=[pt.r, self.spk.r], writes=[yT_.r])

            def out_proj_pair(tok0, g, ci):
                for s in range(8):
                    wbr, wb = load_w(w_out_bf, s * 128, wide=False)
                    dc = s
                    pt = self.ps()
                    for ch in range(2):
                        yT_ = ynTL[ch]
                        for k in range(16):
                            P.op(PE, lambda e, wb=wb, k=k, pt=pt, ch=ch, yT_=yT_: e.matmul(
                                pt[:, ch * 128:(ch + 1) * 128], lhsT=wb[:, k, :], rhs=yT_[:, k, :], start=(k == 0), stop=(k == 15)),
                                reads=[wbr.r, yT_.r], writes=[pt.r], inc=(k == 15))
                    P.op(DVE, lambda e, dc=dc, pt=pt: e.scalar_tensor_tensor(
                        out=self.xT[:, dc, tok0:tok0 + W], in0=pt[:, 0:W], scalar=self.m_gt(1, 0, dc, ci),
                        in1=self.xT[:, dc, tok0:tok0 + W], op0=ALU.mult, op1=ALU.add),
                        reads=[pt.r, self.modT[1].r, self.xr[dc][g]], writes=[self.xr[dc][g]])

            def run_pair(pi, mode):
                tok0 = pi * W
                g = tok0 // GT
                sample = pi < 8
                ci = 0 if sample else 1
                cols = 64 if sample else 256
                if mode == "S":
                    for _ in range(2):
                        if tab1_items:
                            P.streams[POOL].append(tab1_items.pop(0))
                if mode == "S" or (mode == "F" and not sample):
                    frontend(tok0, cols, ci, g)
                    fe_io(pi, True)
                else:
                    if mode == "B":
                        frontend(tok0, cols, ci, g, only_norm=True)
                    fe_io(pi, False)
                if mode == "B":
                    zproj_pair()
                order = (0, 1) if mode != "B" else (1, 0)
                for ch in order:
                    scan_chunk(ch, mode, pi * 2 + ch, tok0, ci)
                if mode == "B":
                    out_proj_pair(tok0, g, ci)

            def set_state(d, src_ap=None):
                if src_ap is None:
                    P.op(DVE, lambda e: e.memset(hT[d][:], 0.0), writes=[hT[d].r])
                else:
                    P.dma(SP, lambda e: e.dma_start(out=hT[d][:], in_=src_ap), writes=[hT[d].r], sbuf=hT[d].r)
                P.op(ACT, lambda e: e.copy(out=hTb[d][:], in_=hT[d][:]), reads=[hT[d].r], writes=[hTb[d].r])

            set_state(0, self.stinit_d[0])
            set_state(1, None)
            P.op(DVE, lambda e: e.memset(Pb[:], 1.0), writes=[Pb.r])
            for pi in range(8):
                run_pair(pi, "S")
            P.dma(SP, lambda e: e.dma_start(out=ysb[:], in_=self.stinit_d[1]), writes=[ysb.r], sbuf=ysb.r)
            P.op(DVE, lambda e: e.tensor_tensor(out=ysb[:].rearrange("p (h q) -> p h q", q=64), in0=ysb[:].rearrange("p (h q) -> p h q", q=64),
                                                in1=Pb[:].unsqueeze(2).to_broadcast([128, 32, 64]), op=ALU.mult),
                 reads=[ysb.r, Pb.r], writes=[ysb.r])
            P.op(DVE, lambda e: e.tensor_tensor(out=hT[1][:], in0=hT[1][:], in1=ysb[:], op=ALU.add), reads=[hT[1].r, ysb.r], writes=[hT[1].r])
            for d in range(2):
                P.op(DVE, lambda e, d=d: e.tensor_scalar(out=hT[d][:], in0=hT[d][:], scalar1=flags[:, d:d + 1], scalar2=None, op0=ALU.mult),
                     reads=[hT[d].r, flags.r], writes=[hT[d].r])
                P.dma(SP, lambda e, d=d: e.dma_start(out=self.cc_in[:, d * 2048:(d + 1) * 2048], in_=hT[d][:]),
                      reads=[hT[d].r], writes=[self.cc_in_r], sbuf=hT[d].r, store=True)
            ccsem = self.st.enter_context(nc.semaphore("ccsem"))
            P.barrier()
            self.P.streams[POOL].append({"waits": [], "fn": lambda e: e.collective_compute(
                "AllReduce", ALU.add, replica_groups=[[0, 1], [2, 3], [4, 5], [6, 7]],
                ins=[self.cc_in], outs=[self.cc_out]).then_inc(ccsem, 1), "inc": None})
            for eng in (POOL, SP, DVE, ACT, PE):
                self.P.streams[eng].append({"waits": [], "fn": lambda e: e.wait_ge(ccsem, 1), "inc": None})
            inits = []
            for d in range(2):
                P.dma(SP, lambda e, d=d: e.dma_start(out=ysb[:], in_=self.cc_out[:, d * 2048:(d + 1) * 2048]), writes=[ysb.r], sbuf=ysb.r)
                P.dma(SP, lambda e, d=d: e.dma_start(out=hT[d][:], in_=self.stinit_d[d]), writes=[hT[d].r], sbuf=hT[d].r)
                P.op(DVE, lambda e, d=d: e.scalar_tensor_tensor(out=hT[d][:], in0=ysb[:], scalar=flags[:, 2 + d:3 + d], in1=hT[d][:],
                                                                 op0=ALU.mult, op1=ALU.add), reads=[ysb.r, flags.r, hT[d].r], writes=[hT[d].r])
                P.dma(SP, lambda e, d=d: e.dma_start(out=self.init_true[d], in_=hT[d][:]), reads=[hT[d].r], writes=[self.init_true_r[d]],
                      sbuf=hT[d].r, store=True)
            P.op(ACT, lambda e: e.copy(out=hTb[0][:], in_=hT[0][:]), reads=[hT[0].r], writes=[hTb[0].r])
            for pi in range(8):
                run_pair(pi, "F")
            for q in range(2):
                set_state(0, None)
                run_pair(8 + q, "F")
                P.dma(SP, lambda e, q=q: e.dma_start(out=self.st_out[q, 0], in_=hT[0][:]), reads=[hT[0].r], sbuf=hT[0].r, store=True)
            P.barrier()
            for q in (1, 0):
                set_state(1, None)
                run_pair(8 + q, "B")
                P.dma(SP, lambda e, q=q: e.dma_start(out=self.st_out[q, 1], in_=hT[1][:]), reads=[hT[1].r], sbuf=hT[1].r, store=True)
            P.dma(SP, lambda e: e.dma_start(out=hT[1][:], in_=self.init_true[1]), reads=[self.init_true_r[1]], writes=[hT[1].r], sbuf=hT[1].r)
            P.op(ACT, lambda e: e.copy(out=hTb[1][:], in_=hT[1][:]), reads=[hT[1].r], writes=[hTb[1].r])
            for pi in range(7, -1, -1):
                run_pair(pi, "B")
            P.barrier()

    def _cum_for(self, ch, d, tri, dA, cumc, tot):
        P = self.P
        pc = self.ps()
        P.op(PE, lambda e, pc=pc: e.matmul(pc[:, 0:32], lhsT=tri[d][:], rhs=dA[:, ch, d * 32:(d + 1) * 32], start=True, stop=True),
             reads=[tri[d].r, dA.r], writes=[pc.r], inc=False)
        P.op(PE, lambda e, pc=pc: e.matmul(pc[:, 32:64], lhsT=self.ones[:], rhs=dA[:, ch, d * 32:(d + 1) * 32], start=True, stop=True),
             reads=[self.ones.r, dA.r], writes=[pc.r], inc=True)
        P.op(ACT, lambda e, pc=pc: e.copy(out=cumc[:], in_=pc[:, 0:32]), reads=[pc.r], writes=[cumc.r])
        P.op(ACT, lambda e, pc=pc: e.copy(out=tot[:], in_=pc[:, 32:64]), reads=[pc.r], writes=[tot.r])

    def phase_final(self):
        nc, P = self.nc, self.P
        with contextlib.ExitStack() as ph:
            sb = self.sb
            yo = [sb(ph, "yo%d" % i, [128, KC, GT]) for i in range(2)]
            rstd = sb(ph, "rstd", [128, GT])
            sq = sb(ph, "sq", [128, 2, GT])
            tmp = sb(ph, "tmp", [128, 2, GT])
            yv = self.yT_d.rearrange("(c p) t -> p c t", p=128)
            for g in range(NG):
                y = yo[g % 2]
                self.norm_group(g, lambda c, ci: self.spk[:, SP_NFF + c:SP_NFF + c + 1], None, [y], rstd, sq, tmp)
                P.dma(SP, lambda e, y=y, g=g: e.dma_start(out=yv[:, :, g * GT:(g + 1) * GT], in_=y[:]),
                      reads=[y.r], sbuf=y.r, store=True)


def _pc(v):
    v = np.asarray(v, np.float32)
    return np.ascontiguousarray(v.reshape(-1, 128).T)


def make_in_maps(inp):
    f = lambda a: np.ascontiguousarray(np.asarray(a, np.float32))
    spk = np.zeros((128, NSP), np.float32)
    for i in range(2):
        spk[:, SP_NMG + 8 * i:SP_NMG + 8 * (i + 1)] = _pc(inp["norm_mix_g"][i])
        spk[:, SP_NFG + 8 * i:SP_NFG + 8 * (i + 1)] = _pc(inp["norm_ffn_g"][i])
        spk[:, SP_ADAB + 48 * i:SP_ADAB + 48 * (i + 1)] = _pc(inp["ada_b"][i])
    spk[:, SP_NFF:SP_NFF + 8] = _pc(inp["norm_f_g"])
    for tap in range(3):
        spk[:, SP_SCW + 8 * tap:SP_SCW + 8 * (tap + 1)] = _pc(inp["sc_conv_w"][0][tap])
        spk[:, SP_SSW + 32 * tap:SP_SSW + 32 * (tap + 1)] = _pc(inp["ssd_conv_w"][0][tap])
    spk[:, SP_SSB:SP_SSB + 32] = _pc(inp["ssd_conv_b"][0])
    spk[:, SP_SNG:SP_SNG + 16] = _pc(inp["ssd_norm_g"][0])
    rowp = np.concatenate([np.asarray(inp["ssd_dt_bias"][0], np.float32).reshape(-1),
                           np.asarray(inp["ssd_a_log"][0], np.float32).reshape(-1),
                           np.asarray(inp["ssd_d"][0], np.float32).reshape(-1)])
    ii = np.arange(128)
    trif = (ii[:, None] <= ii[None, :]).astype(np.float32)
    trib = (ii[:, None] >= ii[None, :]).astype(np.float32)
    shared = {
        "rowp": np.ascontiguousarray(np.broadcast_to(rowp, (128, ROWP_N))),
        "trif": trif, "trib": trib,
        "mnegf": (trif - 1.0) * 30000.0, "mnegb": (trib - 1.0) * 30000.0,
        "ssd_w_in": f(inp["ssd_w_in"]), "ssd_w_out": f(inp["ssd_w_out"]),
        "spk": spk,
        "ident": np.eye(128, dtype=np.float32),
        "ones": np.ones((128, 128), np.float32),
        "iota": np.ascontiguousarray(np.broadcast_to(np.arange(256, dtype=np.float32), (128, 256))),
        "ada_w": f(inp["ada_w"]),
        "sc_w_in": f(inp["sc_w_in"]),
        "sc_w_out": f(inp["sc_w_out"]),
        "peer_wq": f(inp["peer_wq"]),
        "peer_keys": f(inp["peer_keys"]),
        "peer_uv": np.ascontiguousarray(np.concatenate(
            [np.asarray(inp["peer_u"], np.float32), np.asarray(inp["peer_v"], np.float32)], axis=-1)),
    }
    xs = np.asarray(inp["x_sample"], np.float32)
    xp = np.asarray(inp["x_prompt"], np.float32)
    maps = []
    for r in range(8):
        b, hf = r // 2, r % 2
        tok = np.concatenate([xs[b, hf * 2048:(hf + 1) * 2048], xp[2 * r], xp[2 * r + 1]], axis=0)
        condT = np.stack([_pc(inp["c"][b]), _pc(inp["c_ctx"])], axis=2)
        m = dict(shared)
        m["xT"] = np.ascontiguousarray(tok.T)
        m["condT"] = np.ascontiguousarray(condT)
        st = np.asarray(inp["state_ssm"], np.float32)[b, 0]
        stinit = np.zeros((2, 128, 2048), np.float32)
        stinit[hf] = st[hf].reshape(2048, 128).T
        m["stinit"] = stinit
        fl = np.array([1, 0, 0, 1] if hf == 0 else [0, 1, 1, 0], np.float32)
        m["flags"] = np.ascontiguousarray(np.broadcast_to(fl, (128, 4)))
        maps.append(m)
    return maps


_NC_CACHE = {}


def kernel(**inputs):
    key = "full"
    if key not in _NC_CACHE:
        _NC_CACHE[key] = Builder().build()
    nc = _NC_CACHE[key]
    maps = make_in_maps(inputs)
    res = run_bass_kernel_spmd(nc, maps, core_ids=list(range(8)))
    y_prompt = np.zeros((16, 256, D), np.float32)
    y_sample = np.zeros((4, 4096, D), np.float32)
    states = np.zeros((16, 1, 2, 32, 64, 128), np.float32)
    for r in range(8):
        yT = np.asarray(res.results[r]["yT"])
        y = yT.T
        b, hf = r // 2, r % 2
        y_sample[b, hf * 2048:(hf + 1) * 2048] = y[0:2048]
        y_prompt[2 * r] = y[2048:2304]
        y_prompt[2 * r + 1] = y[2304:2560]
        if "st_out" in res.results[r]:
            so = np.asarray(res.results[r]["st_out"])
            for q in range(2):
                for d in range(2):
                    states[2 * r + q, 0, d] = so[q, d].T.reshape(32, 64, 128)
    return (y_prompt, y_sample, states)
```

```python
import contextlib
import os
import types

import numpy as np
import concourse.bass as bass
import concourse.mybir as mybir
from concourse.bass_utils import run_bass_kernel_spmd

F32 = mybir.dt.float32
BF16 = mybir.dt.bfloat16
I32 = mybir.dt.int32
U32 = mybir.dt.uint32
ALU = mybir.AluOpType
AF = mybir.ActivationFunctionType

PE, ACT, DVE, POOL, SP = "pe", "act", "dve", "pool", "sp"
EPOCH = 1000000

D = 1024
KC = 8
T = 2560
NG = 5
GT = 512
EPS = 1e-6
NEG = -1.0e30


def _freeze(fn):
    if fn is None or fn.__closure__ is None:
        return fn
    cells = []
    for c in fn.__closure__:
        try:
            cells.append(types.CellType(c.cell_contents))
        except ValueError:
            cells.append(c)
    return types.FunctionType(fn.__code__, fn.__globals__, fn.__name__, fn.__defaults__, tuple(cells))


class Res:
    __slots__ = ("name", "semname", "w", "w_eng", "rd", "wsem", "wcnt", "rsem", "rcnt")

    def __init__(self, name, semname=None):
        self.name = name
        self.semname = semname or name
        self.w = None
        self.w_eng = None
        self.rd = {}
        self.wsem = None
        self.wcnt = 0
        self.rsem = None
        self.rcnt = 0


class Prog:
    def __init__(self, nc, stack):
        self.nc = nc
        self.stack = stack
        self.streams = {e: [] for e in (PE, ACT, DVE, POOL, SP)}
        self.cnt = {e: 0 for e in self.streams}
        self.pending = {e: [] for e in self.streams}
        self.known = {e: {} for e in self.streams}
        self.sems = {}
        self.nsem = 0
        self.final_tokens = {}
        self.all_dma_tokens = {}
        self.dma_cnt = {}

    def _sem(self, key):
        if key not in self.sems:
            self.sems[key] = self.stack.enter_context(self.nc.semaphore("s%d" % self.nsem))
            self.nsem += 1
        return self.sems[key]

    def _eng_token(self, eng, count):
        ep = (count - 1) // EPOCH
        return (("E", eng, ep), count - ep * EPOCH)

    def _deps(self, eng, reads, writes, dma_wsem_key=None):
        need = {}

        def add(tok):
            if tok is None:
                return
            k, v = tok
            if need.get(k, 0) < v:
                need[k] = v

        for r in reads:
            if r.w is not None:
                add(r.w)
        for w in writes:
            if w.w is not None:
                if dma_wsem_key is not None and w.w[0] == dma_wsem_key:
                    pass
                elif w.w_eng == PE and eng == PE:
                    pass
                else:
                    add(w.w)
            for k, (v, e) in w.rd.items():
                if e == eng and e in (PE, ACT, DVE):
                    continue
                add((k, v))
        waits = []
        kn = self.known[eng]
        for k, v in need.items():
            if kn.get(k, 0) < v:
                kn[k] = v
                waits.append((k, v))
        return waits

    def op(self, eng, fn, reads=(), writes=(), inc=True):
        waits = self._deps(eng, reads, writes)
        item = {"waits": waits, "fn": _freeze(fn), "inc": None}
        self.streams[eng].append(item)
        if inc:
            self.cnt[eng] += 1
            tok = self._eng_token(eng, self.cnt[eng])
            item["inc"] = (tok[0], 1)
            self._sem(tok[0])
            for (rs, ws) in self.pending[eng] + [(reads, writes)]:
                for r in rs:
                    r.rd[tok[0]] = (tok[1], eng)
                for w in ws:
                    w.w = tok
                    w.w_eng = eng
                    w.rd = {}
            self.pending[eng] = []
        else:
            self.pending[eng].append((reads, writes))
        return item

    def dma(self, eng, fn, reads=(), writes=(), sbuf=None, store=False):
        assert not self.pending[eng]
        key = ("R" if store else "W", sbuf.semname)
        if store:
            sbuf.rsem = key
        else:
            sbuf.wsem = key
        self.dma_cnt[key] = self.dma_cnt.get(key, 0) + 1
        val = 16 * self.dma_cnt[key]
        waits = self._deps(eng, reads, writes, dma_wsem_key=None if store else key)
        self._sem(key)
        item = {"waits": waits, "fn": _freeze(fn), "inc": (key, 16)}
        self.streams[eng].append(item)
        tok = (key, val)
        for r in reads:
            r.rd[key] = (val, "dma")
        for w in writes:
            w.w = tok
            w.w_eng = "dma"
            w.rd = {}
        self.all_dma_tokens[key] = val
        if store:
            self.final_tokens[key] = val
        return item

    def barrier(self):
        toks = dict(self.all_dma_tokens)
        for e in self.streams:
            assert not self.pending[e]
            if self.cnt[e] > 0:
                k, v = self._eng_token(e, self.cnt[e])
                toks[k] = v
        for e in self.streams:
            waits = []
            kn = self.known[e]
            for k, v in toks.items():
                if k == ("E", e, (self.cnt[e] - 1) // EPOCH) and e != POOL:
                    pass
                if kn.get(k, 0) < v:
                    kn[k] = v
                    waits.append((k, v))
            self.streams[e].append({"waits": waits, "fn": None, "inc": None})

    def emit(self):
        nc = self.nc
        for e in self.streams:
            assert not self.pending[e], e
        fin = [(k, v) for k, v in self.final_tokens.items()]
        self.streams[SP].append({"waits": fin, "fn": None, "inc": None})
        sems = self.sems
        streams = self.streams

        def run(engobj, items):
            for it in items:
                for (k, v) in it["waits"]:
                    engobj.wait_ge(sems[k], v)
                if it["fn"] is None:
                    continue
                try:
                    ins = it["fn"](engobj)
                except Exception:
                    print("EMIT FAIL at item", items.index(it), "of", len(items), it["waits"], it["inc"])
                    raise
                if it["inc"] is not None:
                    ins.then_inc(sems[it["inc"][0]], it["inc"][1])

        with nc.Block() as block:
            @block.tensor
            def _(e):
                run(e, streams[PE])

            @block.scalar
            def _(e):
                run(e, streams[ACT])

            @block.vector
            def _(e):
                run(e, streams[DVE])

            @block.gpsimd
            def _(e):
                run(e, streams[POOL])

            @block.sync
            def _(e):
                run(e, streams[SP])


class Buf:
    def __init__(self, t, name, base=None):
        self.t = t
        self.r = Res(name, base)

    def __getitem__(self, k):
        return self.t[k]


SP_NMG = 0
SP_NFG = 16
SP_NFF = 32
SP_ADAB = 40
SP_SCW = 136
SP_SSW = 160
SP_SSB = 256
SP_SNG = 288
NSP = 304
RP_DTB = 0
RP_ALOG = 64
RP_D = 128
ROWP_N = 192


class Builder:
    def __init__(self, do_ssd=True, do_peer=True, n_slots=128):
        self.do_ssd = do_ssd
        self.do_peer = do_peer
        self.n_slots = n_slots
        self.nc = bass.Bass("TRN2", target_bir_lowering=False)
        self.uid = 0

    def dram_in(self, name, shape, dt=F32):
        return self.nc.dram_tensor(name, list(shape), dt, kind="ExternalInput").ap()

    def dram_out(self, name, shape, dt=F32):
        return self.nc.dram_tensor(name, list(shape), dt, kind="ExternalOutput").ap()

    def sb(self, stack, name, shape, dt=F32):
        self.uid += 1
        nm = "%s_%d" % (name, self.uid)
        return Buf(stack.enter_context(self.nc.sbuf_tensor(nm, list(shape), dt)), nm, name)

    def ps(self):
        b = self.psb[self.psi % len(self.psb)]
        self.psi += 1
        return b

    def build(self):
        nc = self.nc
        di = self.dram_in
        self.xT_d = di("xT", [D, T])
        self.condT_d = di("condT", [128, KC, 2])
        self.spk_d = di("spk", [128, NSP])
        self.ident_d = di("ident", [128, 128])
        self.ones_d = di("ones", [128, 128])
        self.iota_d = di("iota", [128, 256])
        self.ada_w = di("ada_w", [2, D, 6 * D])
        self.sc_w_in = di("sc_w_in", [1, D, 3 * D])
        self.sc_w_out = di("sc_w_out", [1, D, D])
        self.peer_wq = di("peer_wq", [2, D, 2048])
        self.peer_keys = di("peer_keys", [2, 8, 2, 128, 128])
        self.peer_uv = di("peer_uv", [2, 16384, 2 * D])
        self.ssd_w_in = di("ssd_w_in", [1, D, 6208])
        self.ssd_w_out = di("ssd_w_out", [1, 2048, D])
        self.rowp_d = di("rowp", [128, ROWP_N])
        self.tri_d = [di("trif", [128, 128]), di("trib", [128, 128])]
        self.mneg_d = [di("mnegf", [128, 128]), di("mnegb", [128, 128])]
        self.flags_d = di("flags", [128, 4])
        self.stinit_d = di("stinit", [2, 128, 2048])
        self.st_out = self.dram_out("st_out", [2, 2, 128, 2048])
        self.ypart_d = nc.dram_tensor("ypart", [20, 128, 2048], F32).ap()
        self.ypart_r = [Res("ypart%d" % i) for i in range(20)]
        self.w_in_bf = nc.dram_tensor("w_in_bf", [D, 6208], BF16).ap()
        self.w_out_bf = nc.dram_tensor("w_out_bf", [2048, D], BF16).ap()
        fe_shapes = ((4096, BF16), (2048, BF16), (2048, BF16), (2048, BF16), (128, F32), (128, F32))
        self.fe_d = [nc.dram_tensor("fe%d" % k_, [10, 128, n_], dt_).ap() for k_, (n_, dt_) in enumerate(fe_shapes)]
        self.fe_r = [[Res("fe%d_%d" % (k_, p_)) for p_ in range(10)] for k_ in range(6)]
        self.cc_in = nc.dram_tensor("cc_in", [128, 4096], F32).ap()
        self.cc_out = nc.dram_tensor("cc_out", [128, 4096], F32).ap()
        self.cc_in_r = Res("cc_in")
        self.init_true = [nc.dram_tensor("init_true%d" % d, [128, 2048], F32).ap() for d in range(2)]
        self.init_true_r = [Res("init_true%d" % d) for d in range(2)]
        self.peer_uv_flat = self.peer_uv.rearrange("l n d -> (l n) d")
        self.uv_bf = nc.dram_tensor("uv_bf", [2 * 16384, 2 * D], BF16).ap()
        self.yT_d = self.dram_out("yT", [D, T])

        with contextlib.ExitStack() as st:
            self.st = st
            P = self.P = Prog(nc, st)
            self.psb = [Buf(st.enter_context(nc.psum_tensor("ps%d" % i, [128, 512], F32)), "ps%d" % i)
                        for i in range(8)]
            self.psi = 0
            sb = self.sb
            self.xT = sb(st, "xT", [128, KC, T])
            self.xr = [[Res("x_%d_%d" % (c, g), "xload") for g in range(NG)] for c in range(KC)]
            self.ident = sb(st, "ident", [128, 128])
            self.ones = sb(st, "ones", [128, 128])
            self.iota = sb(st, "iota", [128, 256])
            self.spk = sb(st, "spk", [128, NSP])
            self.condT = sb(st, "condT", [128, KC, 2])
            self.modT = [sb(st, "modT%d" % i, [128, 48, 2]) for i in range(2)]
            self.gs = sb(st, "gs", [128, 2, 2, KC, 2])
            self.epsb = sb(st, "epsb", [128, 1])

            xv = self.xT_d.rearrange("(c p) t -> p c t", p=128)
            for c in range(KC):
                for g in range(NG):
                    P.dma(SP, lambda e, c=c, g=g: e.dma_start(out=self.xT[:, c, g * GT:(g + 1) * GT],
                                                               in_=xv[:, c, g * GT:(g + 1) * GT]),
                          writes=[self.xr[c][g]], sbuf=self.xr[c][g])
            for c in range(KC):
                for g in range(NG):
                    self.xr[c][g].w = (("W", "xload"), 16 * KC * NG)
            for (buf, src) in ((self.ident, self.ident_d), (self.ones, self.ones_d), (self.iota, self.iota_d),
                               (self.spk, self.spk_d), (self.condT, self.condT_d)):
                P.dma(SP, lambda e, buf=buf, src=src: e.dma_start(out=buf[:], in_=src),
                      writes=[buf.r], sbuf=buf.r)
            P.op(DVE, lambda e: e.memset(self.epsb[:], EPS), writes=[self.epsb.r])

            self.phase_tabcast()
            self.phase_ada()
            self.phase_conv_mixer()
            if self.do_peer:
                self.phase_peer(0)
            if self.do_ssd:
                self.phase_ssd()
            if self.do_peer:
                self.phase_peer(1)
            self.phase_final()
            P.emit()
        return nc

    def phase_tabcast(self):
        P = self.P
        key = ("X", "tabl0")
        P._sem(key)
        NCH = 16
        for k_ in range(NCH):
            r0 = k_ * (16384 // NCH)
            r1 = r0 + 16384 // NCH
            P.streams[POOL].append({"waits": [], "inc": (key, 16), "fn": _freeze(
                lambda e, r0=r0, r1=r1: e.dma_start(out=self.uv_bf[r0:r1, :], in_=self.peer_uv_flat[r0:r1, :]))})
        self.tab0_tok = (key, 16 * NCH)

    def phase_ada(self):
        nc, P = self.nc, self.P
        with contextlib.ExitStack() as ph:
            sct = self.sb(ph, "sct", [128, KC, 2])
            wst = [self.sb(ph, "adaw%d" % i, [128, KC, 512]) for i in range(2)]
            P.op(ACT, lambda e: e.activation(out=sct[:], in_=self.condT[:], func=AF.Silu),
                 reads=[self.condT.r], writes=[sct.r])
            n = 0
            for i in range(2):
                wv = self.ada_w[i].rearrange("(k p) c -> p k c", p=128)
                pst = self.ps()
                for s in range(12):
                    w = wst[n % 2]
                    n += 1
                    P.dma(SP, lambda e, w=w, s=s, wv=wv: e.dma_start(out=w[:], in_=wv[:, :, s * 512:(s + 1) * 512]),
                          writes=[w.r], sbuf=w.r)
                    for cb in range(4):
                        dc = s * 4 + cb
                        for k in range(KC):
                            P.op(PE, lambda e, w=w, cb=cb, k=k, dc=dc, pst=pst: e.matmul(
                                pst[:, dc * 2:dc * 2 + 2], lhsT=w[:, k, cb * 128:(cb + 1) * 128],
                                rhs=sct[:, k, :], start=(k == 0), stop=(k == KC - 1)),
                                reads=[w.r, sct.r], writes=[pst.r], inc=(k == KC - 1))
                mod = self.modT[i]
                P.op(DVE, lambda e, mod=mod, pst=pst, i=i: e.tensor_tensor(
                    out=mod[:], in0=pst[:, 0:96].rearrange("p (c j) -> p c j", j=2),
                    in1=self.spk[:, SP_ADAB + 48 * i:SP_ADAB + 48 * (i + 1)].unsqueeze(2).to_broadcast([128, 48, 2]),
                    op=ALU.add), reads=[pst.r, self.spk.r], writes=[mod.r])
                for wh, (goff, sclo) in enumerate(((SP_NMG, 8), (SP_NFG, 32))):
                    P.op(DVE, lambda e, mod=mod, i=i, wh=wh, goff=goff, sclo=sclo: e.scalar_tensor_tensor(
                        out=self.gs[:, i, wh], in0=mod[:, sclo:sclo + 8, :], scalar=1.0,
                        in1=self.spk[:, goff + 8 * i:goff + 8 * (i + 1)].unsqueeze(2).to_broadcast([128, 8, 2]),
                        op0=ALU.add, op1=ALU.mult), reads=[mod.r, self.spk.r], writes=[self.gs.r])
            P.barrier()

    def m_sh(self, i, wh, c, ci):
        return self.modT[i][:, (0 if wh == 0 else 24) + c, ci:ci + 1]

    def m_gs(self, i, wh, c, ci):
        return self.gs[:, i, wh, c, ci:ci + 1]

    def m_gt(self, i, wh, c, ci):
        return self.modT[i][:, (16 if wh == 0 else 40) + c, ci:ci + 1]

    def norm_group(self, g, gs_fn, sh_fn, outs, rstd, sq, tmp):
        P = self.P
        ci = 0 if g < 4 else 1
        pss = self.ps()
        sl = slice(g * GT, (g + 1) * GT)
        for c in range(KC):
            P.op(ACT, lambda e, c=c: e.activation(out=sq[:, c % 2, :], in_=self.xT[:, c, sl], func=AF.Square),
                 reads=[self.xr[c][g]], writes=[sq.r])
            P.op(PE, lambda e, c=c: e.matmul(pss[:], lhsT=self.ones[:], rhs=sq[:, c % 2, :],
                                             start=(c == 0), stop=(c == KC - 1)),
                 reads=[self.ones.r, sq.r], writes=[pss.r], inc=True)
        P.op(ACT, lambda e: e.activation(out=rstd[:], in_=pss[:], func=AF.Sqrt, bias=self.epsb[:], scale=1.0 / D),
             reads=[pss.r, self.epsb.r], writes=[rstd.r])
        P.op(DVE, lambda e: e.reciprocal(out=rstd[:], in_=rstd[:]), reads=[rstd.r], writes=[rstd.r])
        for c in range(KC):
            P.op(DVE, lambda e, c=c: e.tensor_tensor(out=tmp[:, c % 2, :], in0=self.xT[:, c, sl], in1=rstd[:],
                                                    op=ALU.mult),
                 reads=[self.xr[c][g], rstd.r], writes=[tmp.r])
            for ob in outs:
                if sh_fn is None:
                    P.op(ACT, lambda e, c=c, ob=ob: e.activation(out=ob[:, c, :], in_=tmp[:, c % 2, :],
                                                                 func=AF.Identity, scale=gs_fn(c, ci)),
                         reads=[tmp.r, self.gs.r, self.spk.r], writes=[ob.r])
                else:
                    P.op(ACT, lambda e, c=c, ob=ob: e.activation(out=ob[:, c, :], in_=tmp[:, c % 2, :],
                                                                 func=AF.Identity, scale=gs_fn(c, ci),
                                                                 bias=sh_fn(c, ci)),
                         reads=[tmp.r, self.gs.r, self.modT[0].r, self.modT[1].r], writes=[ob.r])

    def phase_conv_mixer(self):
        nc, P = self.nc, self.P
        with contextlib.ExitStack() as ph:
            sb = self.sb
            hnb = sb(ph, "hnb", [128, KC, GT], BF16)
            rstd = sb(ph, "rstd", [128, GT])
            sq = sb(ph, "sq", [128, 2, GT])
            tmp = sb(ph, "tmp", [128, 2, GT])
            wst = [sb(ph, "wst%d" % i, [128, KC, 512]) for i in range(2)]
            wbf = [sb(ph, "wbf%d" % i, [128, KC, 512], BF16) for i in range(2)]
            bg = sb(ph, "bg", [128, KC, GT])
            cg = [sb(ph, "cg%d" % k, [128, GT]) for k in range(KC)]
            yv = [sb(ph, "yv%d" % k, [128, GT]) for k in range(2)]
            mm = sb(ph, "mm", [128, KC, GT], BF16)
            bgr = [Res("bg%d" % k) for k in range(KC)]
            mmr = [Res("mm%d" % k) for k in range(KC)]
            w_in = self.sc_w_in[0].rearrange("(k p) c -> p k c", p=128)
            w_out = self.sc_w_out[0].rearrange("(k p) c -> p k c", p=128)
            nload = 0

            def load_w(view, c0):
                nonlocal nload
                ws, wb = wst[nload % 2], wbf[nload % 2]
                ce = (ACT, DVE)[nload % 2]
                nload += 1
                P.dma(SP, lambda e: e.dma_start(out=ws[:], in_=view[:, :, c0:c0 + 512]), writes=[ws.r], sbuf=ws.r)
                if ce == ACT:
                    P.op(ACT, lambda e: e.copy(out=wb[:], in_=ws[:]), reads=[ws.r], writes=[wb.r])
                else:
                    P.op(DVE, lambda e: e.tensor_copy(out=wb[:], in_=ws[:]), reads=[ws.r], writes=[wb.r])
                return wb

            for g in range(NG):
                ci = 0 if g < 4 else 1
                sl = slice(g * GT, (g + 1) * GT)
                rows, cols = (8, 64) if g < 4 else (2, 256)
                self.norm_group(g, lambda c, ci: self.m_gs(0, 0, c, ci), lambda c, ci: self.m_sh(0, 0, c, ci),
                                [hnb], rstd, sq, tmp)
                for s in range(6):
                    wb = load_w(w_in, s * 512)
                    for cb in range(4):
                        blk = s * 4 + cb
                        pt = self.ps()
                        for k in range(KC):
                            P.op(PE, lambda e, wb=wb, cb=cb, k=k, pt=pt: e.matmul(
                                pt[:], lhsT=wb[:, k, cb * 128:(cb + 1) * 128], rhs=hnb[:, k, :],
                                start=(k == 0), stop=(k == KC - 1)),
                                reads=[wb.r, hnb.r], writes=[pt.r], inc=(k == KC - 1))
                        if blk < 8:
                            P.op(ACT, lambda e, blk=blk, pt=pt: e.copy(out=bg[:, blk, :], in_=pt[:]),
                                 reads=[pt.r], writes=[bgr[blk]])
                        elif blk < 16:
                            cc = cg[blk - 8]
                            P.op(ACT, lambda e, cc=cc, pt=pt: e.copy(out=cc[:], in_=pt[:]),
                                 reads=[pt.r], writes=[cc.r])
                        else:
                            k8 = blk - 16
                            cc = cg[k8]
                            y = yv[k8 % 2]
                            P.op(DVE, lambda e, cc=cc, pt=pt: e.tensor_tensor(out=cc[:], in0=cc[:], in1=pt[:],
                                                                              op=ALU.mult),
                                 reads=[cc.r, pt.r], writes=[cc.r])
                            wcol = lambda tap, k8=k8: self.spk[:, SP_SCW + tap * 8 + k8:SP_SCW + tap * 8 + k8 + 1]
                            u3 = cc[:].rearrange("p (r w) -> p r w", w=cols)
                            y3 = y[:].rearrange("p (r w) -> p r w", w=cols)
                            P.op(DVE, lambda e, cc=cc, y=y, wcol=wcol: e.tensor_scalar(
                                out=y[:], in0=cc[:], scalar1=wcol(1), scalar2=None, op0=ALU.mult),
                                reads=[cc.r, self.spk.r], writes=[y.r])
                            P.op(DVE, lambda e, u3=u3, y3=y3, wcol=wcol: e.scalar_tensor_tensor(
                                out=y3[:, :, 1:], in0=u3[:, :, :cols - 1], scalar=wcol(0), in1=y3[:, :, 1:],
                                op0=ALU.mult, op1=ALU.add), reads=[cc.r, y.r, self.spk.r], writes=[y.r])
                            P.op(DVE, lambda e, u3=u3, y3=y3, wcol=wcol: e.scalar_tensor_tensor(
                                out=y3[:, :, :cols - 1], in0=u3[:, :, 1:], scalar=wcol(2), in1=y3[:, :, :cols - 1],
                                op0=ALU.mult, op1=ALU.add), reads=[cc.r, y.r, self.spk.r], writes=[y.r])
                            P.op(DVE, lambda e, k8=k8, y=y: e.tensor_tensor(out=mm[:, k8, :], in0=bg[:, k8, :],
                                                                          in1=y[:], op=ALU.mult),
                                 reads=[bgr[k8], y.r], writes=[mmr[k8]])
                for s in range(2):
                    wb = load_w(w_out, s * 512)
                    for cb in range(4):
                        dc = s * 4 + cb
                        pt = self.ps()
                        for k in range(KC):
                            P.op(PE, lambda e, wb=wb, cb=cb, k=k, pt=pt: e.matmul(
                                pt[:], lhsT=wb[:, k, cb * 128:(cb + 1) * 128], rhs=mm[:, k, :],
                                start=(k == 0), stop=(k == KC - 1)),
                                reads=[wb.r] + mmr, writes=[pt.r], inc=(k == KC - 1))
                        P.op(DVE, lambda e, dc=dc, pt=pt, ci=ci: e.scalar_tensor_tensor(
                            out=self.xT[:, dc, sl], in0=pt[:], scalar=self.m_gt(0, 0, dc, ci), in1=self.xT[:, dc, sl],
                            op0=ALU.mult, op1=ALU.add),
                            reads=[pt.r, self.modT[0].r, self.xr[dc][g]], writes=[self.xr[dc][g]])
            P.barrier()

    def phase_peer(self, li):
        nc, P = self.nc, self.P
        if li == 1 and getattr(self, "tab1_tok", None) is not None:
            P.streams[POOL].append({"waits": [self.tab1_tok], "fn": None, "inc": None})
        if li == 0:
            P.streams[POOL].append({"waits": [self.tab0_tok], "fn": None, "inc": None})
        NS = self.n_slots
        with contextlib.ExitStack() as ph:
            sb = self.sb
            NB = 10
            gb = [sb(ph, "gb%d" % i, [128, 2 * D], BF16) for i in range(NB)]
            hnf = sb(ph, "hnf", [128, KC, GT])
            rstd = sb(ph, "rstd", [128, GT])
            sq = sb(ph, "sq", [128, 2, GT])
            tmp = sb(ph, "tmp", [128, 2, GT])
            keyT = sb(ph, "keyT", [128, 16, 128])
            wqb = [sb(ph, "wqb%d" % i, [128, KC, 128]) for i in range(2)]
            qT = [sb(ph, "qT%d" % i, [128, GT]) for i in range(2)]
            scs = [sb(ph, "scs%d" % i, [128, 4, 128]) for i in range(2)]
            mrtL = [sb(ph, "mrt%d" % i, [128, 128]) for i in range(4)]
            vv_r = [Res("vv%d" % i) for i in range(4)]
            iu_r = [Res("iu%d" % i) for i in range(4)]
            vv = sb(ph, "vv", [128, 4, 16, 16])
            iu = sb(ph, "iu", [128, 4, 16, 16], U32)
            i_f = sb(ph, "i_f", [128, 16, 16])
            i1s = sb(ph, "i1s", [128, 8, 16])
            pos = sb(ph, "pos", [128, 8, 16], U32)
            posf = sb(ph, "posf", [128, 8, 16])
            pa_ = sb(ph, "pa_", [128, 8, 16])
            pb_ = sb(ph, "pb_", [128, 8, 16])
            idx1 = sb(ph, "idx1", [128, 8, 16])
            idx2 = sb(ph, "idx2", [128, 8, 16])
            thr16 = sb(ph, "thr16", [128, 16])
            P.op(DVE, lambda e: e.tensor_scalar(out=thr16[:], in0=self.iota[:, 0:16], scalar1=16.0, scalar2=16.0,
                                               op0=ALU.mult, op1=ALU.add), reads=[self.iota.r], writes=[thr16.r])
            P.op(DVE, lambda e: e.memset(thr16[:, 15:16], 1.0e9), writes=[thr16.r])
            cand = sb(ph, "cand", [128, 8, 256])
            cidx = cand
            cwork = sb(ph, "cwork", [128, 256])
            sc16 = sb(ph, "sc16", [128, 8, 16])
            idxf = sb(ph, "idxf", [128, 128])
            idxiL = [sb(ph, "idxi%d" % i, [128, 128], I32) for i in range(2)]
            ee = sb(ph, "ee", [128, 8, 16])
            ggL = [sb(ph, "gg%d" % i, [128, 128]) for i in range(2)]
            act_r = [Res("act%d" % j) for j in range(128)]
            coef_r = [Res("coef%d" % j) for j in range(128)]
            negm = sb(ph, "negm", [128, 8])
            esum = sb(ph, "esum", [128, 8])
            hn2 = sb(ph, "hn2", [128, D])
            act = sb(ph, "act", [128, 128])
            coef = sb(ph, "coef", [128, 128])
            junk2v = cand[:, 0:4, :].rearrange("p a b -> p (a b)")
            dg = [sb(ph, "dg%d" % i, [128, 128], BF16) for i in range(3)]
            zer = sb(ph, "zer", [128, 128])
            P.op(DVE, lambda e: e.memset(zer[:], 0.0), writes=[zer.r])
            self.psb_all = self.psb
            pacc = self.psb_all[6:8]
            self.psb = self.psb_all[0:6]
            nq = 0
            ngb = 0

            keyn_b = hnf
            keyn = hnf[:, 0:4, :].rearrange("p k (a e) -> p (k a) e", e=128)
            kv = self.peer_keys[li].rearrange("h p n e -> n (h p) e")
            P.dma(SP, lambda e: e.dma_start(out=keyn, in_=kv), writes=[keyn_b.r], sbuf=keyn_b.r)
            for q4 in range(4):
                pt = self.ps()
                for j in range(4):
                    hp = q4 * 4 + j
                    P.op(PE, lambda e, hp=hp, j=j, pt=pt: e.transpose(out=pt[:, j * 128:(j + 1) * 128],
                                                                    in_=keyn[:, hp, :], identity=self.ident[:]),
                         reads=[keyn_b.r, self.ident.r], writes=[pt.r], inc=(j == 3))
                P.op(ACT, lambda e, q4=q4, pt=pt: e.copy(out=keyT[:, q4 * 4:(q4 + 1) * 4, :],
                                                        in_=pt[:].rearrange("p (j n) -> p j n", n=128)),
                     reads=[pt.r], writes=[keyT.r])

            wqv = self.peer_wq[li].rearrange("(k p) c -> p k c", p=128)
            for g in range(NG):
                ci = 0 if g < 4 else 1
                self.norm_group(g, lambda c, ci: self.m_gs(li, 1, c, ci), lambda c, ci: self.m_sh(li, 1, c, ci),
                                [hnf], rstd, sq, tmp)
                for hp in range(16):
                    wq = wqb[nq % 2]
                    qt = qT[nq % 2]
                    ss = scs[nq % 2]
                    nq += 1
                    P.dma(SP, lambda e, wq=wq, hp=hp: e.dma_start(out=wq[:], in_=wqv[:, :, hp * 128:(hp + 1) * 128]),
                          writes=[wq.r], sbuf=wq.r)
                    pq = self.ps()
                    for k in range(KC):
                        P.op(PE, lambda e, wq=wq, k=k, pq=pq: e.matmul(pq[:], lhsT=wq[:, k, :], rhs=hnf[:, k, :],
                                                                       start=(k == 0), stop=(k == KC - 1)),
                             reads=[wq.r, hnf.r], writes=[pq.r], inc=(k == KC - 1))
                    P.op(ACT, lambda e, qt=qt, pq=pq: e.copy(out=qt[:], in_=pq[:]), reads=[pq.r], writes=[qt.r])
                    psc = self.ps()
                    for tt in range(4):
                        P.op(PE, lambda e, qt=qt, tt=tt, psc=psc, hp=hp: e.matmul(
                            psc[:, tt * 128:(tt + 1) * 128], lhsT=qt[:, tt * 128:(tt + 1) * 128],
                            rhs=keyT[:, hp, :], start=True, stop=True),
                            reads=[qt.r, keyT.r], writes=[psc.r], inc=(tt == 3))
                    P.op(ACT, lambda e, ss=ss, psc=psc: e.copy(out=ss[:], in_=psc[:].rearrange("p (t n) -> p t n", n=128)),
                         reads=[psc.r], writes=[ss.r])
                    for tt in range(4):
                        P.op(DVE, lambda e, ss=ss, tt=tt, hp=hp: e.max(out=vv[:, tt, hp, 0:8], in_=ss[:, tt, :]),
                             reads=[ss.r], writes=[vv_r[tt]])
                    for tt in range(4):
                        P.op(DVE, lambda e, ss=ss, tt=tt, hp=hp: e.match_replace(
                            out=mrtL[tt][:], in_to_replace=vv[:, tt, hp, 0:8], in_values=ss[:, tt, :], imm_value=NEG),
                            reads=[ss.r, vv_r[tt]], writes=[mrtL[tt].r])
                    for tt in range(4):
                        P.op(DVE, lambda e, tt=tt, hp=hp: e.max(out=vv[:, tt, hp, 8:16], in_=mrtL[tt][:]),
                             reads=[mrtL[tt].r], writes=[vv_r[tt]])
                    for tt in range(4):
                        P.op(DVE, lambda e, ss=ss, tt=tt, hp=hp: e.max_index(
                            out=iu[:, tt, hp, 0:8], in_max=vv[:, tt, hp, 0:8], in_values=ss[:, tt, :]),
                            reads=[ss.r, vv_r[tt]], writes=[iu_r[tt]])
                    for tt in range(4):
                        P.op(DVE, lambda e, ss=ss, tt=tt, hp=hp: e.max_index(
                            out=iu[:, tt, hp, 8:16], in_max=vv[:, tt, hp, 8:16], in_values=ss[:, tt, :]),
                            reads=[ss.r, vv_r[tt]], writes=[iu_r[tt]])
                def prep(tt):
                    idxi, gg = idxiL[tt % 2], ggL[tt % 2]
                    P.op(DVE, lambda e, tt=tt: e.tensor_copy(out=i_f[:], in_=iu[:, tt]), reads=[iu_r[tt]], writes=[i_f.r])
                    v1 = vv[:, tt, 0:16:2, :]
                    v2 = vv[:, tt, 1:16:2, :]
                    c4 = cand[:].rearrange("p h (a b) -> p h a b", b=16)
                    x4 = cidx[:].rearrange("p h (a b) -> p h a b", b=16)
                    P.op(DVE, lambda e, v1=v1, v2=v2, c4=c4: e.tensor_tensor(
                        out=c4, in0=v1.unsqueeze(3).to_broadcast([128, 8, 16, 16]),
                        in1=v2.unsqueeze(2).to_broadcast([128, 8, 16, 16]), op=ALU.add),
                        reads=[vv_r[tt]], writes=[cand.r])
                    P.op(DVE, lambda e: e.tensor_scalar(out=i1s[:], in0=i_f[:, 0:16:2, :], scalar1=128.0,
                                                       scalar2=float(li * 16384), op0=ALU.mult, op1=ALU.add),
                         reads=[i_f.r], writes=[i1s.r])
                    for h in range(8):
                        P.op(DVE, lambda e, h=h: e.max(out=sc16[:, h, 0:8], in_=cand[:, h, :]),
                             reads=[cand.r], writes=[sc16.r])
                        P.op(DVE, lambda e, h=h: e.match_replace(out=cwork[:], in_to_replace=sc16[:, h, 0:8],
                                                                in_values=cand[:, h, :], imm_value=NEG),
                             reads=[cand.r, sc16.r], writes=[cwork.r])
                        P.op(DVE, lambda e, h=h: e.max(out=sc16[:, h, 8:16], in_=cwork[:]),
                             reads=[cwork.r], writes=[sc16.r])
                    for h in range(8):
                        for k8 in range(2):
                            P.op(DVE, lambda e, h=h, k8=k8: e.max_index(
                                out=pos[:, h, k8 * 8:(k8 + 1) * 8], in_max=sc16[:, h, k8 * 8:(k8 + 1) * 8],
                                in_values=cand[:, h, :]), reads=[cand.r, sc16.r], writes=[pos.r])
                    P.op(DVE, lambda e: e.tensor_copy(out=posf[:], in_=pos[:]), reads=[pos.r], writes=[posf.r])
                    P.op(DVE, lambda e, x4=x4: e.tensor_tensor(
                        out=x4, in0=posf[:].unsqueeze(3).to_broadcast([128, 8, 16, 16]),
                        in1=thr16[:].unsqueeze(1).unsqueeze(1).to_broadcast([128, 8, 16, 16]), op=ALU.is_ge),
                        reads=[posf.r, thr16.r], writes=[cand.r])
                    P.op(DVE, lambda e, x4=x4: e.reduce_sum(out=pa_[:], in_=x4, axis=mybir.AxisListType.X),
                         reads=[cand.r], writes=[pa_.r])
                    P.op(DVE, lambda e: e.scalar_tensor_tensor(out=pb_[:], in0=pa_[:], scalar=-16.0, in1=posf[:],
                                                               op0=ALU.mult, op1=ALU.add), reads=[pa_.r, posf.r], writes=[pb_.r])
                    io16 = self.iota[:, 0:16].unsqueeze(1).unsqueeze(1).to_broadcast([128, 8, 16, 16])
                    for (sel, tab, dsti) in ((pa_, i1s[:], idx1), (pb_, i_f[:, 1:16:2, :], idx2)):
                        P.op(DVE, lambda e, sel=sel, x4=x4: e.tensor_tensor(
                            out=x4, in0=io16, in1=sel[:].unsqueeze(3).to_broadcast([128, 8, 16, 16]), op=ALU.is_equal),
                            reads=[self.iota.r, sel.r], writes=[cand.r])
                        P.op(DVE, lambda e, tab=tab, x4=x4: e.tensor_tensor(
                            out=x4, in0=x4, in1=tab.unsqueeze(2).to_broadcast([128, 8, 16, 16]), op=ALU.mult),
                            reads=[cand.r, i1s.r, i_f.r], writes=[cand.r])
                        P.op(DVE, lambda e, dsti=dsti, x4=x4: e.reduce_sum(out=dsti[:], in_=x4, axis=mybir.AxisListType.X),
                             reads=[cand.r], writes=[dsti.r])
                    P.op(DVE, lambda e: e.tensor_tensor(out=idxf[:].rearrange("p (h k) -> p h k", k=16), in0=idx1[:], in1=idx2[:],
                                                        op=ALU.add), reads=[idx1.r, idx2.r], writes=[idxf.r])
                    P.op(DVE, lambda e: e.tensor_copy(out=idxi[:], in_=idxf[:]), reads=[idxf.r], writes=[idxi.r])
                    P.op(DVE, lambda e: e.tensor_scalar(out=negm[:], in0=sc16[:, :, 0], scalar1=-1.0, scalar2=None,
                                                       op0=ALU.mult), reads=[sc16.r], writes=[negm.r])
                    for h in range(8):
                        P.op(ACT, lambda e, h=h: e.activation(out=ee[:, h, :], in_=sc16[:, h, :], func=AF.Exp,
                                                             bias=negm[:, h:h + 1], scale=1.0),
                             reads=[sc16.r, negm.r], writes=[ee.r])
                    P.op(DVE, lambda e: e.reduce_sum(out=esum[:], in_=ee[:], axis=mybir.AxisListType.X),
                         reads=[ee.r], writes=[esum.r])
                    P.op(DVE, lambda e: e.reciprocal(out=esum[:], in_=esum[:]), reads=[esum.r], writes=[esum.r])
                    P.op(DVE, lambda e: e.tensor_tensor(
                        out=gg[:].rearrange("p (h k) -> p h k", k=16), in0=ee[:],
                        in1=esum[:].unsqueeze(2).to_broadcast([128, 8, 16]), op=ALU.mult),
                        reads=[ee.r, esum.r], writes=[gg.r])

                def slots(tt):
                    nonlocal ngb
                    idxi, gg = idxiL[tt % 2], ggL[tt % 2]
                    tsl = slice(g * GT + tt * 128, g * GT + (tt + 1) * 128)
                    for half in range(2):
                        pt = self.ps()
                        for j in range(4):
                            c = half * 4 + j
                            P.op(PE, lambda e, c=c, j=j, pt=pt, tt=tt: e.transpose(
                                out=pt[:, j * 128:(j + 1) * 128], in_=hnf[:, c, tt * 128:(tt + 1) * 128],
                                identity=self.ident[:]), reads=[hnf.r, self.ident.r], writes=[pt.r], inc=(j == 3))
                        P.op(ACT, lambda e, half=half, pt=pt: e.copy(out=hn2[:, half * 512:(half + 1) * 512], in_=pt[:]),
                             reads=[pt.r], writes=[hn2.r])
                    for pa in pacc:
                        P.op(PE, lambda e, pa=pa: e.matmul(pa[:], lhsT=zer[:], rhs=hnf[:, 0, :], start=True, stop=False),
                             reads=[zer.r, hnf.r], writes=[pa.r])
                    for j in range(NS):
                        b = gb[ngb % NB]
                        dgb = dg[ngb % 3]
                        ngb += 1
                        P.dma(POOL, lambda e, b=b, j=j: e.indirect_dma_start(
                            out=b[:], out_offset=None, in_=self.uv_bf,
                            in_offset=bass.IndirectOffsetOnAxis(ap=idxi[:, j:j + 1], axis=0)),
                            reads=[idxi.r], writes=[b.r], sbuf=b.r)
                        P.op(DVE, lambda e, b=b, j=j: e.scalar_tensor_tensor(
                            out=junk2v, in0=hn2[:], scalar=1.0, in1=b[:, 0:D], op0=ALU.mult, op1=ALU.mult,
                            accum_out=act[:, j:j + 1]), reads=[hn2.r, b.r], writes=[act_r[j]])
                        P.op(ACT, lambda e, j=j: e.activation(out=coef[:, j:j + 1], in_=act[:, j:j + 1], func=AF.Gelu),
                             reads=[act_r[j]], writes=[coef_r[j]])
                        P.op(ACT, lambda e, j=j: e.activation(out=coef[:, j:j + 1], in_=coef[:, j:j + 1], func=AF.Copy,
                                                             scale=gg[:, j:j + 1]), reads=[coef_r[j], gg.r], writes=[coef_r[j]])
                        P.op(ACT, lambda e, dgb=dgb, j=j: e.activation(out=dgb[:], in_=self.ident[:], func=AF.Copy,
                                                                      scale=coef[:, j:j + 1]),
                             reads=[self.ident.r, coef_r[j]], writes=[dgb.r])
                        for c in range(KC):
                            pa = pacc[c // 4]
                            P.op(PE, lambda e, b=b, dgb=dgb, c=c, pa=pa, j=j: e.matmul(
                                pa[:, (c % 4) * 128:(c % 4 + 1) * 128], lhsT=b[:, D + c * 128:D + (c + 1) * 128], rhs=dgb[:],
                                start=False, stop=(j == NS - 1)),
                                reads=[b.r, dgb.r], writes=[pa.r], inc=(c == KC - 1))
                    for c in range(KC):
                        pa = pacc[c // 4]
                        P.op(DVE, lambda e, c=c, pa=pa, ci=ci, tsl=tsl: e.scalar_tensor_tensor(
                            out=self.xT[:, c, tsl], in0=pa[:, (c % 4) * 128:(c % 4 + 1) * 128],
                            scalar=self.m_gt(li, 1, c, ci), in1=self.xT[:, c, tsl], op0=ALU.mult, op1=ALU.add),
                            reads=[pa.r, self.modT[li].r, self.xr[c][g]], writes=[self.xr[c][g]])

                prep(0)
                for tt in range(4):
                    if tt < 3:
                        prep(tt + 1)
                    slots(tt)
            self.psb = self.psb_all
            P.barrier()

    def norm_slice(self, tok0, W, g, ci, gs_fn, sh_fn, ob, rstd, sq, tmp):
        P = self.P
        pss = self.ps()
        sl = slice(tok0, tok0 + W)
        for c in range(KC):
            P.op(ACT, lambda e, c=c: e.activation(out=sq[:, c % 2, :], in_=self.xT[:, c, sl], func=AF.Square),
                 reads=[self.xr[c][g]], writes=[sq.r])
            P.op(PE, lambda e, c=c: e.matmul(pss[:, 0:W], lhsT=self.ones[:], rhs=sq[:, c % 2, :],
                                             start=(c == 0), stop=(c == KC - 1)),
                 reads=[self.ones.r, sq.r], writes=[pss.r], inc=True)
        P.op(ACT, lambda e: e.activation(out=rstd[:], in_=pss[:, 0:W], func=AF.Sqrt, bias=self.epsb[:], scale=1.0 / D),
             reads=[pss.r, self.epsb.r], writes=[rstd.r])
        P.op(DVE, lambda e: e.reciprocal(out=rstd[:], in_=rstd[:]), reads=[rstd.r], writes=[rstd.r])
        for c in range(KC):
            P.op(DVE, lambda e, c=c: e.tensor_tensor(out=tmp[:, c % 2, :], in0=self.xT[:, c, sl], in1=rstd[:], op=ALU.mult),
                 reads=[self.xr[c][g], rstd.r], writes=[tmp.r])
            P.op(ACT, lambda e, c=c: e.activation(out=ob[:, c, :], in_=tmp[:, c % 2, :], func=AF.Identity,
                                                  scale=gs_fn(c, ci), bias=sh_fn(c, ci)),
                 reads=[tmp.r, self.gs.r, self.modT[0].r, self.modT[1].r], writes=[ob.r])

    def phase_ssd(self):
        nc, P = self.nc, self.P
        key = ("X", "tabl1")
        P._sem(key)
        NCH = 16
        tab1_items = []
        for k_ in range(NCH):
            r0 = 16384 + k_ * (16384 // NCH)
            r1 = r0 + 16384 // NCH
            tab1_items.append({"waits": [], "inc": (key, 16), "fn": _freeze(
                lambda e, r0=r0, r1=r1: e.dma_start(out=self.uv_bf[r0:r1, :], in_=self.peer_uv_flat[r0:r1, :]))})
        self.tab1_tok = (key, 16 * NCH)
        X = mybir.AxisListType.X
        W = 256
        with contextlib.ExitStack() as ph:
            sb = self.sb
            rowp = sb(ph, "rowp", [128, ROWP_N])
            par = [0]
            two = lambda name, shape, dt=F32: [sb(ph, name + str(i), shape, dt) for i in range(2)]
            tri = [sb(ph, "tri%d" % d, [128, 128]) for d in range(2)]
            mneg = [sb(ph, "mneg%d" % d, [128, 128]) for d in range(2)]
            flags = sb(ph, "flags", [128, 4])
            onec = sb(ph, "onec", [128, 1])
            hnb = sb(ph, "hnb", [128, KC, W], BF16)
            rstd = sb(ph, "rstd", [128, W])
            sq = sb(ph, "sq", [128, 2, W])
            tmp = sb(ph, "tmp", [128, 2, W])
            wbf = [sb(ph, "wbf%d" % i, [128, KC * 256], BF16) for i in range(2)]
            wdtb = sb(ph, "wdtb", [128, KC, 64], BF16)
            yb = [sb(ph, "yb%d" % i, [128, W]) for i in range(3)]
            silb = [sb(ph, "silb%d" % i, [128, W]) for i in range(3)]
            BT = sb(ph, "BT", [128, 8, W], BF16)
            CT = sb(ph, "CT", [128, 8, W], BF16)
            xs_tok = sb(ph, "xs_tok", [128, 2, 2048], BF16)
            B_tok = sb(ph, "B_tok", [128, 2, 1024], BF16)
            zs = sb(ph, "zs", [128, 2048], BF16)
            xx = sb(ph, "xx", [128, 64])
            ax = sb(ph, "ax", [128, 64])
            dt = sb(ph, "dt", [128, 2, 64])
            dA = sb(ph, "dA", [128, 2, 64])
            arow = sb(ph, "arow", [128, 64])
            drow = sb(ph, "drow", [128, 32])
            cumcL = two("cumc", [128, 32])
            totL = two("tot", [128, 32])
            wv = sb(ph, "wv", [128, 32])
            dec = sb(ph, "dec", [128, 32])
            Pb = sb(ph, "Pb", [128, 32])
            xdt = [sb(ph, "xdt%d" % d, [128, 2048], BF16) for d in range(2)]
            xw = sb(ph, "xw", [128, 2048], BF16)
            RgL = two("Rg", [128, 512])
            segL = two("seg", [128, 512])
            GmL = [two("Gm%d_" % d, [128, 512], BF16) for d in range(2)]
            EeL = two("Ee", [128, 512])
            CsL = two("Cs", [128, 512], BF16)
            CBtL = two("CBt", [128, 128])
            hT = [sb(ph, "hT%d" % d, [128, 2048]) for d in range(2)]
            hTb = [sb(ph, "hTb%d" % d, [128, 2048], BF16) for d in range(2)]
            ysb = sb(ph, "ysb", [128, 2048])

            t1 = sb(ph, "t1", [128, 256])
            ssq = sb(ph, "ssq", [128, 1])
            ynT = sb(ph, "ynT", [128, 16, 128], BF16)
            class _AB:
                def __init__(self, ap, name):
                    self.ap = ap
                    self.r = Res(name)

                def __getitem__(self, k):
                    return self.ap[k]

            _h0b = hT[0][:].bitcast(BF16)
            zsL = [zs, _AB(_h0b[:, 0:2048], "zs_alt")]
            ynTL = [ynT, _AB(_h0b[:, 2048:4096].rearrange("p (k n) -> p k n", n=128), "ynT_alt")]
            w_in = self.ssd_w_in[0].rearrange("(k p) c -> p k c", p=128)
            w_out = self.ssd_w_out[0].rearrange("(k p) c -> p k c", p=128)
            w_in_bf = self.w_in_bf.rearrange("(k p) c -> p k c", p=128)
            w_out_bf = self.w_out_bf.rearrange("(k p) c -> p k c", p=128)
            nld = [0]

            def load_w(view, c0, wide=True):
                wb = wbf[nld[0] % 2]
                nld[0] += 1
                ncol = 256 if wide else 128
                dst = wb[:].rearrange("p (k c) -> p k c", c=ncol)
                P.dma(SP, lambda e: e.dma_start(out=dst, in_=view[:, :, c0:c0 + ncol]), writes=[wb.r], sbuf=wb.r)
                return wb, dst

            for (buf, src) in ((rowp, self.rowp_d), (tri[0], self.tri_d[0]), (tri[1], self.tri_d[1]),
                               (mneg[0], self.mneg_d[0]), (mneg[1], self.mneg_d[1]), (flags, self.flags_d)):
                P.dma(SP, lambda e, buf=buf, src=src: e.dma_start(out=buf[:], in_=src), writes=[buf.r], sbuf=buf.r)
            P.op(DVE, lambda e: e.memset(onec[:], 1.0), writes=[onec.r])
            P.op(ACT, lambda e: e.activation(out=arow[:], in_=rowp[:, RP_ALOG:RP_ALOG + 64], func=AF.Exp),
                 reads=[rowp.r], writes=[arow.r])
            P.op(DVE, lambda e: e.tensor_scalar(out=arow[:], in0=arow[:], scalar1=-1.0, scalar2=None, op0=ALU.mult),
                 reads=[arow.r], writes=[arow.r])
            P.op(DVE, lambda e: e.tensor_tensor(out=drow[:], in0=rowp[:, RP_D:RP_D + 32], in1=rowp[:, RP_D + 32:RP_D + 64],
                                                op=ALU.add), reads=[rowp.r], writes=[drow.r])
            stgs = [ysb, hT[0], hT[1]]
            cast_eng = (DVE, ACT, DVE)
            nst = 0
            for (src, dstv, K_, ncol, total) in ((w_in, w_in_bf, KC, 256, 6208), (w_out, w_out_bf, 16, 128, D)):
                for c0 in range(0, total, ncol):
                    n_ = min(ncol, total - c0)
                    stg = stgs[nst % 3]
                    wb = wbf[nst % 2]
                    ce = cast_eng[nst % 3]
                    nst += 1
                    sv = stg[:].rearrange("p (k c) -> p k c", c=ncol)[:, :, 0:n_]
                    wv_ = wb[:].rearrange("p (k c) -> p k c", c=ncol)[:, :, 0:n_]
                    P.dma(SP, lambda e, sv=sv, src=src, c0=c0, n_=n_: e.dma_start(out=sv, in_=src[:, :, c0:c0 + n_]),
                          writes=[stg.r], sbuf=stg.r)
                    if ce == ACT:
                        P.op(ACT, lambda e, sv=sv, wv_=wv_: e.copy(out=wv_, in_=sv), reads=[stg.r], writes=[wb.r])
                    elif ce == DVE:
                        P.op(DVE, lambda e, sv=sv, wv_=wv_: e.tensor_copy(out=wv_, in_=sv), reads=[stg.r], writes=[wb.r])
                    else:
                        P.op(POOL, lambda e, sv=sv, wv_=wv_: e.tensor_copy(out=wv_, in_=sv), reads=[stg.r], writes=[wb.r])
                    P.dma(SP, lambda e, wv_=wv_, dstv=dstv, c0=c0, n_=n_: e.dma_start(out=dstv[:, :, c0:c0 + n_], in_=wv_),
                          reads=[wb.r], sbuf=wb.r, store=True)
            P.barrier()
            P.dma(SP, lambda e: e.dma_start(out=wdtb[:], in_=w_in_bf[:, :, 6144:6208]), writes=[wdtb.r], sbuf=wdtb.r)

            def fe_io(pi, store):
                items = ((xs_tok, "p a b -> p (a b)"), (B_tok, "p a b -> p (a b)"), (BT, "p a b -> p (a b)"),
                         (CT, "p a b -> p (a b)"), (dt, "p a b -> p (a b)"), (dA, "p a b -> p (a b)"))
                for k_, (buf, pat) in enumerate(items):
                    view = buf[:].rearrange(pat)
                    dram = self.fe_d[k_][pi]
                    rr = self.fe_r[k_][pi]
                    if store:
                        P.dma(SP, lambda e, view=view, dram=dram: e.dma_start(out=dram, in_=view), reads=[buf.r], writes=[rr],
                              sbuf=buf.r, store=True)
                    else:
                        P.dma(SP, lambda e, view=view, dram=dram: e.dma_start(out=view, in_=dram), reads=[rr], writes=[buf.r],
                              sbuf=buf.r)

            def frontend(tok0, cols, ci, g, only_norm=False):
                self.norm_slice(tok0, W, g, ci, lambda c, ci: self.m_gs(1, 0, c, ci), lambda c, ci: self.m_sh(1, 0, c, ci),
                                hnb, rstd, sq, tmp)
                if only_norm:
                    return
                for ch in range(2):
                    pd = self.ps()
                    for k in range(KC):
                        P.op(PE, lambda e, k=k, ch=ch, pd=pd: e.matmul(pd[:, 0:64], lhsT=hnb[:, k, ch * 128:(ch + 1) * 128],
                                                                       rhs=wdtb[:, k, :], start=(k == 0), stop=(k == KC - 1)),
                             reads=[hnb.r, wdtb.r], writes=[pd.r], inc=(k == KC - 1))
                    P.op(DVE, lambda e, pd=pd: e.tensor_tensor(out=xx[:], in0=pd[:, 0:64], in1=rowp[:, RP_DTB:RP_DTB + 64],
                                                               op=ALU.add), reads=[pd.r, rowp.r], writes=[xx.r])
                    P.op(ACT, lambda e: e.activation(out=ax[:], in_=xx[:], func=AF.Abs), reads=[xx.r], writes=[ax.r])
                    P.op(ACT, lambda e: e.activation(out=ax[:], in_=ax[:], func=AF.Exp, scale=-1.0), reads=[ax.r], writes=[ax.r])
                    P.op(ACT, lambda e: e.activation(out=ax[:], in_=ax[:], func=AF.Ln, bias=onec[:], scale=1.0),
                         reads=[ax.r, onec.r], writes=[ax.r])
                    P.op(DVE, lambda e, ch=ch: e.scalar_tensor_tensor(out=dt[:, ch, :], in0=xx[:], scalar=0.0, in1=ax[:],
                                                                      op0=ALU.max, op1=ALU.add),
                         reads=[xx.r, ax.r], writes=[dt.r])
                    P.op(DVE, lambda e, ch=ch: e.tensor_tensor(out=dA[:, ch, :], in0=dt[:, ch, :], in1=arow[:], op=ALU.mult),
                         reads=[dt.r, arow.r], writes=[dA.r])
                nblk = 0
                for s in range(16):
                    wbr, wb = load_w(w_in_bf, 2048 + s * 256)
                    for cb in range(2):
                        blk = s * 2 + cb
                        y, sl_ = yb[nblk % 3], silb[nblk % 3]
                        nblk += 1
                        pt = self.ps()
                        for k in range(KC):
                            P.op(PE, lambda e, wb=wb, cb=cb, k=k, pt=pt: e.matmul(
                                pt[:, 0:W], lhsT=wb[:, k, cb * 128:(cb + 1) * 128], rhs=hnb[:, k, :],
                                start=(k == 0), stop=(k == KC - 1)), reads=[wbr.r, hnb.r], writes=[pt.r], inc=(k == KC - 1))
                        u = pt
                        wcol = lambda tap, blk=blk: self.spk[:, SP_SSW + tap * 32 + blk:SP_SSW + tap * 32 + blk + 1]
                        bcol = self.spk[:, SP_SSB + blk:SP_SSB + blk + 1]
                        u3 = pt[:, 0:W].rearrange("p (r w) -> p r w", w=cols)
                        y3 = y[:].rearrange("p (r w) -> p r w", w=cols)
                        P.op(DVE, lambda e, u=u, y=y, wcol=wcol: e.tensor_scalar(out=y[:], in0=u[:, 0:W], scalar1=wcol(1), scalar2=None,
                                                                                op0=ALU.mult), reads=[u.r, self.spk.r], writes=[y.r])
                        P.op(DVE, lambda e, u3=u3, y3=y3, wcol=wcol: e.scalar_tensor_tensor(
                            out=y3[:, :, 1:], in0=u3[:, :, :cols - 1], scalar=wcol(0), in1=y3[:, :, 1:], op0=ALU.mult, op1=ALU.add),
                            reads=[u.r, y.r, self.spk.r], writes=[y.r])
                        P.op(DVE, lambda e, u3=u3, y3=y3, wcol=wcol: e.scalar_tensor_tensor(
                            out=y3[:, :, :cols - 1], in0=u3[:, :, 1:], scalar=wcol(2), in1=y3[:, :, :cols - 1], op0=ALU.mult, op1=ALU.add),
                            reads=[u.r, y.r, self.spk.r], writes=[y.r])
                        if blk >= 24:
                            P.op(ACT, lambda e, y=y, blk=blk, bcol=bcol: e.activation(out=CT[:, blk - 24, :], in_=y[:], func=AF.Silu,
                                                                                     bias=bcol, scale=1.0),
                                 reads=[y.r, self.spk.r], writes=[CT.r])
                            continue
                        P.op(ACT, lambda e, y=y, sl_=sl_, bcol=bcol: e.activation(out=sl_[:], in_=y[:], func=AF.Silu, bias=bcol, scale=1.0),
                             reads=[y.r, self.spk.r], writes=[sl_.r])
                        p2 = self.ps()
                        for ch in range(2):
                            P.op(PE, lambda e, sl_=sl_, ch=ch, p2=p2: e.transpose(out=p2[:, ch * 128:(ch + 1) * 128],
                                                                                 in_=sl_[:, ch * 128:(ch + 1) * 128], identity=self.ident[:]),
                                 reads=[sl_.r, self.ident.r], writes=[p2.r], inc=(ch == 1))
                        p23 = p2[:, 0:256].rearrange("p (c n) -> p c n", n=128)
                        if blk < 16:
                            P.op(ACT, lambda e, blk=blk, p23=p23: e.copy(out=xs_tok[:, :, blk * 128:(blk + 1) * 128], in_=p23),
                                 reads=[p2.r], writes=[xs_tok.r])
                        else:
                            gb_ = blk - 16
                            P.op(ACT, lambda e, gb_=gb_, p23=p23: e.copy(out=B_tok[:, :, gb_ * 128:(gb_ + 1) * 128], in_=p23),
                                 reads=[p2.r], writes=[B_tok.r])
                            P.op(DVE, lambda e, gb_=gb_, sl_=sl_: e.tensor_copy(out=BT[:, gb_, :], in_=sl_[:]),
                                 reads=[sl_.r], writes=[BT.r])

            def zproj_pair():
                for s in range(8):
                    wbr, wb = load_w(w_in_bf, s * 256)
                    for ch in range(2):
                        pz = self.ps()
                        zz = zsL[ch]
                        for k in range(KC):
                            P.op(PE, lambda e, wb=wb, k=k, pz=pz, ch=ch: e.matmul(pz[:, 0:256], lhsT=hnb[:, k, ch * 128:(ch + 1) * 128],
                                                                                rhs=wb[:, k, :], start=(k == 0), stop=(k == KC - 1)),
                                 reads=[wbr.r, hnb.r], writes=[pz.r], inc=(k == KC - 1))
                        P.op(ACT, lambda e, s=s, pz=pz, zz=zz: e.activation(out=zz[:, s * 256:(s + 1) * 256], in_=pz[:, 0:256], func=AF.Silu),
                             reads=[pz.r], writes=[zz.r])

            def dir_common(ch, d):
                pc = self.ps()
                P.op(PE, lambda e, pc=pc: e.matmul(pc[:, 0:32], lhsT=tri[d][:], rhs=dA[:, ch, d * 32:(d + 1) * 32], start=True, stop=True),
                     reads=[tri[d].r, dA.r], writes=[pc.r], inc=False)
                P.op(PE, lambda e, pc=pc: e.matmul(pc[:, 32:64], lhsT=self.ones[:], rhs=dA[:, ch, d * 32:(d + 1) * 32], start=True, stop=True),
                     reads=[self.ones.r, dA.r], writes=[pc.r], inc=True)
                P.op(ACT, lambda e, pc=pc: e.copy(out=cumcL[d][:], in_=pc[:, 0:32]), reads=[pc.r], writes=[cumcL[d].r])
                P.op(ACT, lambda e, pc=pc: e.copy(out=totL[d][:], in_=pc[:, 32:64]), reads=[pc.r], writes=[totL[d].r])
                P.op(DVE, lambda e: e.tensor_tensor(
                    out=xdt[d][:].rearrange("p (h q) -> p h q", q=64), in0=xs_tok[:, ch, :].rearrange("p (h q) -> p h q", q=64),
                    in1=dt[:, ch, d * 32:(d + 1) * 32].unsqueeze(2).to_broadcast([128, 32, 64]), op=ALU.mult),
                    reads=[xs_tok.r, dt.r], writes=[xdt[d].r])

            def cumbc_group(ch, d, g):
                Rg = RgL[par[0] % 2]
                P.op(POOL, lambda e: e.tensor_tensor(
                    out=Rg[:].rearrange("p (h i) -> p h i", i=128), in0=tri[d][:].unsqueeze(1).to_broadcast([128, 4, 128]),
                    in1=dA[:, ch, d * 32 + g * 4:d * 32 + g * 4 + 4].unsqueeze(2).to_broadcast([128, 4, 128]), op=ALU.mult),
                    reads=[tri[d].r, dA.r], writes=[Rg.r])
                pb = self.ps()
                P.op(PE, lambda e, pb=pb: e.matmul(pb[:], lhsT=self.ones[:], rhs=Rg[:], start=True, stop=True),
                     reads=[self.ones.r, Rg.r], writes=[pb.r])
                return pb

            def diag_group(ch, d, g, pb):
                sg = segL[(par[0] + d) % 2]
                CBt = CBtL[par[0] % 2]
                Gm = [GmL[0][par[0] % 2], GmL[1][par[0] % 2]]
                P.op(DVE, lambda e, pb=pb: e.tensor_tensor(
                    out=sg[:].rearrange("p (h i) -> p h i", i=128), in0=pb[:].rearrange("p (h i) -> p h i", i=128),
                    in1=cumcL[d][:, g * 4:g * 4 + 4].unsqueeze(2).to_broadcast([128, 4, 128]), op=ALU.subtract),
                    reads=[pb.r, cumcL[d].r], writes=[sg.r])
                P.op(DVE, lambda e: e.tensor_tensor(
                    out=sg[:].rearrange("p (h i) -> p h i", i=128), in0=sg[:].rearrange("p (h i) -> p h i", i=128),
                    in1=mneg[d][:].unsqueeze(1).to_broadcast([128, 4, 128]), op=ALU.add),
                    reads=[sg.r, mneg[d].r], writes=[sg.r])
                P.op(ACT, lambda e: e.activation(out=sg[:], in_=sg[:], func=AF.Exp), reads=[sg.r], writes=[sg.r])
                P.op(DVE, lambda e: e.tensor_tensor(
                    out=Gm[d][:].rearrange("p (h i) -> p h i", i=128), in0=sg[:].rearrange("p (h i) -> p h i", i=128),
                    in1=CBt[:].unsqueeze(1).to_broadcast([128, 4, 128]), op=ALU.mult),
                    reads=[sg.r, CBt.r], writes=[Gm[d].r])

            def off_group(ch, d, g, pb):
                Ee = EeL[par[0] % 2]
                Cs = CsL[par[0] % 2]
                P.op(ACT, lambda e, pb=pb: e.activation(out=Ee[:], in_=pb[:], func=AF.Exp), reads=[pb.r], writes=[Ee.r])
                P.op(DVE, lambda e: e.tensor_tensor(
                    out=Cs[:].rearrange("p (h i) -> p h i", i=128), in0=Ee[:].rearrange("p (h i) -> p h i", i=128),
                    in1=CT[:, g, ch * 128:(ch + 1) * 128].unsqueeze(1).to_broadcast([128, 4, 128]), op=ALU.mult),
                    reads=[Ee.r, CT.r], writes=[Cs.r])

            def state_update(ch, d, accumulate_prefix=False):
                P.op(DVE, lambda e: e.tensor_tensor(out=wv[:], in0=totL[d][:], in1=cumcL[d][:], op=ALU.subtract),
                     reads=[totL[d].r, cumcL[d].r], writes=[wv.r])
                P.op(ACT, lambda e: e.activation(out=wv[:], in_=wv[:], func=AF.Exp), reads=[wv.r], writes=[wv.r])
                P.op(ACT, lambda e: e.activation(out=dec[:], in_=totL[d][:], func=AF.Exp), reads=[totL[d].r], writes=[dec.r])
                P.op(DVE, lambda e: e.tensor_tensor(
                    out=xw[:].rearrange("p (h q) -> p h q", q=64), in0=xdt[d][:].rearrange("p (h q) -> p h q", q=64),
                    in1=wv[:].unsqueeze(2).to_broadcast([128, 32, 64]), op=ALU.mult), reads=[xdt[d].r, wv.r], writes=[xw.r])
                for g in range(8):
                    pS = self.ps()
                    P.op(PE, lambda e, g=g, pS=pS: e.matmul(pS[:, 0:256], lhsT=B_tok[:, ch, g * 128:(g + 1) * 128],
                                                           rhs=xw[:, g * 256:(g + 1) * 256], start=True, stop=True),
                         reads=[B_tok.r, xw.r], writes=[pS.r])
                    hv = hT[d][:, g * 256:(g + 1) * 256].rearrange("p (h q) -> p h q", q=64)
                    if accumulate_prefix:
                        P.op(DVE, lambda e, g=g, pS=pS: e.tensor_tensor(
                            out=t1[:].rearrange("p (h q) -> p h q", q=64), in0=pS[:, 0:256].rearrange("p (h q) -> p h q", q=64),
                            in1=Pb[:, g * 4:g * 4 + 4].unsqueeze(2).to_broadcast([128, 4, 64]), op=ALU.mult),
                            reads=[pS.r, Pb.r], writes=[t1.r])
                        P.op(DVE, lambda e, g=g: e.tensor_tensor(out=hT[d][:, g * 256:(g + 1) * 256], in0=hT[d][:, g * 256:(g + 1) * 256],
                                                                 in1=t1[:], op=ALU.add), reads=[t1.r, hT[d].r], writes=[hT[d].r])
                    else:
                        P.op(DVE, lambda e, hv=hv, g=g: e.tensor_tensor(
                            out=hv, in0=hv, in1=dec[:, g * 4:g * 4 + 4].unsqueeze(2).to_broadcast([128, 4, 64]), op=ALU.mult),
                            reads=[hT[d].r, dec.r], writes=[hT[d].r])
                        P.op(DVE, lambda e, g=g, pS=pS: e.tensor_tensor(out=hT[d][:, g * 256:(g + 1) * 256],
                                                                        in0=hT[d][:, g * 256:(g + 1) * 256], in1=pS[:, 0:256], op=ALU.add),
                             reads=[pS.r, hT[d].r], writes=[hT[d].r])
                if accumulate_prefix:
                    P.op(DVE, lambda e: e.tensor_tensor(out=Pb[:], in0=Pb[:], in1=dec[:], op=ALU.mult), reads=[Pb.r, dec.r], writes=[Pb.r])
                else:
                    P.op(ACT, lambda e: e.copy(out=hTb[d][:], in_=hT[d][:]), reads=[hT[d].r], writes=[hTb[d].r])

            def scan_chunk(ch, mode, ychunk, tok0, ci):
                if mode == "S":
                    for d in range(2):
                        dir_common(ch, d)
                        state_update(ch, d, accumulate_prefix=(d == 1))
                    return
                dirs = (0, 1) if mode == "F" else (1,)
                od = 0 if mode == "F" else 1
                for d in dirs:
                    dir_common(ch, d)
                    if d == od:
                        pass
                if mode == "B":
                    P.dma(SP, lambda e: e.dma_start(out=ysb[:], in_=self.ypart_d[ychunk]), reads=[self.ypart_r[ychunk]],
                          writes=[ysb.r], sbuf=ysb.r)
                for g in range(8):
                    par[0] += 1
                    CBt = CBtL[par[0] % 2]
                    Gm = [GmL[0][par[0] % 2], GmL[1][par[0] % 2]]
                    Cs = CsL[par[0] % 2]
                    py = self.ps()
                    seq = []
                    if mode == "F":
                        pcb = self.ps()
                        P.op(PE, lambda e, g=g, pcb=pcb: e.matmul(pcb[:, 0:128], lhsT=BT[:, g, ch * 128:(ch + 1) * 128],
                                                                  rhs=CT[:, g, ch * 128:(ch + 1) * 128], start=True, stop=True),
                             reads=[BT.r, CT.r], writes=[pcb.r])
                        P.op(ACT, lambda e, pcb=pcb: e.copy(out=CBt[:], in_=pcb[:, 0:128]), reads=[pcb.r], writes=[CBt.r])
                    for d in dirs:
                        pb = cumbc_group(ch, d, g)
                        if mode == "F":
                            diag_group(ch, d, g, pb)
                            seq.append((Gm[d], xdt[d]))
                        if d == od:
                            off_group(ch, d, g, pb)
                            seq.append((Cs, None))
                    n = len(seq)
                    for hh in range(4):
                        h = g * 4 + hh
                        for qi, (lt, rx) in enumerate(seq):
                            if rx is None:
                                P.op(PE, lambda e, lt=lt, hh=hh, h=h, py=py, qi=qi: e.matmul(
                                    py[:, hh * 64:(hh + 1) * 64], lhsT=lt[:, hh * 128:(hh + 1) * 128],
                                    rhs=hTb[od][:, h * 64:(h + 1) * 64], start=(qi == 0), stop=(qi == n - 1)),
                                    reads=[lt.r, hTb[od].r], writes=[py.r], inc=(qi == n - 1 and hh == 3))
                            else:
                                P.op(PE, lambda e, lt=lt, rx=rx, hh=hh, h=h, py=py, qi=qi: e.matmul(
                                    py[:, hh * 64:(hh + 1) * 64], lhsT=lt[:, hh * 128:(hh + 1) * 128],
                                    rhs=rx[:, h * 64:(h + 1) * 64], start=(qi == 0), stop=(qi == n - 1)),
                                    reads=[lt.r, rx.r], writes=[py.r], inc=(qi == n - 1 and hh == 3))
                    if mode == "F":
                        P.op(DVE, lambda e, g=g: e.tensor_tensor(
                            out=t1[:].rearrange("p (h q) -> p h q", q=64),
                            in0=xs_tok[:, ch, g * 256:(g + 1) * 256].rearrange("p (h q) -> p h q", q=64),
                            in1=drow[:, g * 4:g * 4 + 4].unsqueeze(2).to_broadcast([128, 4, 64]), op=ALU.mult),
                            reads=[xs_tok.r, drow.r], writes=[t1.r])
                        P.op(DVE, lambda e, g=g, py=py: e.tensor_tensor(out=ysb[:, g * 256:(g + 1) * 256], in0=t1[:], in1=py[:, 0:256],
                                                                        op=ALU.add), reads=[t1.r, py.r], writes=[ysb.r])
                    else:
                        P.op(DVE, lambda e, g=g, py=py: e.tensor_tensor(out=ysb[:, g * 256:(g + 1) * 256], in0=ysb[:, g * 256:(g + 1) * 256],
                                                                        in1=py[:, 0:256], op=ALU.add), reads=[ysb.r, py.r], writes=[ysb.r])
                state_update(ch, od)
                if mode == "F":
                    P.dma(SP, lambda e: e.dma_start(out=self.ypart_d[ychunk], in_=ysb[:]), reads=[ysb.r],
                          writes=[self.ypart_r[ychunk]], sbuf=ysb.r, store=True)
                else:
                    zz = zsL[ch]
                    yT_ = ynTL[ch]
                    P.op(DVE, lambda e: e.tensor_tensor(out=ysb[:], in0=ysb[:], in1=zz[:, 0:2048], op=ALU.mult), reads=[ysb.r, zz.r], writes=[ysb.r])
                    P.op(DVE, lambda e: e.scalar_tensor_tensor(out=xdt[0][:], in0=ysb[:], scalar=1.0, in1=ysb[:], op0=ALU.mult, op1=ALU.mult,
                                                               accum_out=ssq[:]), reads=[ysb.r], writes=[xdt[0].r, ssq.r])
                    P.op(ACT, lambda e: e.activation(out=ssq[:], in_=ssq[:], func=AF.Sqrt, bias=self.epsb[:], scale=1.0 / 2048.0),
                         reads=[ssq.r, self.epsb.r], writes=[ssq.r])
                    P.op(DVE, lambda e: e.reciprocal(out=ssq[:], in_=ssq[:]), reads=[ssq.r], writes=[ssq.r])
                    P.op(DVE, lambda e: e.tensor_scalar(out=ysb[:], in0=ysb[:], scalar1=ssq[:], scalar2=None, op0=ALU.mult),
                         reads=[ysb.r, ssq.r], writes=[ysb.r])
                    for q4 in range(4):
                        pt = self.ps()
                        for j in range(4):
                            k = q4 * 4 + j
                            P.op(PE, lambda e, k=k, j=j, pt=pt: e.transpose(out=pt[:, j * 128:(j + 1) * 128], in_=ysb[:, k * 128:(k + 1) * 128],
                                                                            identity=self.ident[:]),
                                 reads=[ysb.r, self.ident.r], writes=[pt.r], inc=(j == 3))
                        for j in range(4):
                            k = q4 * 4 + j
                            P.op(ACT, lambda e, k=k, j=j, pt=pt: e.activation(
                                out=yT_[:, k, :], in_=pt[:, j * 128:(j + 1) * 128], func=AF.Copy,
                                scale=self.spk[:, SP_SNG + k:SP_SNG + k + 1]), reads=[pt.r, self.spk.r], writes=[yT_.r])

            def out_proj_pair(tok0, g, ci):
                for s in range(8):
                    wbr, wb = load_w(w_out_bf, s * 128, wide=False)
                    dc = s
                    pt = self.ps()
                    for ch in range(2):
                        yT_ = ynTL[ch]
                        for k in range(16):
                            P.op(PE, lambda e, wb=wb, k=k, pt=pt, ch=ch, yT_=yT_: e.matmul(
                                pt[:, ch * 128:(ch + 1) * 128], lhsT=wb[:, k, :], rhs=yT_[:, k, :], start=(k == 0), stop=(k == 15)),
                                reads=[wbr.r, yT_.r], writes=[pt.r], inc=(k == 15))
                    P.op(DVE, lambda e, dc=dc, pt=pt: e.scalar_tensor_tensor(
                        out=self.xT[:, dc, tok0:tok0 + W], in0=pt[:, 0:W], scalar=self.m_gt(1, 0, dc, ci),
                        in1=self.xT[:, dc, tok0:tok0 + W], op0=ALU.mult, op1=ALU.add),
                        reads=[pt.r, self.modT[1].r, self.xr[dc][g]], writes=[self.xr[dc][g]])

            def run_pair(pi, mode):
                tok0 = pi * W
                g = tok0 // GT
                sample = pi < 8
                ci = 0 if sample else 1
                cols = 64 if sample else 256
                if mode == "S":
                    for _ in range(2):
                        if tab1_items:
                            P.streams[POOL].append(tab1_items.pop(0))
                if mode == "S" or (mode == "F" and not sample):
                    frontend(tok0, cols, ci, g)
                    fe_io(pi, True)
                else:
                    if mode == "B":
                        frontend(tok0, cols, ci, g, only_norm=True)
                    fe_io(pi, False)
                if mode == "B":
                    zproj_pair()
                order = (0, 1) if mode != "B" else (1, 0)
                for ch in order:
                    scan_chunk(ch, mode, pi * 2 + ch, tok0, ci)
                if mode == "B":
                    out_proj_pair(tok0, g, ci)

            def set_state(d, src_ap=None):
                if src_ap is None:
                    P.op(DVE, lambda e: e.memset(hT[d][:], 0.0), writes=[hT[d].r])
                else:
                    P.dma(SP, lambda e: e.dma_start(out=hT[d][:], in_=src_ap), writes=[hT[d].r], sbuf=hT[d].r)
                P.op(ACT, lambda e: e.copy(out=hTb[d][:], in_=hT[d][:]), reads=[hT[d].r], writes=[hTb[d].r])

            set_state(0, self.stinit_d[0])
            set_state(1, None)
            P.op(DVE, lambda e: e.memset(Pb[:], 1.0), writes=[Pb.r])
            for pi in range(8):
                run_pair(pi, "S")
            P.dma(SP, lambda e: e.dma_start(out=ysb[:], in_=self.stinit_d[1]), writes=[ysb.r], sbuf=ysb.r)
            P.op(DVE, lambda e: e.tensor_tensor(out=ysb[:].rearrange("p (h q) -> p h q", q=64), in0=ysb[:].rearrange("p (h q) -> p h q", q=64),
                                                in1=Pb[:].unsqueeze(2).to_broadcast([128, 32, 64]), op=ALU.mult),
                 reads=[ysb.r, Pb.r], writes=[ysb.r])
            P.op(DVE, lambda e: e.tensor_tensor(out=hT[1][:], in0=hT[1][:], in1=ysb[:], op=ALU.add), reads=[hT[1].r, ysb.r], writes=[hT[1].r])
            for d in range(2):
                P.op(DVE, lambda e, d=d: e.tensor_scalar(out=hT[d][:], in0=hT[d][:], scalar1=flags[:, d:d + 1], scalar2=None, op0=ALU.mult),
                     reads=[hT[d].r, flags.r], writes=[hT[d].r])
                P.dma(SP, lambda e, d=d: e.dma_start(out=self.cc_in[:, d * 2048:(d + 1) * 2048], in_=hT[d][:]),
                      reads=[hT[d].r], writes=[self.cc_in_r], sbuf=hT[d].r, store=True)
            ccsem = self.st.enter_context(nc.semaphore("ccsem"))
            P.barrier()
            self.P.streams[POOL].append({"waits": [], "fn": lambda e: e.collective_compute(
                "AllReduce", ALU.add, replica_groups=[[0, 1], [2, 3], [4, 5], [6, 7]],
                ins=[self.cc_in], outs=[self.cc_out]).then_inc(ccsem, 1), "inc": None})
            for eng in (POOL, SP, DVE, ACT, PE):
                self.P.streams[eng].append({"waits": [], "fn": lambda e: e.wait_ge(ccsem, 1), "inc": None})
            inits = []
            for d in range(2):
                P.dma(SP, lambda e, d=d: e.dma_start(out=ysb[:], in_=self.cc_out[:, d * 2048:(d + 1) * 2048]), writes=[ysb.r], sbuf=ysb.r)
                P.dma(SP, lambda e, d=d: e.dma_start(out=hT[d][:], in_=self.stinit_d[d]), writes=[hT[d].r], sbuf=hT[d].r)
                P.op(DVE, lambda e, d=d: e.scalar_tensor_tensor(out=hT[d][:], in0=ysb[:], scalar=flags[:, 2 + d:3 + d], in1=hT[d][:],
                                                                 op0=ALU.mult, op1=ALU.add), reads=[ysb.r, flags.r, hT[d].r], writes=[hT[d].r])
                P.dma(SP, lambda e, d=d: e.dma_start(out=self.init_true[d], in_=hT[d][:]), reads=[hT[d].r], writes=[self.init_true_r[d]],
                      sbuf=hT[d].r, store=True)
            P.op(ACT, lambda e: e.copy(out=hTb[0][:], in_=hT[0][:]), reads=[hT[0].r], writes=[hTb[0].r])
            for pi in range(8):
                run_pair(pi, "F")
            for q in range(2):
                set_state(0, None)
                run_pair(8 + q, "F")
                P.dma(SP, lambda e, q=q: e.dma_start(out=self.st_out[q, 0], in_=hT[0][:]), reads=[hT[0].r], sbuf=hT[0].r, store=True)
            P.barrier()
            for q in (1, 0):
                set_state(1, None)
                run_pair(8 + q, "B")
                P.dma(SP, lambda e, q=q: e.dma_start(out=self.st_out[q, 1], in_=hT[1][:]), reads=[hT[1].r], sbuf=hT[1].r, store=True)
            P.dma(SP, lambda e: e.dma_start(out=hT[1][:], in_=self.init_true[1]), reads=[self.init_true_r[1]], writes=[hT[1].r], sbuf=hT[1].r)
            P.op(ACT, lambda e: e.copy(out=hTb[1][:], in_=hT[1][:]), reads=[hT[1].r], writes=[hTb[1].r])
            for pi in range(7, -1, -1):
                run_pair(pi, "B")
            P.barrier()

    def _cum_for(self, ch, d, tri, dA, cumc, tot):
        P = self.P
        pc = self.ps()
        P.op(PE, lambda e, pc=pc: e.matmul(pc[:, 0:32], lhsT=tri[d][:], rhs=dA[:, ch, d * 32:(d + 1) * 32], start=True, stop=True),
             reads=[tri[d].r, dA.r], writes=[pc.r], inc=False)
        P.op(PE, lambda e, pc=pc: e.matmul(pc[:, 32:64], lhsT=self.ones[:], rhs=dA[:, ch, d * 32:(d + 1) * 32], start=True, stop=True),
             reads=[self.ones.r, dA.r], writes=[pc.r], inc=True)
        P.op(ACT, lambda e, pc=pc: e.copy(out=cumc[:], in_=pc[:, 0:32]), reads=[pc.r], writes=[cumc.r])
        P.op(ACT, lambda e, pc=pc: e.copy(out=tot[:], in_=pc[:, 32:64]), reads=[pc.r], writes=[tot.r])

    def phase_final(self):
        nc, P = self.nc, self.P
        with contextlib.ExitStack() as ph:
            sb = self.sb
            yo = [sb(ph, "yo%d" % i, [128, KC, GT]) for i in range(2)]
            rstd = sb(ph, "rstd", [128, GT])
            sq = sb(ph, "sq", [128, 2, GT])
            tmp = sb(ph, "tmp", [128, 2, GT])
            yv = self.yT_d.rearrange("(c p) t -> p c t", p=128)
            for g in range(NG):
                y = yo[g % 2]
                self.norm_group(g, lambda c, ci: self.spk[:, SP_NFF + c:SP_NFF + c + 1], None, [y], rstd, sq, tmp)
                P.dma(SP, lambda e, y=y, g=g: e.dma_start(out=yv[:, :, g * GT:(g + 1) * GT], in_=y[:]),
                      reads=[y.r], sbuf=y.r, store=True)


def _pc(v):
    v = np.asarray(v, np.float32)
    return np.ascontiguousarray(v.reshape(-1, 128).T)


def make_in_maps(inp):
    f = lambda a: np.ascontiguousarray(np.asarray(a, np.float32))
    spk = np.zeros((128, NSP), np.float32)
    for i in range(2):
        spk[:, SP_NMG + 8 * i:SP_NMG + 8 * (i + 1)] = _pc(inp["norm_mix_g"][i])
        spk[:, SP_NFG + 8 * i:SP_NFG + 8 * (i + 1)] = _pc(inp["norm_ffn_g"][i])
        spk[:, SP_ADAB + 48 * i:SP_ADAB + 48 * (i + 1)] = _pc(inp["ada_b"][i])
    spk[:, SP_NFF:SP_NFF + 8] = _pc(inp["norm_f_g"])
    for tap in range(3):
        spk[:, SP_SCW + 8 * tap:SP_SCW + 8 * (tap + 1)] = _pc(inp["sc_conv_w"][0][tap])
        spk[:, SP_SSW + 32 * tap:SP_SSW + 32 * (tap + 1)] = _pc(inp["ssd_conv_w"][0][tap])
    spk[:, SP_SSB:SP_SSB + 32] = _pc(inp["ssd_conv_b"][0])
    spk[:, SP_SNG:SP_SNG + 16] = _pc(inp["ssd_norm_g"][0])
    rowp = np.concatenate([np.asarray(inp["ssd_dt_bias"][0], np.float32).reshape(-1),
                           np.asarray(inp["ssd_a_log"][0], np.float32).reshape(-1),
                           np.asarray(inp["ssd_d"][0], np.float32).reshape(-1)])
    ii = np.arange(128)
    trif = (ii[:, None] <= ii[None, :]).astype(np.float32)
    trib = (ii[:, None] >= ii[None, :]).astype(np.float32)
    shared = {
        "rowp": np.ascontiguousarray(np.broadcast_to(rowp, (128, ROWP_N))),
        "trif": trif, "trib": trib,
        "mnegf": (trif - 1.0) * 30000.0, "mnegb": (trib - 1.0) * 30000.0,
        "ssd_w_in": f(inp["ssd_w_in"]), "ssd_w_out": f(inp["ssd_w_out"]),
        "spk": spk,
        "ident": np.eye(128, dtype=np.float32),
        "ones": np.ones((128, 128), np.float32),
        "iota": np.ascontiguousarray(np.broadcast_to(np.arange(256, dtype=np.float32), (128, 256))),
        "ada_w": f(inp["ada_w"]),
        "sc_w_in": f(inp["sc_w_in"]),
        "sc_w_out": f(inp["sc_w_out"]),
        "peer_wq": f(inp["peer_wq"]),
        "peer_keys": f(inp["peer_keys"]),
        "peer_uv": np.ascontiguousarray(np.concatenate(
            [np.asarray(inp["peer_u"], np.float32), np.asarray(inp["peer_v"], np.float32)], axis=-1)),
    }
    xs = np.asarray(inp["x_sample"], np.float32)
    xp = np.asarray(inp["x_prompt"], np.float32)
    maps = []
    for r in range(8):
        b, hf = r // 2, r % 2
        tok = np.concatenate([xs[b, hf * 2048:(hf + 1) * 2048], xp[2 * r], xp[2 * r + 1]], axis=0)
        condT = np.stack([_pc(inp["c"][b]), _pc(inp["c_ctx"])], axis=2)
        m = dict(shared)
        m["xT"] = np.ascontiguousarray(tok.T)
        m["condT"] = np.ascontiguousarray(condT)
        st = np.asarray(inp["state_ssm"], np.float32)[b, 0]
        stinit = np.zeros((2, 128, 2048), np.float32)
        stinit[hf] = st[hf].reshape(2048, 128).T
        m["stinit"] = stinit
        fl = np.array([1, 0, 0, 1] if hf == 0 else [0, 1, 1, 0], np.float32)
        m["flags"] = np.ascontiguousarray(np.broadcast_to(fl, (128, 4)))
        maps.append(m)
    return maps


_NC_CACHE = {}


def kernel(**inputs):
    key = "full"
    if key not in _NC_CACHE:
        _NC_CACHE[key] = Builder().build()
    nc = _NC_CACHE[key]
    maps = make_in_maps(inputs)
    res = run_bass_kernel_spmd(nc, maps, core_ids=list(range(8)))
    y_prompt = np.zeros((16, 256, D), np.float32)
    y_sample = np.zeros((4, 4096, D), np.float32)
    states = np.zeros((16, 1, 2, 32, 64, 128), np.float32)
    for r in range(8):
        yT = np.asarray(res.results[r]["yT"])
        y = yT.T
        b, hf = r // 2, r % 2
        y_sample[b, hf * 2048:(hf + 1) * 2048] = y[0:2048]
        y_prompt[2 * r] = y[2048:2304]
        y_prompt[2 * r + 1] = y[2304:2560]
        if "st_out" in res.results[r]:
            so = np.asarray(res.results[r]["st_out"])
            for q in range(2):
                for d in range(2):
                    states[2 * r + q, 0, d] = so[q, d].T.reshape(32, 64, 128)
    return (y_prompt, y_sample, states)
```
